# Optimizing a Trainium2 kernel written in Bass

```python
import jax
import jax.numpy as jnp
from jax import lax
import numpy as np

D_MODEL = 1024
BATCH = 8
SEQ = 2048
DEPTH = 4
DEC_BATCH = 128
DEC_SEQ = 4
PAST_LEN = 2048
PAGE_SIZE = 128

HEAD_DIM = 64
N_HEADS = D_MODEL // HEAD_DIM
N_A_LAYERS = DEPTH // 2
N_B_LAYERS = DEPTH - N_A_LAYERS
D_FF = ((8 * D_MODEL // 3 + 127) // 128) * 128
LORA_W = 64
LORA_A = 64
LORA_V = 32
LORA_G = 128
Q_BLOCK = 128
LN_EPS = 1e-5
GN_EPS = 64e-5
SB_SCALE = HEAD_DIM ** -0.5
DEEPNORM_ALPHA = (2 * DEPTH) ** 0.25
DEEPNORM_BETA = (8 * DEPTH) ** -0.25

kernel_name = 'rwkv7_stickbreak_yoco_decoder_step'


def layer_norm(x, g, b):
    xf = x.astype(jnp.float32)
    mu = jnp.mean(xf, axis=-1, keepdims=True)
    var = jnp.mean(jnp.square(xf - mu), axis=-1, keepdims=True)
    return ((xf - mu) * lax.rsqrt(var + LN_EPS) * g.astype(jnp.float32) + b.astype(jnp.float32)).astype(x.dtype)


def post_norm(x, sub, li, slot, P):
    return layer_norm(DEEPNORM_ALPHA * x + sub, P['ln_g'][li, slot], P['ln_b'][li, slot])


def ffn_half(x, li, j, P):
    h = jax.nn.silu(x @ P['ffn_w_gate'][li, j]) * (x @ P['ffn_w_up'][li, j])
    return post_norm(x, 0.5 * (h @ P['ffn_w_down'][li, j]), li, 2 * j, P)


def rwkv7_scan(r, decay, k, v, kk, a, s0):
    def step(s, inp):
        r_t, d_t, k_t, v_t, kk_t, a_t = inp
        sa = jnp.einsum('bhvk,bhk->bhv', s, -kk_t)
        s = (s * d_t[:, :, None, :]
             + sa[..., None] * (kk_t * a_t)[:, :, None, :]
             + v_t[..., None] * k_t[:, :, None, :])
        y = jnp.einsum('bhvk,bhk->bhv', s, r_t)
        return s, y
    xs = tuple(jnp.swapaxes(t, 0, 1) for t in (r, decay, k, v, kk, a))
    s_final, ys = lax.scan(step, s0, xs)
    return jnp.swapaxes(ys, 0, 1), s_final


def rwkv7_time_mix(x, x_prev, s0, v_first, li, P):
    B, T, D = x.shape
    H, Dh = N_HEADS, HEAD_DIM
    f32 = jnp.float32
    x_shift = jnp.concatenate([x_prev[:, None, :].astype(x.dtype), x[:, :-1, :]], axis=1)
    xx = x_shift - x
    mu = P['tm_mu'][li]
    xr, xw, xk, xv, xa, xg = (x + xx * mu[i] for i in range(6))
    r = xr @ P['tm_w_r'][li]
    w_in = (P['tm_w0'][li] + jnp.tanh(xw @ P['tm_w1'][li]) @ P['tm_w2'][li]).astype(f32)
    w_log = -jax.nn.softplus(-w_in) - 0.5
    decay = jnp.exp(-jnp.exp(w_log))
    k = xk @ P['tm_w_k'][li]
    v = xv @ P['tm_w_v'][li]
    if li == 0:
        v_first = v
    else:
        j = li - 1
        v = v + (v_first - v) * jax.nn.sigmoid(P['tm_v0'][j] + (xv @ P['tm_v1'][j]) @ P['tm_v2'][j])
    a = jax.nn.sigmoid(P['tm_a0'][li] + (xa @ P['tm_a1'][li]) @ P['tm_a2'][li])
    g = jax.nn.sigmoid(xg @ P['tm_g1'][li]) @ P['tm_g2'][li]

    def heads(t):
        return t.reshape(B, T, H, Dh).astype(f32)

    kk = heads(k * P['tm_k_k'][li])
    kk = kk / jnp.maximum(jnp.linalg.norm(kk, axis=-1, keepdims=True), 1e-12)
    k_h = heads(k * (1 + (a - 1) * P['tm_k_a'][li]))
    r_h, v_h, a_h = heads(r), heads(v), heads(a)
    y, s_new = rwkv7_scan(r_h, heads(decay), k_h, v_h, kk, a_h, s0.astype(f32))
    mu_y = jnp.mean(y, axis=-1, keepdims=True)
    var_y = jnp.mean(jnp.square(y - mu_y), axis=-1, keepdims=True)
    y = ((y - mu_y) * lax.rsqrt(var_y + GN_EPS) * P['tm_gn_g'][li].reshape(H, Dh).astype(f32)
         + P['tm_gn_b'][li].reshape(H, Dh).astype(f32))
    y = y + jnp.sum(r_h * k_h * P['tm_r_k'][li].astype(f32), axis=-1, keepdims=True) * v_h
    y = y.reshape(B, T, D).astype(x.dtype)
    out = (y * g) @ P['tm_w_o'][li]
    return out, s_new.astype(s0.dtype), x[:, -1, :], v_first


def stick_breaking(q, k, v, bias, q_pos, k_pos):
    z = (jnp.einsum('bqhd,bkhd->bhqk', q, k).astype(jnp.float32) * SB_SCALE
         + bias.astype(jnp.float32)[None, :, None, None])
    mask = k_pos[None, :] < q_pos[:, None]
    log_1mb = jnp.where(mask, jax.nn.log_sigmoid(-z), 0.0)
    log_surv = lax.cumsum(log_1mb, axis=3, reverse=True) - log_1mb
    weights = jnp.where(mask, jnp.exp(jax.nn.log_sigmoid(z) + log_surv), 0.0)
    return jnp.einsum('bhqk,bkhd->bqhd', weights.astype(v.dtype), v)


def attend_prompt(q, k, v, bias):
    B, S, H, Dh = q.shape
    n_blk = S // Q_BLOCK
    qb = jnp.swapaxes(q.reshape(B, n_blk, Q_BLOCK, H, Dh), 0, 1)
    k_pos = jnp.arange(S)

    def one_block(args):
        q_blk, start = args
        return stick_breaking(q_blk, k, v, bias, start + jnp.arange(Q_BLOCK), k_pos)

    ob = lax.map(one_block, (qb, jnp.arange(n_blk) * Q_BLOCK))
    return jnp.swapaxes(ob, 0, 1).reshape(B, S, H, Dh)


def trunk(x, wkv_in, shift_in, attend, P):
    B, T, D = x.shape
    new_wkv, new_shift = [], []
    v_first = None
    k_sh = v_sh = None
    for li in range(DEPTH):
        x = ffn_half(x, li, 0, P)
        if li < N_A_LAYERS:
            out, s_new, x_last, v_first = rwkv7_time_mix(x, shift_in[li], wkv_in[li], v_first, li, P)
            new_wkv.append(s_new)
            new_shift.append(x_last)
        else:
            j = li - N_A_LAYERS
            q = (x @ P['sb_w_q'][j]).reshape(B, T, N_HEADS, HEAD_DIM)
            o = attend(q, k_sh, v_sh, P['sb_bias'][j]).reshape(B, T, D)
            out = o @ P['sb_w_o'][j]
        x = post_norm(x, out, li, 1, P)
        x = ffn_half(x, li, 1, P)
        if li == N_A_LAYERS - 1:
            k_sh = (x @ P['sb_w_k']).reshape(B, T, N_HEADS, HEAD_DIM)
            v_sh = (x @ P['sb_w_v']).reshape(B, T, N_HEADS, HEAD_DIM)
    return x, jnp.stack(new_wkv), jnp.stack(new_shift), k_sh, v_sh


def setup_inputs(seed: int = 0) -> dict:
    key = jax.random.key(seed)
    ks = iter(jax.random.split(key, 48))
    f32 = jnp.float32

    def nrm(shape, scale):
        return jax.random.normal(next(ks), shape, f32) * scale

    def uni(shape, lo, hi):
        return jax.random.uniform(next(ks), shape, f32, lo, hi)

    D, H, Dh, F = D_MODEL, N_HEADS, HEAD_DIM, D_FF
    NA, NB = N_A_LAYERS, N_B_LAYERS
    n_pages = PAST_LEN // PAGE_SIZE
    n_used = DEC_BATCH * n_pages
    n_phys = n_used + max(1, n_used // 4)
    perm = jax.random.permutation(next(ks), n_phys)
    page_table = perm[:n_used].reshape(DEC_BATCH, n_pages).astype(jnp.int32)
    sd = D ** -0.5
    return {
        'x_prompt': nrm((BATCH, SEQ, D), 1.0),
        'x_sample': nrm((DEC_BATCH, DEC_SEQ, D), 1.0),
        'cache_k': nrm((n_phys, PAGE_SIZE, H, Dh), 1.0),
        'cache_v': nrm((n_phys, PAGE_SIZE, H, Dh), 1.0),
        'state_wkv': nrm((NA, DEC_BATCH, H, Dh, Dh), 0.3),
        'state_shift': nrm((NA, DEC_BATCH, D), 1.0),
        'page_table': page_table,
        'ln_g': 1.0 + nrm((DEPTH, 3, D), 0.02),
        'ln_b': nrm((DEPTH, 3, D), 0.02),
        'ffn_w_gate': nrm((DEPTH, 2, D, F), sd),
        'ffn_w_up': nrm((DEPTH, 2, D, F), sd),
        'ffn_w_down': nrm((DEPTH, 2, F, D), F ** -0.5 * DEEPNORM_BETA),
        'tm_mu': uni((NA, 6, D), 0.0, 1.0),
        'tm_w0': uni((NA, D), -6.5, -1.5),
        'tm_w1': nrm((NA, D, LORA_W), sd),
        'tm_w2': nrm((NA, LORA_W, D), 0.1),
        'tm_a0': nrm((NA, D), 0.1),
        'tm_a1': nrm((NA, D, LORA_A), sd),
        'tm_a2': nrm((NA, LORA_A, D), 0.1),
        'tm_v0': nrm((NA - 1, D), 0.1),
        'tm_v1': nrm((NA - 1, D, LORA_V), sd),
        'tm_v2': nrm((NA - 1, LORA_V, D), 0.1),
        'tm_g1': nrm((NA, D, LORA_G), sd),
        'tm_g2': nrm((NA, LORA_G, D), LORA_G ** -0.5),
        'tm_k_k': 0.85 + nrm((NA, D), 0.02),
        'tm_k_a': 1.0 + nrm((NA, D), 0.02),
        'tm_r_k': nrm((NA, H, Dh), 0.1),
        'tm_w_r': nrm((NA, D, D), sd),
        'tm_w_k': nrm((NA, D, D), sd),
        'tm_w_v': nrm((NA, D, D), sd),
        'tm_w_o': nrm((NA, D, D), sd * DEEPNORM_BETA),
        'tm_gn_g': 1.0 + nrm((NA, D), 0.02),
        'tm_gn_b': nrm((NA, D), 0.02),
        'sb_w_k': nrm((D, D), sd),
        'sb_w_v': nrm((D, D), sd),
        'sb_w_q': nrm((NB, D, D), sd),
        'sb_bias': uni((NB, H), -8.0, -5.0),
        'sb_w_o': nrm((NB, D, D), sd * DEEPNORM_BETA),
    }


def reference(x_prompt, x_sample, cache_k, cache_v, state_wkv, state_shift, page_table,
              ln_g, ln_b, ffn_w_gate, ffn_w_up, ffn_w_down,
              tm_mu, tm_w0, tm_w1, tm_w2, tm_a0, tm_a1, tm_a2, tm_v0, tm_v1, tm_v2,
              tm_g1, tm_g2, tm_k_k, tm_k_a, tm_r_k, tm_w_r, tm_w_k, tm_w_v, tm_w_o,
              tm_gn_g, tm_gn_b, sb_w_k, sb_w_v, sb_w_q, sb_bias, sb_w_o):
    P = dict(ln_g=ln_g, ln_b=ln_b, ffn_w_gate=ffn_w_gate, ffn_w_up=ffn_w_up, ffn_w_down=ffn_w_down,
             tm_mu=tm_mu, tm_w0=tm_w0, tm_w1=tm_w1, tm_w2=tm_w2, tm_a0=tm_a0, tm_a1=tm_a1, tm_a2=tm_a2,
             tm_v0=tm_v0, tm_v1=tm_v1, tm_v2=tm_v2, tm_g1=tm_g1, tm_g2=tm_g2, tm_k_k=tm_k_k,
             tm_k_a=tm_k_a, tm_r_k=tm_r_k, tm_w_r=tm_w_r, tm_w_k=tm_w_k, tm_w_v=tm_w_v, tm_w_o=tm_w_o,
             tm_gn_g=tm_gn_g, tm_gn_b=tm_gn_b, sb_w_k=sb_w_k, sb_w_v=sb_w_v, sb_w_q=sb_w_q,
             sb_bias=sb_bias, sb_w_o=sb_w_o)

    Bp, Sp, D = x_prompt.shape
    wkv0 = jnp.zeros((N_A_LAYERS, Bp, N_HEADS, HEAD_DIM, HEAD_DIM), jnp.float32)
    shift0 = jnp.zeros((N_A_LAYERS, Bp, D), x_prompt.dtype)
    y_prompt, wkv_prompt, shift_prompt, k_p, v_p = trunk(x_prompt, wkv0, shift0, attend_prompt, P)
    k_prompt = k_p.reshape(Bp, Sp // PAGE_SIZE, PAGE_SIZE, N_HEADS, HEAD_DIM)
    v_prompt = v_p.reshape(Bp, Sp // PAGE_SIZE, PAGE_SIZE, N_HEADS, HEAD_DIM)

    Bs, n_pages = page_table.shape
    page = cache_k.shape[1]
    past_len = n_pages * page
    k_past = cache_k[page_table].reshape(Bs, past_len, N_HEADS, HEAD_DIM)
    v_past = cache_v[page_table].reshape(Bs, past_len, N_HEADS, HEAD_DIM)

    def attend_sample(q, k_new, v_new, bias):
        k_all = jnp.concatenate([k_past, k_new.astype(k_past.dtype)], axis=1)
        v_all = jnp.concatenate([v_past, v_new.astype(v_past.dtype)], axis=1)
        q_pos = past_len + jnp.arange(q.shape[1])
        k_pos = jnp.arange(k_all.shape[1])
        return stick_breaking(q, k_all, v_all, bias, q_pos, k_pos)

    y_sample, wkv_sample, shift_sample, k_sample, v_sample = trunk(
        x_sample, state_wkv, state_shift, attend_sample, P)

    return (y_prompt, y_sample, wkv_prompt, shift_prompt, k_prompt, v_prompt,
            wkv_sample, shift_sample, k_sample, v_sample)
```

```python
import contextlib
import numpy as np
import concourse.bass as bass
import concourse.mybir as mybir
from concourse.bass_utils import run_bass_kernel_spmd

F32, BF16, I32 = mybir.dt.float32, mybir.dt.bfloat16, mybir.dt.int32
AF = mybir.ActivationFunctionType
ALU = mybir.AluOpType

D = 1024
NCH = 8
FF = 2816
NFC = 22
H = 16
DH = 64
DEPTH = 4
NA = 2
ALPHA = (2 * DEPTH) ** 0.25
LN_EPS = 1e-5
GN_EPS = 64e-5
SB_SCALE = DH ** -0.5


class Cfg:
    def __init__(self, seq=2048, npg=16, nphys=2560, sb=16, dseq=4, ncores=8):
        self.seq, self.npg, self.nphys, self.sb, self.dseq, self.ncores = seq, npg, nphys, sb, dseq, ncores
        self.nts = sb * dseq
        self.nt = seq + self.nts


class Reg:
    __slots__ = ("name", "w", "rs", "track", "id", "ndma", "psum")
    _n = 0

    def __init__(self, name=""):
        self.name = name
        self.w = None
        self.rs = {}
        self.track = True
        Reg._n += 1
        self.id = Reg._n
        self.ndma = 0
        self.psum = name.startswith("ps")


ENGS = ["pe", "act", "dve", "pool", "sp"]


class Prog:
    def __init__(self):
        self.ops = {e: [] for e in ENGS}
        self.seen = {e: {} for e in ENGS}
        self.dma_regs = {}

    def op(self, eng, fn, reads=(), writes=(), dma=None):
        need = {}

        def add(k, v):
            if need.get(k, 0) < v:
                need[k] = v

        isdma = dma is not None
        for r in reads:
            if r.w is not None:
                add(*r.w)
            if r.psum and eng in ("act", "dve"):
                other = "dve" if eng == "act" else "act"
                if other in r.rs:
                    add(other, r.rs[other])
        strict = isdma or eng != "pe"
        for r in writes:
            if r.w is not None and (strict or r.w[0] != eng):
                add(*r.w)
            for k, v in r.rs.items():
                if strict or k != eng:
                    add(k, v)
        seen = self.seen[eng]
        waits = []
        for k, v in need.items():
            if seen.get(k, 0) < v:
                seen[k] = v
                waits.append((k, v))
                if isinstance(k, str):
                    self.ops[k][v - 1]["inc"] = True
        rec = dict(fn=fn, waits=waits, inc=False, dma=dma)
        self.ops[eng].append(rec)
        idx = len(self.ops[eng])
        if fn is None:
            return
        if isdma:
            dma.ndma += 1
            self.dma_regs[dma.id] = dma
            tok = (("d", dma.id), 16 * dma.ndma)
        else:
            tok = (eng, idx)
        for r in reads:
            if r.track:
                if r.rs.get(tok[0], 0) < tok[1]:
                    r.rs[tok[0]] = tok[1]
        for r in writes:
            r.w = tok
            r.rs = {}

    def barrier(self):
        toks = []
        for e in ENGS:
            for i in range(len(self.ops[e]), 0, -1):
                rec = self.ops[e][i - 1]
                if rec["fn"] is not None and rec["dma"] is None:
                    toks.append((e, i))
                    break
        for r in self.dma_regs.values():
            toks.append((("d", r.id), 16 * r.ndma))
        for e in ENGS:
            seen = self.seen[e]
            waits = []
            for k, v in toks:
                if seen.get(k, 0) < v:
                    seen[k] = v
                    waits.append((k, v))
                    if isinstance(k, str):
                        self.ops[k][v - 1]["inc"] = True
            self.ops[e].append(dict(fn=None, waits=waits, inc=False, dma=None))

    def emit(self, nc, es):
        sems = {e: es.enter_context(nc.semaphore("s_" + e)) for e in ENGS}
        dsem = {i: es.enter_context(nc.semaphore("d%d" % i)) for i in self.dma_regs}
        pref = {}
        for e in ENGS:
            c = 0
            arr = [0]
            for rec in self.ops[e]:
                if rec["inc"]:
                    c += 1
                arr.append(c)
            pref[e] = arr
        block = es.enter_context(nc.Block())

        def run(eh, eng):
            for rec in self.ops[eng]:
                for k, v in rec["waits"]:
                    if isinstance(k, str):
                        eh.wait_ge(sems[k], pref[k][v])
                    else:
                        eh.wait_ge(dsem[k[1]], v)
                if rec["fn"] is None:
                    continue
                ins = rec["fn"](eh)
                if rec["dma"] is not None:
                    ins.then_inc(dsem[rec["dma"].id], 16)
                elif rec["inc"]:
                    ins.then_inc(sems[eng], 1)

        @block.tensor
        def _(e):
            run(e, "pe")

        @block.scalar
        def _(e):
            run(e, "act")

        @block.vector
        def _(e):
            run(e, "dve")

        @block.gpsimd
        def _(e):
            run(e, "pool")

        @block.sync
        def _(e):
            run(e, "sp")


def vec_index():
    names = []
    for li in range(DEPTH):
        for s in range(3):
            names.append(("ln_g", li, s))
            names.append(("ln_b", li, s))
    for li in range(NA):
        for i in range(6):
            names.append(("mu", li, i))
        for n in ("w0", "a0", "k_k", "k_a", "r_k", "gn_g", "gn_b"):
            names.append((n, li, 0))
    names.append(("v0", 0, 0))
    return {n: i for i, n in enumerate(names)}


VIDX = vec_index()
NV = len(VIDX)


def pack_vecs(inp):
    out = np.zeros((NV, D), np.float32)
    for (n, li, s), i in VIDX.items():
        if n == "ln_g":
            v = inp["ln_g"][li, s]
        elif n == "ln_b":
            v = inp["ln_b"][li, s]
        elif n == "mu":
            v = inp["tm_mu"][li, s]
        elif n == "v0":
            v = inp["tm_v0"][0]
        elif n == "r_k":
            v = inp["tm_r_k"][li].reshape(D)
        else:
            v = inp["tm_" + n][li]
        out[i] = v
    return np.ascontiguousarray(out.reshape(NV, NCH, 128).transpose(2, 0, 1).reshape(128, NV * NCH))


def pack_consts():
    i = np.arange(128)
    ident = (i[:, None] == i[None, :]).astype(np.float32)
    ones = np.ones((128, 128), np.float32)
    blk = ((i[:, None] // 64) == (i[None, :] // 64)).astype(np.float32)
    tri = (i[:, None] > i[None, :]).astype(np.float32)
    dmask = (i[:, None] < i[None, :]).astype(np.float32)
    return np.ascontiguousarray(np.concatenate([ident, ones, blk, tri, dmask], axis=1))


def pack_ffn(inp):
    g = inp["ffn_w_gate"].reshape(8, NCH, 128, NFC, 128).transpose(0, 3, 2, 1, 4).reshape(8, NFC, 128, 1024)
    u = inp["ffn_w_up"].reshape(8, NCH, 128, NFC, 128).transpose(0, 3, 2, 1, 4).reshape(8, NFC, 128, 1024)
    d = inp["ffn_w_down"].reshape(8, NFC, 128, 1024)
    return np.ascontiguousarray(np.concatenate([g, u, d], axis=3))


def pack_proj(w):
    lead = w.shape[:-2]
    n = w.shape[-1]
    w = w.reshape(lead + (NCH, 128, n))
    nd = len(lead)
    perm = tuple(range(nd)) + (nd + 1, nd, nd + 2)
    return np.ascontiguousarray(w.transpose(perm))


def I(name, *a, **k):
    return lambda e: getattr(e, name)(*a, **k)


class Builder:
    def __init__(self, cfg, dbg=False, nlayers=DEPTH, stop_after=None, mixers=True, skip=()):
        self.mixers = mixers
        self.skip = skip
        self.cfg = cfg
        self.dbg = dbg
        self.nc = bass.Bass("TRN2", target_bir_lowering=False)
        self.P = Prog()
        self.es = contextlib.ExitStack()
        self.nlayers = nlayers
        self.stop_after = stop_after

    def sb(self, name, shape, dt):
        return self.es.enter_context(self.nc.sbuf_tensor(name, shape, dt))

    def dram_in(self, name, shape, dt=F32):
        return self.nc.dram_tensor(name, list(shape), dt, kind="ExternalInput").ap()

    def dram_out(self, name, shape, dt=F32):
        return self.nc.dram_tensor(name, list(shape), dt, kind="ExternalOutput").ap()

    def build(self):
        cfg, nc, P = self.cfg, self.nc, self.P
        NT, SEQ, NTS = cfg.nt, cfg.seq, cfg.nts
        self.tiles = [(t0, min(512, NT - t0)) for t0 in range(0, NT, 512)]
        self.d_xp = self.dram_in("xp", [SEQ, D])
        self.d_xs = self.dram_in("xs", [128, D])
        self.d_vecs = self.dram_in("vecs", [128, NV * NCH])
        self.d_cst = self.dram_in("cst", [128, 5 * 128])
        self.d_wffn = self.dram_in("wffn", [8, NFC, 128, 3072])
        SBN = cfg.sb
        self.d_v64 = self.dram_in("v64", [64, NV64 * H])
        self.d_c64 = self.dram_in("c64", [64, 576])
        self.d_wrkv = self.dram_in("wrkv", [NA, 3, 128, NCH, D])
        self.d_wo = self.dram_in("wo", [NA, 128, NCH, D])
        self.d_l1 = self.dram_in("l1", [NA, 128, NCH, 288])
        self.d_l2 = self.dram_in("l2", [NA, 128, 4, D])
        self.d_sshift = self.dram_in("sshift", [NA, SBN, D])
        self.d_swkv = self.dram_in("swkv", [NA, SBN, H, DH, DH])
        self.d_wkv_p = self.dram_out("wkv_p", [NA, H, DH, DH])
        self.d_wkv_s = self.dram_out("wkv_s", [NA, SBN, H, DH, DH])
        self.d_shift_p = self.dram_out("shift_p", [NA, D])
        self.d_shift_s = self.dram_out("shift_s", [NA, SBN, D])
        self.d_vf = self.nc.dram_tensor("vf", [64, H, NT], F32, kind="Internal").ap()
        NPG = cfg.npg
        self.d_wkv2 = self.dram_in("wkv2", [2, 128, NCH, D])
        self.d_wq = self.dram_in("wq", [2, 128, NCH, D])
        self.d_wo2 = self.dram_in("wo2", [2, 128, NCH, D])
        self.d_sbb = self.dram_in("sbb", [128, 2 * H])
        self.d_pt = self.dram_in("pt", [1, SBN * NPG], I32)
        self.d_ck = self.dram_in("ck", [cfg.nphys * 128, D])
        self.d_cv = self.dram_in("cv", [cfg.nphys * 128, D])
        self.d_kp = self.dram_out("kp", [SEQ, D])
        self.d_vp = self.dram_out("vp", [SEQ, D])
        self.d_ks = self.dram_out("ks", [128, D])
        self.d_vs = self.dram_out("vs", [128, D])
        self.d_KT = self.nc.dram_tensor("KTs", [NCH, 128, NT + 64], BF16, kind="Internal").ap()
        self.d_VB = self.nc.dram_tensor("VBs", [NT + 64, D], BF16, kind="Internal").ap()
        self.d_yp = self.dram_out("yp", [SEQ, D])
        self.d_ys = self.dram_out("ys", [128, D])
        self.xT = self.sb("xT", [128, NCH, NT + 64], F32)
        self.xb = self.sb("xb", [128, NCH, NT + 64], BF16)
        self.vecs = self.sb("vecs_sb", [128, NV * NCH], F32)
        self.cst = self.sb("cst_sb", [128, 5 * 128], F32)
        self.cstb = self.sb("cstb", [128, 5 * 128], BF16)
        self.vecs64 = self.sb("v64_sb", [64, NV64 * H], F32)
        self.cst64 = self.sb("c64_sb", [64, 576], F32)
        self.tiny = self.sb("tiny", [128, 1], F32)
        self.gneps = self.sb("gneps", [128, 1], F32)
        self.rv64, self.rc64, self.rtiny = Reg("v64"), Reg("c64"), Reg("tiny")
        self.onec = self.sb("onec", [128, 1], F32)
        self.sbb = self.sb("sbb_sb", [128, 2 * H], F32)
        self.pt_sb = self.sb("pt_sb", [128, cfg.sb * cfg.npg], I32)
        self.pidx = self.sb("pidx", [128, cfg.sb * cfg.npg], I32)
        self.iota_p = self.sb("iota_p", [128, 1], I32)
        self.rsbb, self.rpidx = Reg("sbb"), Reg("pidx")
        self.ARENA = getattr(self, 'ARENA_OVERRIDE', 24300)
        self.arena = self.sb("arena", [128, self.ARENA], F32)
        self.ps = [self.es.enter_context(nc.psum_tensor("ps%d" % i, [128, 512], F32)) for i in range(8)]
        self.rps = [Reg("ps%d" % i) for i in range(8)]
        self.rx = [Reg("x%d" % i) for i in range(len(self.tiles))]
        self.rb = [Reg("xb%d" % i) for i in range(len(self.tiles))]
        self.rvec = Reg("vecs")
        self.rcst = Reg("cst")
        self.out_regs = []

        P.op("sp", I("dma_start", out=self.vecs[:], in_=self.d_vecs[:, :]), writes=[self.rvec], dma=self.rvec)
        P.op("sp", I("dma_start", out=self.cst[:], in_=self.d_cst[:, :]), writes=[self.rcst], dma=self.rcst)
        P.op("sp", I("dma_start", out=self.vecs64[:], in_=self.d_v64[:, :]), writes=[self.rv64], dma=self.rv64)
        P.op("sp", I("dma_start", out=self.cst64[:], in_=self.d_c64[:, :]), writes=[self.rc64], dma=self.rc64)
        P.op("pool", I("memset", self.tiny[:], 1e-24), writes=[self.rtiny])
        P.op("pool", I("memset", self.gneps[:], GN_EPS), writes=[self.rtiny])
        P.op("pool", I("memset", self.onec[:], 1.0), writes=[self.rtiny])
        P.op("sp", I("dma_start", out=self.sbb[:], in_=self.d_sbb[:, :]), writes=[self.rsbb], dma=self.rsbb)
        rpt = Reg("ptsb")
        P.op("sp", I("dma_start", out=self.pt_sb[:], in_=self.d_pt[0:1, :].partition_broadcast(128)), writes=[rpt], dma=rpt)
        P.op("pool", I("iota", self.iota_p[:], pattern=[[0, 1]], base=0, channel_multiplier=1), writes=[self.rpidx])
        P.op("pool", I("tensor_scalar", out=self.pidx[:], in0=self.pt_sb[:], scalar1=128, scalar2=self.iota_p[:, 0:1], op0=ALU.mult, op1=ALU.add),
             reads=[rpt, self.rpidx], writes=[self.rpidx])
        rcb = Reg("cstb")
        P.op("dve", I("tensor_copy", out=self.cstb[:], in_=self.cst[:]), reads=[self.rcst], writes=[rcb])
        self.rcstb = rcb
        self.ident = self.cst[:, 0:128]
        self.ones = self.cst[:, 128:256]
        self.blk = self.cst[:, 256:384]

        if 'load' not in self.skip:
            self.load_x()
        for li in range(self.nlayers):
            if "ffn" not in self.skip:
                self.ffn(li, 0)
            if "ln" not in self.skip:
                self.layernorm(li, 0)
            if self.stop_after == (li, 0):
                break
            if self.mixers:
                if li < NA:
                    self.mixer_rwkv(li)
                else:
                    self.mixer_sb(li - NA)
                self.layernorm(li, 1)
            self.ffn(li, 1)
            self.layernorm(li, 2)
            if self.mixers and li == NA - 1:
                self.kv_proj()
        if 'store' not in self.skip:
            self.store_y()
        P.op("sp", None, reads=self.out_regs)
        P.barrier()
        P.emit(nc, self.es)
        self.es.close()
        return nc

    def vec(self, name, li, s, c):
        i = VIDX[(name, li, s)]
        return self.vecs[:, i * NCH + c:i * NCH + c + 1]

    def tile_of(self, t):
        return t // 512

    def aview(self, off, n, dt=F32):
        a = self.arena[:, off:off + n]
        return a if dt == F32 else a.bitcast(dt)

    def load_x(self):
        cfg, P = self.cfg, self.P
        stage = [self.aview(i * 1024, 1024) for i in range(2)]
        rst = [Reg("xstage%d" % i) for i in range(2)]
        srcs = [(self.d_xp, r0, 128, r0) for r0 in range(0, cfg.seq, 128)]
        srcs.append((self.d_xs, 0, 128, cfg.seq))
        import os
        srcs = srcs[:int(os.environ.get("NSRC", "99"))]
        for i, (src, r0, n, t0) in enumerate(srcs):
            s = i % 2
            P.op("sp", I("dma_start", out=stage[s][0:n, :], in_=src[r0:r0 + n, :]),
                 writes=[rst[s]], dma=rst[s])
            for half in range(2):
                pb = (i * 2 + half) % 2
                pst = self.ps[pb]
                for cc in range(4):
                    c = half * 4 + cc
                    P.op("pe", I("transpose",
                        out=pst[:, cc * 128:cc * 128 + 128], in_=stage[s][:, c * 128:(c + 1) * 128], identity=self.ident),
                        reads=[rst[s], self.rcst], writes=[self.rps[pb]])
                ti = self.tile_of(t0)
                pv = pst[:].rearrange("p (c t) -> p c t", c=4)[:, :, 0:n]
                P.op("dve", I("tensor_copy",
                    out=self.xT[:, half * 4:half * 4 + 4, t0:t0 + n], in_=pv), reads=[self.rps[pb]], writes=[self.rx[ti]])
                if not os.environ.get("NOACT"):
                    P.op("act", I("copy",
                        out=self.xb[:, half * 4:half * 4 + 4, t0:t0 + n], in_=pv), reads=[self.rps[pb]], writes=[self.rb[ti]])
        P.barrier()

    def store_y(self):
        cfg, P = self.cfg, self.P
        P.barrier()
        stage = [self.aview(i * 1024, 1024) for i in range(2)]
        rst = [Reg("ystage%d" % i) for i in range(2)]
        dsts = [(self.d_yp, r0, 128, r0) for r0 in range(0, cfg.seq, 128)]
        dsts.append((self.d_ys, 0, 128, cfg.seq))
        for i, (dst, r0, n, t0) in enumerate(dsts):
            s = i % 2
            ti = self.tile_of(t0)
            for half in range(2):
                pb = (i * 2 + half) % 2
                pst = self.ps[pb]
                for cc in range(4):
                    c = half * 4 + cc
                    P.op("pe", I("transpose",
                        out=pst[:, cc * 128:(cc + 1) * 128], in_=self.xT[:, c, t0:t0 + 128], identity=self.ident),
                        reads=[self.rx[ti], self.rcst], writes=[self.rps[pb]])
                eng = "dve" if half == 0 else "act"
                if eng == "dve":
                    P.op("dve", I("tensor_copy",
                        out=stage[s][0:n, half * 512:(half + 1) * 512], in_=pst[0:n, :]), reads=[self.rps[pb]], writes=[rst[s]])
                else:
                    P.op("act", I("copy",
                        out=stage[s][0:n, half * 512:(half + 1) * 512], in_=pst[0:n, :]), reads=[self.rps[pb]], writes=[rst[s]])
            P.op("sp", I("dma_start", out=dst[r0:r0 + n, :], in_=stage[s][0:n, :]),
                 reads=[rst[s]], dma=rst[s])
            if rst[s] not in self.out_regs:
                self.out_regs.append(rst[s])

    def ffn(self, li, j):
        cfg, P = self.cfg, self.P
        NT = cfg.nt
        P.barrier()
        widx = li * 2 + j
        G = 3
        groups = [list(range(g0, min(g0 + G, NFC))) for g0 in range(0, NFC, G)]
        o = 0
        stage = []
        for i in range(G):
            stage.append(self.aview(o, 3072)); o += 3072
        wb = []
        for i in range(2 * G):
            wb.append(self.aview(o, 1536, BF16)); o += 1536
        NH = 4
        hb = []
        for i in range(NH):
            hb.append(self.aview(o, 128, BF16)); o += 128
        sg = []
        for i in range(NH):
            sg.append(self.aview(o, 256)); o += 256
        assert o <= self.ARENA
        rstage = [Reg("wst%d" % i) for i in range(G)]
        rwb = [Reg("wb%d" % i) for i in range(2 * G)]
        rh = [Reg("h%d" % i) for i in range(NH)]
        rsg = [Reg("sg%d" % i) for i in range(NH)]
        ttiles = [(t0, min(256, NT - t0)) for t0 in range(0, NT, 256)]
        for ti, (t0, n) in enumerate(self.tiles):
            P.op("pool", I("tensor_scalar",
                out=self.xT[:, :, t0:t0 + n], in0=self.xT[:, :, t0:t0 + n], scalar1=ALPHA, scalar2=None, op0=ALU.mult),
                reads=[self.rx[ti]], writes=[self.rx[ti]])
        YB = [4, 5, 6, 7]
        GB = [0, 1, 2, 3]
        gctr = 0

        def w_dma(gi, k):
            fc = groups[gi][k]
            P.op("sp", I("dma_start", out=stage[k][:, :], in_=self.d_wffn[widx, fc, :, :]),
                 writes=[rstage[k]], dma=rstage[k])

        def w_cast(gi, k):
            slot = (gi % 2) * G + k
            P.op("pool", I("tensor_copy", out=wb[slot][:, 0:2048], in_=stage[k][:, 0:2048]),
                 reads=[rstage[k]], writes=[rwb[slot]])
            P.op("act", I("copy", out=wb[slot][:, 2048:3072], in_=stage[k][:, 2048:3072]),
                 reads=[rstage[k]], writes=[rwb[slot]])

        for k in range(len(groups[0])):
            w_dma(0, k)
        for k in range(len(groups[0])):
            w_cast(0, k)
        for gi, grp in enumerate(groups):
            nxt = groups[gi + 1] if gi + 1 < len(groups) else []
            for k in range(len(nxt)):
                w_dma(gi + 1, k)
            nsteps = len(ttiles) * len(grp)
            cast_at = {max(0, min(nsteps - len(nxt), nsteps // 2)) + k: k for k in range(len(nxt))}
            step = 0
            for tj, (t0, n) in enumerate(ttiles):
                ti = self.tile_of(t0)
                pend = None
                for k, fc in enumerate(grp):
                    if step in cast_at:
                        w_cast(gi + 1, cast_at[step])
                    step += 1
                    slot = (gi % 2) * G + k
                    gb = GB[gctr % 4]
                    hs = gctr % NH
                    gctr += 1
                    w = wb[slot]
                    for which in range(2):
                        for kc in range(NCH):
                            P.op("pe", I("matmul",
                                self.ps[gb][:, which * 256:which * 256 + n],
                                lhsT=w[:, which * 1024 + kc * 128:which * 1024 + (kc + 1) * 128],
                                rhs=self.xb[:, kc, t0:t0 + n], start=(kc == 0), stop=(kc == NCH - 1)),
                                reads=[rwb[slot], self.rb[ti]], writes=[self.rps[gb]])
                    P.op("act", I("activation", out=sg[hs][:, 0:n], in_=self.ps[gb][:, 0:n], func=AF.Silu),
                         reads=[self.rps[gb]], writes=[rsg[hs]])
                    P.op("dve", I("tensor_tensor",
                        out=hb[hs][:, 0:n], in0=sg[hs][:, 0:n], in1=self.ps[gb][:, 256:256 + n], op=ALU.mult),
                        reads=[rsg[hs], self.rps[gb]], writes=[rh[hs]])
                    if pend is not None:
                        self.ffn_down(*pend)
                    pend = (w, slot, rwb, hb[hs], rh[hs], n, YB, k == 0, k == len(grp) - 1)
                self.ffn_down(*pend)
                for dc in range(NCH):
                    yb = YB[dc // 2]
                    P.op("dve", I("scalar_tensor_tensor",
                        out=self.xT[:, dc, t0:t0 + n], in0=self.ps[yb][:, (dc % 2) * 256:(dc % 2) * 256 + n], scalar=0.5,
                        in1=self.xT[:, dc, t0:t0 + n], op0=ALU.mult, op1=ALU.add),
                        reads=[self.rps[yb], self.rx[ti]], writes=[self.rx[ti]])
            for st2 in range(step, nsteps + len(nxt)):
                if st2 in cast_at:
                    w_cast(gi + 1, cast_at[st2])

    def ffn_down(self, w, slot, rwb, h, rh, n, YB, first, last):
        P = self.P
        for dc in range(NCH):
            yb = YB[dc // 2]
            P.op("pe", I("matmul",
                self.ps[yb][:, (dc % 2) * 256:(dc % 2) * 256 + n],
                lhsT=w[:, 2048 + dc * 128:2048 + (dc + 1) * 128], rhs=h[:, 0:n],
                start=(first and dc % 2 == 0), stop=last, skip_group_check=True),
                reads=[rwb[slot], rh], writes=[self.rps[yb]])

    def mk_alloc(self):
        cfg = self.cfg
        xbw = self.xb[:].rearrange("p c t -> p (c t)").bitcast(F32)
        pools = [[self.arena, 0, self.ARENA], [xbw, 0, 4 * (cfg.nt + 64)]]

        def alloc(words, dt=F32):
            for p in pools:
                if p[1] + words <= p[2]:
                    a = p[0][:, p[1]:p[1] + words]
                    p[1] += words
                    return a if dt == F32 else a.bitcast(dt)
            raise RuntimeError("arena overflow")
        alloc.pools = pools
        return alloc

    def v64(self, name, li, h):
        i = V64[(name, li)]
        return self.vecs64[0:64, i * H + h:i * H + h + 1]

    def mixer_rwkv(self, li):
        cfg, P = self.cfg, self.P
        SEQ, NT = cfg.seq, cfg.nt
        P.barrier()
        alloc = self.mk_alloc()
        C0 = float(np.exp(-0.5))
        stage = alloc(4096)
        rstage = Reg("tmstage")
        wts = []
        rw = Reg("tmw")
        for wi in range(4):
            wb_ = alloc(4096, BF16).rearrange("p (k n) -> p k n", k=NCH)
            src = self.d_wrkv[li, wi] if wi < 3 else self.d_wo[li]
            for half in range(2):
                P.op("sp", I("dma_start",
                    out=stage.rearrange("p (k n) -> p k n", k=4), in_=src[:, half * 4:half * 4 + 4, :]),
                    writes=[rstage], dma=rstage)
                eng = "pool" if half == 0 else "act"
                if eng == "pool":
                    P.op("pool", I("tensor_copy",
                        out=wb_[:, half * 4:half * 4 + 4, :], in_=stage.rearrange("p (k n) -> p k n", k=4)), reads=[rstage], writes=[rw])
                else:
                    P.op("act", I("copy",
                        out=wb_[:, half * 4:half * 4 + 4, :], in_=stage.rearrange("p (k n) -> p k n", k=4)), reads=[rstage], writes=[rw])
            wts.append(wb_)
        wr, wk, wv, wo = wts
        l1 = alloc(1152, BF16).rearrange("p (k n) -> p k n", k=NCH)
        l2 = alloc(2048, BF16).rearrange("p (k n) -> p k n", k=4)
        P.op("sp", I("dma_start", out=stage[:, 0:2304].rearrange("p (k n) -> p k n", k=NCH), in_=self.d_l1[li]),
             writes=[rstage], dma=rstage)
        P.op("pool", I("tensor_copy", out=l1, in_=stage[:, 0:2304].rearrange("p (k n) -> p k n", k=NCH)), reads=[rstage], writes=[rw])
        P.op("sp", I("dma_start", out=stage.rearrange("p (k n) -> p k n", k=4), in_=self.d_l2[li]),
             writes=[rstage], dma=rstage)
        P.op("pool", I("tensor_copy", out=l2, in_=stage.rearrange("p (k n) -> p k n", k=4)), reads=[rstage], writes=[rw])
        LO = {"w": (0, 64), "a": (64, 64), "v": (128, 32), "g": (160, 128)}
        P.barrier()
        alloc.pools.append([stage, 0, 4096])

        TT = 64
        HG = 4
        mixes = [alloc(256, BF16).rearrange("p (c t) -> p c t", c=NCH) for _ in range(6)]
        rmix = [Reg("mix%d" % i) for i in range(6)]
        xx = alloc(512).rearrange("p (c t) -> p c t", c=NCH)
        rxx = Reg("xx")
        xlast = alloc(8)
        rxlast = Reg("xlast")
        lo1 = {k: alloc(32, BF16) for k in LO}
        rlo1 = {k: Reg("lo1" + k) for k in LO}
        ygs = alloc(256, BF16).rearrange("p (c t) -> p c t", c=NCH)
        ygo = alloc(256, BF16).rearrange("p (c t) -> p c t", c=NCH)
        rygs, rygo = Reg("ygs"), Reg("ygo")
        NB = 12
        fb = [alloc(256) for _ in range(NB)]
        rfb = [Reg("fb%d" % i) for i in range(NB)]
        f3 = lambda i: fb[i][0:64, :].rearrange("p (h t) -> p h t", h=HG)
        AR = alloc(256, BF16)
        BT = alloc(128, BF16)
        KT = alloc(128, BF16)
        BH = alloc(128, BF16)
        KH = alloc(128, BF16)
        rAR, rBT, rKT, rBH, rKH = Reg("AR"), Reg("BT"), Reg("KT"), Reg("BH"), Reg("KH")
        bkv = alloc(384, BF16)
        rbkv = Reg("bkv")
        Mb = alloc(256, BF16)
        Mk = alloc(256, BF16)
        rMb, rMk = Reg("Mb"), Reg("Mk")
        nbuf = [alloc(128, BF16) for _ in range(6)]
        rnb = [Reg("nb%d" % i) for i in range(6)]
        Wt = alloc(128, BF16)
        Ut = alloc(128, BF16)
        rWt, rUt = Reg("Wt"), Reg("Ut")
        S32 = alloc(1024)
        Sb = alloc(512, BF16)
        rS32, rSb = Reg("S32"), Reg("Sb")
        sst = [alloc(256) for _ in range(2)]
        rsst = [Reg("sst%d" % i) for i in range(2)]
        s32s = alloc(256)
        sbs = alloc(128, BF16)
        rs32s, rsbs = Reg("s32s"), Reg("sbs")
        sost = [alloc(256) for _ in range(2)]
        rsost = [Reg("sost%d" % i) for i in range(2)]
        shT = alloc(128)
        rshT = Reg("shT")
        ones64 = self.cst[0:64, 128:192]
        ident = self.ident
        identb = self.cstb[:, 0:128]

        P.op("pool", I("memset", S32[0:64, :], 0.0), writes=[rS32])
        P.op("pool", I("memset", Sb[0:64, :], 0.0), writes=[rSb])
        P.op("pool", I("memset", xlast, 0.0), writes=[rxlast])
        for c in range(NCH):
            P.op("sp", I("dma_start", out=shT[:, c * 16:(c + 1) * 16], in_=self.d_sshift[li][:, c * 128:(c + 1) * 128].rearrange("b p -> p b"),
                         allow_slow_non_contiguous=True), writes=[rshT], dma=rshT)

        tiles = [(t0, 64, 1, False) for t0 in range(0, SEQ, TT)] + [(SEQ, 4, 16, True)]
        for (t0, C, nch, samp) in tiles:
            ti = self.tile_of(t0)
            rxt = self.rx[ti]
            msk = self.cst64[0:C, (256 if samp else 0):(256 if samp else 0) + 256]
            m_si = msk[:, 0:2 * C]
            m_lo = msk[:, 128:128 + C]
            m_ey = msk[:, 192:192 + C]
            xt = self.xT[:, :, t0:t0 + TT]
            if not samp:
                P.op("dve", I("tensor_tensor", out=xx[:, :, 1:TT], in0=self.xT[:, :, t0:t0 + TT - 1],
                                                             in1=self.xT[:, :, t0 + 1:t0 + TT], op=ALU.subtract),
                     reads=[rxt], writes=[rxx])
                P.op("dve", I("tensor_tensor", out=xx[:, :, 0], in0=xlast, in1=self.xT[:, :, t0], op=ALU.subtract),
                     reads=[rxt, rxlast], writes=[rxx])
                P.op("pool", I("tensor_copy", out=xlast, in_=self.xT[:, :, t0 + TT - 1]), reads=[rxt, rxx], writes=[rxlast])
                if t0 + TT == SEQ:
                    P.op("sp", I("dma_start", out=self.d_shift_p[li].rearrange("(c p) -> p c", p=128), in_=xlast,
                                                     allow_slow_non_contiguous=True), reads=[rxlast], dma=rxlast)
                    self.out_regs.append(rxlast)
            else:
                for c in range(NCH):
                    xv = self.xT[:, c, t0:t0 + TT].rearrange("p (b t) -> p b t", t=4)
                    xxv = xx[:, c, :].rearrange("p (b t) -> p b t", t=4)
                    P.op("dve", I("tensor_tensor", out=xxv[:, :, 1:4], in0=xv[:, :, 0:3], in1=xv[:, :, 1:4], op=ALU.subtract),
                         reads=[rxt], writes=[rxx])
                    P.op("dve", I("tensor_tensor", out=xxv[:, :, 0], in0=shT[:, c * 16:(c + 1) * 16], in1=xv[:, :, 0], op=ALU.subtract),
                         reads=[rxt, rshT], writes=[rxx])
                rsh_out = Reg("shout")
                for c in range(NCH):
                    P.op("sp", I("dma_start", out=self.d_shift_s[li][:, c * 128:(c + 1) * 128].rearrange("b p -> p b"),
                                 in_=self.xT[:, c, t0:t0 + TT].rearrange("p (b t) -> p b t", t=4)[:, :, 3],
                                 allow_slow_non_contiguous=True), reads=[rxt], dma=rxt)
                if rxt not in self.out_regs:
                    self.out_regs.append(rxt)
            for m in range(6):
                for c in range(NCH):
                    P.op("dve", I("scalar_tensor_tensor",
                        out=mixes[m][:, c, :], in0=xx[:, c, :], scalar=self.vec("mu", li, m, c), in1=self.xT[:, c, t0:t0 + TT],
                        op0=ALU.mult, op1=ALU.add), reads=[rxx, rxt, self.rvec], writes=[rmix[m]])
            for key, mi in (("w", 1), ("a", 4), ("v", 3), ("g", 5)):
                if key == "v" and li == 0:
                    continue
                o_, wdt = LO[key]
                pb = 0
                for kc in range(NCH):
                    P.op("pe", I("matmul",
                        self.ps[pb][0:wdt, 0:TT], lhsT=l1[:, kc, o_:o_ + wdt], rhs=mixes[mi][:, kc, :], start=(kc == 0), stop=(kc == NCH - 1)),
                        reads=[rw, rmix[mi]], writes=[self.rps[pb]])
                fn = {"w": AF.Tanh, "a": AF.Identity, "v": AF.Identity, "g": AF.Sigmoid}[key]
                P.op("act", I("activation", out=lo1[key][0:wdt, 0:TT], in_=self.ps[pb][0:wdt, 0:TT], func=fn),
                     reads=[self.rps[pb]], writes=[rlo1[key]])
            def group(hg):
                h0 = hg * HG
                NCOL = HG * TT
                psA, psB, psC, psD = 0, 1, 2, 3

                def proj(wmat, mi, pb):
                    for hh in range(HG):
                        h = h0 + hh
                        for kc in range(NCH):
                            P.op("pe", I("matmul",
                                self.ps[pb][0:64, hh * TT:(hh + 1) * TT], lhsT=wmat[:, kc, h * 64:(h + 1) * 64], rhs=mixes[mi][:, kc, :],
                                start=(kc == 0 and hh == 0), stop=(kc == NCH - 1), skip_group_check=True),
                                reads=[rw, rmix[mi]], writes=[self.rps[pb]])

                def lora2(key, pb):
                    o_, wdt = LO[key]
                    idx = {"w": 0, "a": 1, "v": 2, "g": 3}[key]
                    for hh in range(HG):
                        h = h0 + hh
                        P.op("pe", I("matmul",
                            self.ps[pb][0:64, hh * TT:(hh + 1) * TT], lhsT=l2[0:wdt, idx, h * 64:(h + 1) * 64], rhs=lo1[key][0:wdt, 0:TT],
                            start=(hh == 0), stop=True, skip_group_check=True), reads=[rw, rlo1[key]], writes=[self.rps[pb]])

                def F(i):
                    return fb[i][0:64, 0:NCOL]
                R32, K32, V32, SG, A32, G32, KK, KP, CS, PP, PI, PX = range(12)
                proj(wr, 0, psA)
                P.op("act", I("copy", out=F(R32), in_=self.ps[psA][0:64, 0:NCOL]), reads=[self.rps[psA]], writes=[rfb[R32]])
                proj(wk, 2, psB)
                P.op("act", I("copy", out=F(K32), in_=self.ps[psB][0:64, 0:NCOL]), reads=[self.rps[psB]], writes=[rfb[K32]])
                proj(wv, 3, psC)
                P.op("act", I("copy", out=F(V32), in_=self.ps[psC][0:64, 0:NCOL]), reads=[self.rps[psC]], writes=[rfb[V32]])
                lora2("w", psD)
                for hh in range(HG):
                    P.op("act", I("activation", out=F(SG)[:, hh * TT:(hh + 1) * TT], in_=self.ps[psD][0:64, hh * TT:(hh + 1) * TT],
                                                            func=AF.Sigmoid, bias=self.v64("w0", li, h0 + hh)),
                         reads=[self.rps[psD], self.rv64], writes=[rfb[SG]])
                lora2("a", psA)
                for hh in range(HG):
                    P.op("act", I("activation", out=F(A32)[:, hh * TT:(hh + 1) * TT], in_=self.ps[psA][0:64, hh * TT:(hh + 1) * TT],
                                                            func=AF.Sigmoid, bias=self.v64("a0", li, h0 + hh)),
                         reads=[self.rps[psA], self.rv64], writes=[rfb[A32]])
                lora2("g", psB)
                P.op("act", I("copy", out=F(G32), in_=self.ps[psB][0:64, 0:NCOL]), reads=[self.rps[psB]], writes=[rfb[G32]])
                vfv = self.d_vf[:, h0:h0 + HG, t0:t0 + TT]
                if li == 0:
                    P.op("sp", I("dma_start", out=vfv, in_=F(V32).rearrange("p (h t) -> p h t", h=HG)), reads=[rfb[V32]], dma=rfb[V32])
                else:
                    lora2("v", psC)
                    P.op("sp", I("dma_start", out=F(KK).rearrange("p (h t) -> p h t", h=HG), in_=vfv), writes=[rfb[KK]], dma=rfb[KK])
                    for hh in range(HG):
                        P.op("act", I("activation", out=F(KP)[:, hh * TT:(hh + 1) * TT], in_=self.ps[psC][0:64, hh * TT:(hh + 1) * TT],
                                                                func=AF.Sigmoid, bias=self.v64("v0", li, h0 + hh)),
                             reads=[self.rps[psC], self.rv64], writes=[rfb[KP]])
                    P.op("dve", I("tensor_tensor", out=F(KK), in0=F(KK), in1=F(V32), op=ALU.subtract), reads=[rfb[KK], rfb[V32]], writes=[rfb[KK]])
                    P.op("dve", I("tensor_tensor", out=F(KK), in0=F(KK), in1=F(KP), op=ALU.mult), reads=[rfb[KK], rfb[KP]], writes=[rfb[KK]])
                    P.op("dve", I("tensor_tensor", out=F(V32), in0=F(V32), in1=F(KK), op=ALU.add), reads=[rfb[KK], rfb[V32]], writes=[rfb[V32]])
                for hh in range(HG):
                    P.op("dve", I("tensor_scalar", out=F(KK)[:, hh * TT:(hh + 1) * TT], in0=F(K32)[:, hh * TT:(hh + 1) * TT],
                                                               scalar1=self.v64("k_k", li, h0 + hh), scalar2=None, op0=ALU.mult),
                         reads=[rfb[K32], self.rv64], writes=[rfb[KK]])
                P.op("act", I("activation", out=F(PP), in_=F(KK), func=AF.Square), reads=[rfb[KK]], writes=[rfb[PP]])
                P.op("pe", I("matmul", self.ps[psD][0:64, 0:NCOL], lhsT=ones64, rhs=F(PP), start=True, stop=True),
                     reads=[self.rcst, rfb[PP]], writes=[self.rps[psD]])
                P.op("act", I("activation", out=F(PP), in_=self.ps[psD][0:64, 0:NCOL], func=AF.Ln, bias=self.tiny[0:64, 0:1]),
                     reads=[self.rps[psD], self.rtiny], writes=[rfb[PP]])
                P.op("act", I("activation", out=F(PP), in_=F(PP), func=AF.Exp, scale=-0.5), reads=[rfb[PP]], writes=[rfb[PP]])
                P.op("dve", I("tensor_tensor", out=F(KK), in0=F(KK), in1=F(PP), op=ALU.mult), reads=[rfb[KK], rfb[PP]], writes=[rfb[KK]])
                for hh in range(HG):
                    P.op("dve", I("tensor_scalar", out=F(KP)[:, hh * TT:(hh + 1) * TT], in0=F(A32)[:, hh * TT:(hh + 1) * TT],
                                                               scalar1=-1.0, scalar2=self.v64("k_a", li, h0 + hh), op0=ALU.add, op1=ALU.mult),
                         reads=[rfb[A32], self.rv64], writes=[rfb[KP]])
                P.op("dve", I("scalar_tensor_tensor", out=F(KP), in0=F(KP), scalar=1.0, in1=F(K32), op0=ALU.add, op1=ALU.mult),
                     reads=[rfb[KP], rfb[K32]], writes=[rfb[KP]])
                for hh in range(HG):
                    for j in range(nch):
                        sl = slice(hh * TT + j * C, hh * TT + (j + 1) * C)
                        P.op("dve", I("tensor_tensor_scan", out=F(CS)[:, sl], data0=self.cst[0:64, 128:128 + C], data1=F(SG)[:, sl],
                                                                        initial=0.0, op0=ALU.mult, op1=ALU.add),
                             reads=[rfb[SG], self.rcst], writes=[rfb[CS]])
                P.op("act", I("activation", out=F(PP), in_=F(CS), func=AF.Exp, scale=-C0), reads=[rfb[CS]], writes=[rfb[PP]])
                P.op("act", I("activation", out=F(PI), in_=F(CS), func=AF.Exp, scale=C0), reads=[rfb[CS]], writes=[rfb[PI]])
                P.op("dve", I("tensor_tensor", out=F(PX), in0=F(CS), in1=F(SG), op=ALU.subtract), reads=[rfb[CS], rfb[SG]], writes=[rfb[PX]])
                P.op("act", I("activation", out=F(PX), in_=F(PX), func=AF.Exp, scale=-C0), reads=[rfb[PX]], writes=[rfb[PX]])
                ARv = AR[0:64, 0:2 * NCOL].rearrange("p (h j a c) -> p h j a c", h=HG, j=nch, a=2)
                v4 = lambda ap: ap.rearrange("p (h j c) -> p h j c", h=HG, j=nch)
                P.op("dve", I("scalar_tensor_tensor", out=ARv[:, :, :, 0, :], in0=v4(F(KK)), scalar=-1.0, in1=v4(F(PX)), op0=ALU.mult, op1=ALU.mult),
                     reads=[rfb[KK], rfb[PX]], writes=[rAR])
                P.op("dve", I("tensor_tensor", out=ARv[:, :, :, 1, :], in0=v4(F(R32)), in1=v4(F(PP)), op=ALU.mult),
                     reads=[rfb[R32], rfb[PP]], writes=[rAR])
                ppv = v4(F(PP))
                a_ = ppv.ap
                PCb = bass.AP(ppv.tensor, ppv.offset + (C - 1), [list(a_[0]), list(a_[1]), list(a_[2]), [0, C]])
                P.op("dve", I("tensor_tensor", out=F(SG), in0=F(KK), in1=F(A32), op=ALU.mult), reads=[rfb[KK], rfb[A32]], writes=[rfb[SG]])
                P.op("dve", I("tensor_tensor", out=F(SG), in0=F(SG), in1=F(PI), op=ALU.mult), reads=[rfb[SG], rfb[PI]], writes=[rfb[SG]])
                P.op("pool", I("tensor_copy", out=BT[0:64, 0:NCOL], in_=F(SG)), reads=[rfb[SG]], writes=[rBT])
                P.op("dve", I("tensor_tensor", out=v4(BH[0:64, 0:NCOL]), in0=v4(F(SG)), in1=PCb, op=ALU.mult), reads=[rfb[SG], rfb[PP]], writes=[rBH])
                P.op("dve", I("tensor_tensor", out=F(CS), in0=F(KP), in1=F(PI), op=ALU.mult), reads=[rfb[KP], rfb[PI]], writes=[rfb[CS]])
                P.op("pool", I("tensor_copy", out=KT[0:64, 0:NCOL], in_=F(CS)), reads=[rfb[CS]], writes=[rKT])
                P.op("dve", I("tensor_tensor", out=v4(KH[0:64, 0:NCOL]), in0=v4(F(CS)), in1=PCb, op=ALU.mult), reads=[rfb[CS], rfb[PP]], writes=[rKH])
                for hh in range(HG):
                    P.op("dve", I("scalar_tensor_tensor", out=F(K32)[:, hh * TT:(hh + 1) * TT], in0=F(R32)[:, hh * TT:(hh + 1) * TT],
                                                                      scalar=self.v64("r_k", li, h0 + hh), in1=F(KP)[:, hh * TT:(hh + 1) * TT],
                                                                      op0=ALU.mult, op1=ALU.mult),
                         reads=[rfb[R32], rfb[KP], self.rv64], writes=[rfb[K32]])
                P.op("pe", I("matmul", self.ps[psD][0:64, 0:NCOL], lhsT=ones64, rhs=F(K32), start=True, stop=True),
                     reads=[self.rcst, rfb[K32]], writes=[self.rps[psD]])
                P.op("dve", I("tensor_tensor", out=F(K32), in0=self.ps[psD][0:64, 0:NCOL], in1=F(V32), op=ALU.mult),
                     reads=[self.rps[psD], rfb[V32]], writes=[rfb[K32]])
                VB = fb[KP][0:64, 0:NCOL // 2].bitcast(BF16)
                P.op("pool", I("tensor_copy", out=VB, in_=F(V32)), reads=[rfb[V32], rKT, rKH], writes=[rfb[KP]])
                NM = HG * nch
                Mbv = Mb[0:C, 0:NM * 2 * C].rearrange("p (m c) -> p m c", m=NM)
                Mkv = Mk[0:C, 0:NM * 2 * C].rearrange("p (m c) -> p m c", m=NM)
                nv = [nbuf[i][0:C, 0:NM * C].rearrange("p (m c) -> p m c", m=NM) for i in range(6)]
                BTv = BT[0:64, 0:NCOL].rearrange("p (m c) -> p m c", m=NM)
                KTv = KT[0:64, 0:NCOL].rearrange("p (m c) -> p m c", m=NM)
                BHv = BH[0:64, 0:NCOL].rearrange("p (m c) -> p m c", m=NM)
                KHv = KH[0:64, 0:NCOL].rearrange("p (m c) -> p m c", m=NM)
                VBv = VB.rearrange("p (m c) -> p m c", m=NM)
                ARm = AR[0:64, 0:2 * NCOL].rearrange("p (m c) -> p m c", m=NM)
                pMb = self.ps[psA][0:C, 0:NM * 2 * C].rearrange("p (m c) -> p m c", m=NM)
                pMk = self.ps[psB][0:C, 0:NM * 2 * C].rearrange("p (m c) -> p m c", m=NM)
                pNT = self.ps[psC][0:C, 0:NM * C].rearrange("p (m c) -> p m c", m=NM)
                for m in range(NM):
                    P.op("pe", I("matmul", pMb[:, m, :], lhsT=BTv[:, m, :], rhs=ARm[:, m, :], start=(m == 0), stop=True, skip_group_check=True),
                         reads=[rBT, rAR], writes=[self.rps[psA]])
                for m in range(NM):
                    P.op("pe", I("matmul", pMk[:, m, :], lhsT=KTv[:, m, :], rhs=ARm[:, m, :], start=(m == 0), stop=True, skip_group_check=True),
                         reads=[rKT, rAR], writes=[self.rps[psB]])
                for m in range(NM):
                    P.op("pe", I("matmul", pNT[:, m, :], lhsT=ARm[:, m, 0:C], rhs=BTv[:, m, :], start=(m == 0), stop=True, skip_group_check=True),
                         reads=[rBT, rAR], writes=[self.rps[psC]])
                bc = lambda mk, w: bass.AP(mk.tensor, mk.offset, [list(mk.ap[0]), [0, NM], [1, w]])
                P.op("dve", I("tensor_tensor", out=Mbv, in0=pMb, in1=bc(m_si, 2 * C), op=ALU.mult), reads=[self.rps[psA], self.rc64], writes=[rMb])
                P.op("dve", I("tensor_tensor", out=Mkv, in0=pMk, in1=bc(m_si, 2 * C), op=ALU.mult), reads=[self.rps[psB], self.rc64], writes=[rMk])
                N_, NT_, N2_, N2T_, T_, Tt_ = range(6)
                P.op("dve", I("tensor_tensor", out=nv[NT_], in0=pNT, in1=bc(m_lo, C), op=ALU.mult), reads=[self.rps[psC], self.rc64], writes=[rnb[NT_]])
                P.op("pool", I("tensor_copy", out=nv[N_], in_=Mbv[:, :, 0:C]), reads=[rMb], writes=[rnb[N_]])
                P.op("pool", I("tensor_tensor", out=nv[T_], in0=Mbv[:, :, 0:C], in1=bc(m_ey, C), op=ALU.add), reads=[rMb, self.rc64], writes=[rnb[T_]])
                P.op("pool", I("tensor_tensor", out=nv[Tt_], in0=nv[NT_], in1=bc(m_ey, C), op=ALU.add), reads=[rnb[NT_], self.rc64], writes=[rnb[Tt_]])
                nsteps = {64: 5, 4: 1}[C]
                cur, curT, nxt, nxtT = N_, NT_, N2_, N2T_
                pX = [self.ps[i][0:C, 0:NM * C].rearrange("p (m c) -> p m c", m=NM) for i in (psA, psB, psC, psD)]
                for s_ in range(nsteps):
                    last = (s_ == nsteps - 1)
                    for m in range(NM):
                        P.op("pe", I("matmul", pX[0][:, m, :], lhsT=nv[curT][:, m, :], rhs=nv[cur][:, m, :], start=(m == 0), stop=True, skip_group_check=True),
                             reads=[rnb[cur], rnb[curT]], writes=[self.rps[psA]])
                    P.op("act", I("copy", out=nv[nxt], in_=pX[0]), reads=[self.rps[psA]], writes=[rnb[nxt]])
                    if not last:
                        for m in range(NM):
                            P.op("pe", I("matmul", pX[1][:, m, :], lhsT=nv[cur][:, m, :], rhs=nv[curT][:, m, :], start=(m == 0), stop=True, skip_group_check=True),
                                 reads=[rnb[cur], rnb[curT]], writes=[self.rps[psB]])
                        P.op("act", I("copy", out=nv[nxtT], in_=pX[1]), reads=[self.rps[psB]], writes=[rnb[nxtT]])
                    for m in range(NM):
                        P.op("pe", I("matmul", pX[2][:, m, :], lhsT=nv[Tt_][:, m, :], rhs=nv[nxt][:, m, :], start=(m == 0), stop=True, skip_group_check=True),
                             reads=[rnb[Tt_], rnb[nxt]], writes=[self.rps[psC]])
                    if not last:
                        for m in range(NM):
                            P.op("pe", I("matmul", pX[3][:, m, :], lhsT=nv[nxt][:, m, :], rhs=nv[Tt_][:, m, :], start=(m == 0), stop=True, skip_group_check=True),
                                 reads=[rnb[Tt_], rnb[nxt]], writes=[self.rps[psD]])
                    P.op("dve", I("tensor_tensor", out=nv[T_], in0=pX[2], in1=nv[T_], op=ALU.add), reads=[self.rps[psC], rnb[T_]], writes=[rnb[T_]])
                    if not last:
                        P.op("dve", I("tensor_tensor", out=nv[Tt_], in0=pX[3], in1=nv[Tt_], op=ALU.add), reads=[self.rps[psD], rnb[Tt_]], writes=[rnb[Tt_]])
                    cur, curT, nxt, nxtT = nxt, nxtT, cur, curT
                pY = self.ps[4][0:64, 0:NCOL].rearrange("p (h j c) -> p h j c", h=HG, j=nch)
                for j in range(nch):
                    mm_ = lambda hh: hh * nch + j
                    pT = self.ps[5][0:C, 0:HG * 3 * 32].bitcast(BF16).rearrange("p (h a k) -> p h a k", h=HG, a=3)
                    bkvv = bkv[0:C, 0:HG * 3 * 64].rearrange("p (h a k) -> p h a k", h=HG, a=3)
                    for hh in range(HG):
                        for a, srcv, rsrc in ((0, BHv, rBH), (1, KHv, rKH), (2, VBv, rfb[KP])):
                            P.op("pe", I("transpose", out=pT[:, hh, a, :], in_=srcv[:, mm_(hh), :], identity=identb[0:64, 0:64]),
                                 reads=[rsrc, self.rcstb], writes=[self.rps[5]])
                    P.op("act", I("copy", out=bkvv, in_=pT), reads=[self.rps[5]], writes=[rbkv])
                    if samp:
                        b = j
                        s = j % 2
                        snat = sst[s][0:64, :]
                        P.op("sp", I("dma_start", out=snat.rearrange("p (h k) -> p h k", h=HG),
                                                                        in_=self.d_swkv[li, b, h0:h0 + HG].rearrange("h v k -> v h k")),
                             writes=[rsst[s]], dma=rsst[s])
                        snb = s32s[0:64, 0:128].bitcast(BF16)
                        P.op("pool", I("tensor_copy", out=snb, in_=snat), reads=[rsst[s]], writes=[rs32s])
                        pS = self.ps[6][0:64, 0:128].bitcast(BF16).rearrange("p (h v) -> p h v", h=HG)
                        for hh in range(HG):
                            P.op("pe", I("transpose", out=pS[:, hh, :], in_=snb[:, hh * 64:(hh + 1) * 64], identity=identb[0:64, 0:64]),
                                 reads=[rs32s, self.rcstb], writes=[self.rps[6]])
                        sbcur = sbs[0:64, 0:256].rearrange("p (h v) -> p h v", h=HG)
                        P.op("act", I("copy", out=sbcur, in_=pS), reads=[self.rps[6]], writes=[rsbs])
                        rsb_cur = rsbs
                    else:
                        sbcur = Sb[0:64, h0 * 64:(h0 + HG) * 64].rearrange("p (h v) -> p h v", h=HG)
                        rsb_cur = rSb
                    pW = self.ps[6][0:C, 256:256 + HG * 64].rearrange("p (h v) -> p h v", h=HG)
                    for hh in range(HG):
                        P.op("pe", I("matmul", pW[:, hh, :], lhsT=ARm[:, mm_(hh), 0:C], rhs=sbcur[:, hh, :],
                                                                        start=(hh == 0), stop=False, skip_group_check=True),
                             reads=[rAR, rsb_cur], writes=[self.rps[6]])
                        P.op("pe", I("matmul", pW[:, hh, :], lhsT=Mkv[:, mm_(hh), 0:C], rhs=bkvv[:, hh, 2, :],
                                                           start=False, stop=True, skip_group_check=True),
                             reads=[rMk, rbkv], writes=[self.rps[6]])
                    Wtv = Wt[0:C, 0:HG * 64].rearrange("p (h v) -> p h v", h=HG)
                    Utv = Ut[0:C, 0:HG * 64].rearrange("p (h v) -> p h v", h=HG)
                    P.op("act", I("copy", out=Wtv, in_=pW), reads=[self.rps[6]], writes=[rWt])
                    pU = self.ps[7][0:C, 0:HG * 64].rearrange("p (h v) -> p h v", h=HG)
                    for hh in range(HG):
                        P.op("pe", I("matmul", pU[:, hh, :], lhsT=nv[T_][:, mm_(hh), :], rhs=Wtv[:, hh, :], start=(hh == 0), stop=True, skip_group_check=True),
                             reads=[rnb[T_], rWt], writes=[self.rps[7]])
                    P.op("act", I("copy", out=Utv, in_=pU), reads=[self.rps[7]], writes=[rUt])
                    for hh in range(HG):
                        P.op("pe", I("matmul", pY[:, hh, j, :], lhsT=sbcur[:, hh, :], rhs=ARm[:, mm_(hh), C:2 * C],
                                                                        start=(hh == 0 and j == 0), stop=False, skip_group_check=True),
                             reads=[rAR, rsb_cur], writes=[self.rps[4]])
                        P.op("pe", I("matmul", pY[:, hh, j, :], lhsT=Utv[:, hh, :], rhs=Mbv[:, mm_(hh), C:2 * C],
                                                           start=False, stop=False, skip_group_check=True), reads=[rUt, rMb], writes=[self.rps[4]])
                        P.op("pe", I("matmul", pY[:, hh, j, :], lhsT=bkvv[:, hh, 2, :], rhs=Mkv[:, mm_(hh), C:2 * C],
                                                           start=False, stop=True, skip_group_check=True), reads=[rbkv, rMk], writes=[self.rps[4]])
                    pD = self.ps[7][0:64, 256:256 + HG * 64].rearrange("p (h v) -> p h v", h=HG)
                    if not samp:
                        for hh in range(HG):
                            P.op("pe", I("matmul", pD[:, hh, :], lhsT=bkvv[:, hh, 0, :], rhs=Utv[:, hh, :], start=(hh == 0), stop=False, skip_group_check=True),
                                 reads=[rbkv, rUt], writes=[self.rps[7]])
                            P.op("pe", I("matmul", pD[:, hh, :], lhsT=bkvv[:, hh, 1, :], rhs=bkvv[:, hh, 2, :], start=False, stop=True, skip_group_check=True),
                                 reads=[rbkv], writes=[self.rps[7]])
                        for hh in range(HG):
                            h = h0 + hh
                            pc = F(PP)[:, hh * TT + (j + 1) * C - 1:hh * TT + (j + 1) * C]
                            P.op("dve", I("scalar_tensor_tensor",
                                out=S32[0:64, h * 64:(h + 1) * 64], in0=S32[0:64, h * 64:(h + 1) * 64], scalar=pc, in1=pD[:, hh, :],
                                op0=ALU.mult, op1=ALU.add), reads=[rS32, rfb[PP], self.rps[7]], writes=[rS32])
                        P.op("pool", I("tensor_copy", out=Sb[0:64, h0 * 64:(h0 + HG) * 64], in_=S32[0:64, h0 * 64:(h0 + HG) * 64]),
                             reads=[rS32], writes=[rSb])
                    else:
                        for hh in range(HG):
                            P.op("pe", I("matmul", pD[:, hh, :], lhsT=Utv[:, hh, :], rhs=bkvv[:, hh, 0, :], start=(hh == 0), stop=False, skip_group_check=True),
                                 reads=[rbkv, rUt], writes=[self.rps[7]])
                            P.op("pe", I("matmul", pD[:, hh, :], lhsT=bkvv[:, hh, 2, :], rhs=bkvv[:, hh, 1, :], start=False, stop=True, skip_group_check=True),
                                 reads=[rbkv], writes=[self.rps[7]])
                        dg = fb[CS][0:64, 0:HG * 64]
                        for hh in range(HG):
                            pc = F(PP)[:, hh * TT + (j + 1) * C - 1:hh * TT + (j + 1) * C]
                            P.op("dve", I("tensor_scalar", out=dg[:, hh * 64:(hh + 1) * 64], in0=ident[0:64, 0:64], scalar1=pc, scalar2=None, op0=ALU.mult),
                                 reads=[rfb[PP], self.rcst, rKT, rKH], writes=[rfb[CS]])
                        pR = self.ps[6][0:64, 0:256]
                        P.op("pe", I("matmul", self.ps[5][0:64, 256:512], lhsT=ones64, rhs=dg, start=True, stop=True, skip_group_check=True),
                             reads=[rfb[CS], self.rcst], writes=[self.rps[5]])
                        so = sost[s][0:64, :]
                        P.op("dve", I("tensor_tensor", out=so, in0=snat, in1=self.ps[5][0:64, 256:512], op=ALU.mult),
                             reads=[rsst[s], self.rps[5]], writes=[rsost[s]])
                        P.op("dve", I("tensor_tensor", out=so, in0=so, in1=self.ps[7][0:64, 256:512], op=ALU.add),
                             reads=[self.rps[7], rsost[s]], writes=[rsost[s]])
                        P.op("sp", I("dma_start", out=self.d_wkv_s[li, b, h0:h0 + HG].rearrange("h v k -> v h k"),
                                                                    in_=so.rearrange("p (h k) -> p h k", h=HG)), reads=[rsost[s]], dma=rsost[s])
                        if rsost[s] not in self.out_regs:
                            self.out_regs.append(rsost[s])
                Y32, YSQ, MU, RS = R32, SG, PI, PX
                P.op("act", I("copy", out=F(Y32), in_=self.ps[4][0:64, 0:NCOL]), reads=[self.rps[4], rAR], writes=[rfb[Y32]])
                P.op("act", I("activation", out=F(YSQ), in_=F(Y32), func=AF.Square), reads=[rfb[Y32], rBT, rBH], writes=[rfb[YSQ]])
                P.op("pe", I("matmul", self.ps[psA][0:64, 0:NCOL], lhsT=ones64, rhs=F(Y32), start=True, stop=True), reads=[self.rcst, rfb[Y32]], writes=[self.rps[psA]])
                P.op("pe", I("matmul", self.ps[psB][0:64, 0:NCOL], lhsT=ones64, rhs=F(YSQ), start=True, stop=True), reads=[self.rcst, rfb[YSQ]], writes=[self.rps[psB]])
                P.op("dve", I("tensor_scalar", out=F(MU), in0=self.ps[psA][0:64, 0:NCOL], scalar1=1.0 / DH, scalar2=None, op0=ALU.mult),
                     reads=[self.rps[psA]], writes=[rfb[MU]])
                P.op("dve", I("tensor_tensor", out=F(RS), in0=F(MU), in1=F(MU), op=ALU.mult), reads=[rfb[MU]], writes=[rfb[RS]])
                P.op("dve", I("scalar_tensor_tensor", out=F(RS), in0=self.ps[psB][0:64, 0:NCOL], scalar=1.0 / DH, in1=F(RS), op0=ALU.mult, op1=ALU.subtract),
                     reads=[self.rps[psB], rfb[RS]], writes=[rfb[RS]])
                P.op("act", I("activation", out=F(RS), in_=F(RS), func=AF.Ln, bias=self.gneps[0:64, 0:1]), reads=[rfb[RS], self.rtiny], writes=[rfb[RS]])
                P.op("act", I("activation", out=F(RS), in_=F(RS), func=AF.Exp, scale=-0.5), reads=[rfb[RS]], writes=[rfb[RS]])
                P.op("dve", I("tensor_tensor", out=F(Y32), in0=F(Y32), in1=F(MU), op=ALU.subtract), reads=[rfb[Y32], rfb[MU]], writes=[rfb[Y32]])
                P.op("dve", I("tensor_tensor", out=F(Y32), in0=F(Y32), in1=F(RS), op=ALU.mult), reads=[rfb[Y32], rfb[RS]], writes=[rfb[Y32]])
                for hh in range(HG):
                    sl = slice(hh * TT, (hh + 1) * TT)
                    P.op("act", I("activation", out=F(Y32)[:, sl], in_=F(Y32)[:, sl], func=AF.Identity,
                                                                   bias=self.v64("gn_b", li, h0 + hh), scale=self.v64("gn_g", li, h0 + hh)),
                         reads=[rfb[Y32], self.rv64], writes=[rfb[Y32]])
                P.op("dve", I("tensor_tensor", out=F(Y32), in0=F(Y32), in1=F(K32), op=ALU.add), reads=[rfb[Y32], rfb[K32]], writes=[rfb[Y32]])
                for hh in range(HG):
                    h = h0 + hh
                    dst, rd = (ygs, rygs) if h % 2 == 0 else (ygo, rygo)
                    sl = slice(hh * TT, (hh + 1) * TT)
                    P.op("dve", I("tensor_tensor", out=dst[0:64, h // 2, :], in0=F(Y32)[:, sl], in1=F(G32)[:, sl], op=ALU.mult),
                         reads=[rfb[Y32], rfb[G32]], writes=[rd])

            for hg in range(H // HG):
                group(hg)
            P.op("sp", I("dma_start", out=ygs[64:128, :, :], in_=ygo[0:64, :, :]), reads=[rygo], writes=[rygs], dma=rygs)
            for oc in range(NCH):
                pb = 1 + (oc % 2)
                for kc in range(NCH):
                    P.op("pe", I("matmul", self.ps[pb][:, 0:TT], lhsT=wo[:, kc, oc * 128:(oc + 1) * 128], rhs=ygs[:, kc, :],
                                                                    start=(kc == 0), stop=(kc == NCH - 1)),
                         reads=[rw, rygs], writes=[self.rps[pb]])
                P.op("dve", I("scalar_tensor_tensor",
                    out=self.xT[:, oc, t0:t0 + TT], in0=self.xT[:, oc, t0:t0 + TT], scalar=ALPHA, in1=self.ps[pb][:, 0:TT],
                    op0=ALU.mult, op1=ALU.add), reads=[self.rps[pb], rxt, rxlast, rxx] + rmix, writes=[rxt])
        for hq in range(4):
            pb = 3 + (hq % 2)
            for hh in range(4):
                h = hq * 4 + hh
                P.op("pe", I("transpose", out=self.ps[pb][0:64, hh * 64:(hh + 1) * 64], in_=S32[0:64, h * 64:(h + 1) * 64],
                                                                  identity=ident[0:64, 0:64]), reads=[rS32, self.rcst], writes=[self.rps[pb]])
            so = sost[hq % 2]
            P.op("act", I("copy", out=so[0:64, :], in_=self.ps[pb][0:64, 0:256]), reads=[self.rps[pb]], writes=[rsost[hq % 2]])
            P.op("sp", I("dma_start", out=self.d_wkv_p[li, hq * 4:hq * 4 + 4].rearrange("h v k -> v h k"),
                                                          in_=so[0:64, :].rearrange("p (h k) -> p h k", h=4)), reads=[rsost[hq % 2]], dma=rsost[hq % 2])
            if rsost[hq % 2] not in self.out_regs:
                self.out_regs.append(rsost[hq % 2])

    def load_w(self, alloc, src, stage, rstage, rw):
        P = self.P
        wb_ = alloc(4096, BF16).rearrange("p (k n) -> p k n", k=NCH)
        for half in range(2):
            P.op("sp", I("dma_start", out=stage.rearrange("p (k n) -> p k n", k=4), in_=src[:, half * 4:half * 4 + 4, :]),
                 writes=[rstage], dma=rstage)
            if half == 0:
                P.op("pool", I("tensor_copy", out=wb_[:, 0:4, :], in_=stage.rearrange("p (k n) -> p k n", k=4)), reads=[rstage], writes=[rw])
            else:
                P.op("act", I("copy", out=wb_[:, 4:8, :], in_=stage.rearrange("p (k n) -> p k n", k=4)), reads=[rstage], writes=[rw])
        return wb_

    def kv_proj(self):
        cfg, P = self.cfg, self.P
        SEQ, NT = cfg.seq, cfg.nt
        P.barrier()
        alloc = self.mk_alloc_arena()
        stage = alloc(4096)
        rstage, rw = Reg("kvstage"), Reg("kvw")
        wk = self.load_w(alloc, self.d_wkv2[0], stage, rstage, rw)
        wv = self.load_w(alloc, self.d_wkv2[1], stage, rstage, rw)
        st32 = [alloc(1024) for _ in range(2)]
        rst32 = [Reg("kvst%d" % i) for i in range(2)]
        vb16 = [alloc(512, BF16) for _ in range(2)]
        rvb16 = [Reg("vb16%d" % i) for i in range(2)]
        ktb = [alloc(256, BF16) for _ in range(2)]
        rktb = [Reg("ktb%d" % i) for i in range(2)]
        ttiles = [(t0, 128) for t0 in range(0, SEQ, 128)] + [(SEQ, 128)]
        ctr = 0
        for (t0, n) in ttiles:
            ti = self.tile_of(t0)
            samp = t0 >= SEQ
            for which, w_, dp, ds in ((0, wk, self.d_kp, self.d_ks), (1, wv, self.d_vp, self.d_vs)):
                s = ctr % 2
                ctr += 1
                for half in range(2):
                    pb = (ctr * 2 + half) % 4
                    for kc in range(NCH):
                        P.op("pe", I("matmul", self.ps[pb][:, :], lhsT=self.xb[:, kc, t0:t0 + 128], rhs=w_[:, kc, half * 512:(half + 1) * 512],
                                     start=(kc == 0), stop=(kc == NCH - 1)), reads=[rw, self.rb[ti]], writes=[self.rps[pb]])
                    P.op("act", I("copy", out=st32[s][:, half * 512:(half + 1) * 512], in_=self.ps[pb][:, :]), reads=[self.rps[pb]], writes=[rst32[s]])
                dst = ds[:, :] if samp else dp[t0:t0 + 128, :]
                P.op("sp", I("dma_start", out=dst, in_=st32[s]), reads=[rst32[s]], dma=rst32[s])
                if rst32[s] not in self.out_regs:
                    self.out_regs.append(rst32[s])
                if which == 1:
                    P.op("dve", I("tensor_copy", out=vb16[s], in_=st32[s]), reads=[rst32[s]], writes=[rvb16[s]])
                    P.op("sp", I("dma_start", out=self.d_VB[t0:t0 + 128, :], in_=vb16[s]), reads=[rvb16[s]], dma=rvb16[s])
        ctr = 0
        for oc in range(NCH):
            for ti, (t0, n) in enumerate(self.tiles):
                s = ctr % 2
                pb = 4 + ctr % 2
                ctr += 1
                for kc in range(NCH):
                    P.op("pe", I("matmul", self.ps[pb][:, 0:n], lhsT=wk[:, kc, oc * 128:(oc + 1) * 128], rhs=self.xb[:, kc, t0:t0 + n],
                                 start=(kc == 0), stop=(kc == NCH - 1)), reads=[rw, self.rb[ti]], writes=[self.rps[pb]])
                P.op("act", I("copy", out=ktb[s][:, 0:n], in_=self.ps[pb][:, 0:n]), reads=[self.rps[pb]], writes=[rktb[s]])
                P.op("sp", I("dma_start", out=self.d_KT[oc, :, t0:t0 + n], in_=ktb[s][:, 0:n]), reads=[rktb[s]], dma=rktb[s])
        P.barrier()

    def mk_alloc_arena(self):
        pools = [[self.arena, 0, self.ARENA]]

        def alloc(words, dt=F32):
            for p in pools:
                if p[1] + words <= p[2]:
                    a = p[0][:, p[1]:p[1] + words]
                    p[1] += words
                    return a if dt == F32 else a.bitcast(dt)
            raise RuntimeError("arena overflow (%d words)" % words)
        alloc.pools = pools
        return alloc

    def mixer_sb(self, j):
        cfg, P = self.cfg, self.P
        SEQ, NT, NPG, SBN = cfg.seq, cfg.nt, cfg.npg, cfg.sb
        NQB = SEQ // 128
        P.barrier()
        alloc = self.mk_alloc_arena()
        stage = alloc(4096)
        rstage, rw = Reg("sbstage"), Reg("sbw")
        wq = self.load_w(alloc, self.d_wq[j], stage, rstage, rw)
        wo = self.load_w(alloc, self.d_wo2[j], stage, rstage, rw)
        qT = alloc(4 * NT, BF16).rearrange("p (c t) -> p c t", c=NCH)
        rqT = Reg("qT")
        oT = self.xb
        roT = Reg("oT")
        bias = self.sbb[:, j * H:(j + 1) * H]
        ident, identb = self.ident, self.cstb[:, 0:128]
        ones = self.ones
        tri = self.cst[:, 384:512]
        dmask = self.cst[:, 512:640]
        ctr = 0
        for oc in range(NCH):
            for ti, (t0, n) in enumerate(self.tiles):
                pb = ctr % 2
                ctr += 1
                for kc in range(NCH):
                    P.op("pe", I("matmul", self.ps[pb][:, 0:n], lhsT=wq[:, kc, oc * 128:(oc + 1) * 128], rhs=self.xb[:, kc, t0:t0 + n],
                                 start=(kc == 0), stop=(kc == NCH - 1)), reads=[rw, self.rb[ti]], writes=[self.rps[pb]])
                P.op("act", I("activation", out=qT[:, oc, t0:t0 + n], in_=self.ps[pb][:, 0:n], func=AF.Identity, scale=SB_SCALE),
                     reads=[self.rps[pb]], writes=[rqT])
        P.barrier()
        wk_ = {}
        for nm in ("E", "L", "lb", "Ls", "t"):
            wk_[nm] = [alloc(128) for _ in range(2)]
        rwk = {nm: [Reg(nm + "0"), Reg(nm + "1")] for nm in wk_}
        Wt = [alloc(64, BF16) for _ in range(2)]
        rWt = [Reg("W0"), Reg("W1")]
        alloc.pools.append([stage, 0, 4096])
        pmark = [q[1] for q in alloc.pools]
        KTp = [alloc(SEQ // 2, BF16) for _ in range(2)]
        VZ = [[alloc(NQB * 64, BF16).rearrange("p (k d) -> p k d", k=NQB) for e in range(2)] for _ in range(2)]
        rKTp = [Reg("KTp0"), Reg("KTp1")]
        rVZ = [Reg("VZ0"), Reg("VZ1")]
        for s in range(2):
            for e in range(2):
                P.op("pool", I("memset", VZ[s][e], 0.0), writes=[rVZ[s]])
        blk = 0
        import os
        for c in range(NCH if not os.environ.get("SKIP_SBP") else 0):
            s = c % 2
            P.op("sp", I("dma_start", out=KTp[s], in_=self.d_KT[c, :, 0:SEQ]), writes=[rKTp[s]], dma=rKTp[s])
            for e in range(2):
                P.op("sp", I("dma_start", out=VZ[s][e][:, :, e * 64:(e + 1) * 64],
                             in_=self.d_VB[0:SEQ, c * 128 + e * 64:c * 128 + (e + 1) * 64].rearrange("(k p) d -> p k d", p=128)),
                     writes=[rVZ[s]], dma=rVZ[s])
            for qb in range(NQB):
                po = 6 + (qb % 2)
                first = True
                for e in range(2):
                    h = 2 * c + e
                    bh = bias[:, h:h + 1]
                    qv = qT[e * 64:(e + 1) * 64, c, qb * 128:(qb + 1) * 128]
                    for kb in range(qb, -1, -1):
                        w = blk % 2
                        blk += 1
                        pz, pc = (blk % 2), 2 + (blk % 2)
                        diag = (kb == qb)
                        E, L, lb, Ls, tt_ = (wk_[nm][w] for nm in ("E", "L", "lb", "Ls", "t"))
                        rE, rL, rlb, rLs, rtt = (rwk[nm][w] for nm in ("E", "L", "lb", "Ls", "t"))
                        P.op("pe", I("matmul", self.ps[pz][:, 0:128], lhsT=KTp[s][e * 64:(e + 1) * 64, kb * 128:(kb + 1) * 128], rhs=qv, start=True, stop=True),
                             reads=[rKTp[s], rqT], writes=[self.rps[pz]])
                        P.op("act", I("activation", out=E, in_=self.ps[pz][:, 0:128], func=AF.Exp, bias=bh), reads=[self.rps[pz], self.rsbb], writes=[rE])
                        P.op("act", I("activation", out=L, in_=E, func=AF.Ln, bias=self.onec[:, 0:1]), reads=[rE, self.rtiny], writes=[rL])
                        P.op("dve", I("scalar_tensor_tensor", out=lb, in0=self.ps[pz][:, 0:128], scalar=bh, in1=L, op0=ALU.add, op1=ALU.subtract),
                             reads=[self.rps[pz], rL, self.rsbb], writes=[rlb])
                        if diag:
                            P.op("dve", I("tensor_tensor", out=L, in0=L, in1=dmask, op=ALU.mult), reads=[rL, self.rcst], writes=[rL])
                        Lsum = wk_["Ls"][0]
                        rLsum = rwk["Ls"][0]
                        P.op("pe", I("matmul", self.ps[pc][:, 0:128], lhsT=tri, rhs=L, start=True, stop=diag), reads=[self.rcst, rL], writes=[self.rps[pc]])
                        if not diag:
                            P.op("pe", I("matmul", self.ps[pc][:, 0:128], lhsT=ones, rhs=Lsum, start=False, stop=True), reads=[self.rcst, rLsum], writes=[self.rps[pc]])
                        P.op("dve", I("tensor_tensor", out=tt_, in0=lb, in1=self.ps[pc][:, 0:128], op=ALU.subtract), reads=[rlb, self.rps[pc]], writes=[rtt])
                        if diag:
                            P.op("pool", I("tensor_copy", out=Lsum, in_=L), reads=[rL], writes=[rLsum])
                            P.op("act", I("activation", out=tt_, in_=tt_, func=AF.Exp), reads=[rtt], writes=[rtt])
                            P.op("dve", I("tensor_tensor", out=Wt[w], in0=tt_, in1=dmask, op=ALU.mult), reads=[rtt, self.rcst], writes=[rWt[w]])
                        else:
                            if kb > 0:
                                P.op("pool", I("tensor_tensor", out=Lsum, in0=Lsum, in1=L, op=ALU.add), reads=[rL, rLsum], writes=[rLsum])
                            P.op("act", I("activation", out=Wt[w], in_=tt_, func=AF.Exp), reads=[rtt], writes=[rWt[w]])
                        last = (e == 1 and kb == 0)
                        P.op("pe", I("matmul", self.ps[po][:, 0:128], lhsT=VZ[s][e][:, kb, :], rhs=Wt[w], start=first, stop=last),
                             reads=[rVZ[s], rWt[w]], writes=[self.rps[po]])
                        first = False
                P.op("act", I("copy", out=oT[:, c, qb * 128:(qb + 1) * 128], in_=self.ps[po][:, 0:128]), reads=[self.rps[po]], writes=[roT])
        P.barrier()
        for q, m in zip(alloc.pools, pmark):
            q[1] = m
        if not os.environ.get("SKIP_SBS"):
            self.sb_sample(j, alloc, qT, rqT, oT, roT, wk_, rwk)
        P.barrier()
        ctr = 0
        for oc in range(NCH):
            for ti, (t0, n) in enumerate(self.tiles):
                pb = ctr % 2
                ctr += 1
                for kc in range(NCH):
                    P.op("pe", I("matmul", self.ps[pb][:, 0:n], lhsT=wo[:, kc, oc * 128:(oc + 1) * 128], rhs=oT[:, kc, t0:t0 + n],
                                 start=(kc == 0), stop=(kc == NCH - 1)), reads=[rw, roT], writes=[self.rps[pb]])
                P.op("dve", I("scalar_tensor_tensor", out=self.xT[:, oc, t0:t0 + n], in0=self.xT[:, oc, t0:t0 + n], scalar=ALPHA, in1=self.ps[pb][:, 0:n],
                              op0=ALU.mult, op1=ALU.add), reads=[self.rps[pb], self.rx[ti]], writes=[self.rx[ti]])

    def sb_sample(self, j, alloc, qT, rqT, oT, roT, wk_, rwk):
        cfg, P = self.cfg, self.P
        SEQ, NT, NPG, SBN = cfg.seq, cfg.nt, cfg.npg, cfg.sb
        bias = self.sbb[:, j * H:(j + 1) * H]
        identb = self.cstb[:, 0:128]
        ones, tri = self.ones, self.cst[:, 384:512]
        Kst = [alloc(1024)] * 2
        Vst = [alloc(1024)] * 2
        rKst, rVst = [Reg("Kst0")] * 2, [Reg("Vst0")] * 2
        Kb = alloc(512, BF16)
        Vb = alloc(512, BF16)
        KTg = alloc(512, BF16).rearrange("p (c k) -> p c k", c=NCH)
        rKb, rVb, rKTg = Reg("Kb"), Reg("Vb"), Reg("KTg")
        KTn = alloc(256, BF16).rearrange("p (c t) -> p c t", c=NCH)
        rKTn = Reg("KTn")
        brow = alloc(64)
        rbrow = Reg("brow")
        Vn = alloc(512, BF16)
        rVn = Reg("Vn")
        P.op("sp", I("dma_start", out=Vn[0:64, :], in_=self.d_VB[SEQ:SEQ + 64, :]), writes=[rVn], dma=rVn)
        P.op("sp", I("dma_start", out=KTn, in_=self.d_KT[:, :, SEQ:SEQ + 64].rearrange("c p t -> p c t")), writes=[rKTn], dma=rKTn)
        bsrc = bass.AP(bias.tensor, bias.offset, [list(bias.ap[0]), [1, H], [0, 4]])
        P.op("dve", I("tensor_copy", out=brow.rearrange("p (h q) -> p h q", h=H), in_=bsrc), reads=[self.rsbb], writes=[rbrow])
        m3 = self.cst64[0:64, 512:576]
        blk = 0
        lsn = 128
        for b in range(SBN):
            po = 6 + (b % 2)
            q0 = SEQ + 4 * b
            Lsum = wk_["Ls"][0][:, 0:64]
            rLsum = rwk["Ls"][0]
            blocks = [("new", None)] + [("page", pg) for pg in range(NPG - 1, -1, -1)]
            import os
            if os.environ.get("SB_NOPAGE"):
                blocks = blocks[:1]
            if os.environ.get("SB_NONEW"):
                blocks = blocks[1:]
            for bi, (kind, pg) in enumerate(blocks):
                w = blk % 2
                blk += 1
                pz, pc = (blk % 2), 2 + (blk % 2)
                E, L, lb, tt_ = (wk_[nm][w][:, 0:64] for nm in ("E", "L", "lb", "t"))
                rE, rL, rlb, rtt = (rwk[nm][w] for nm in ("E", "L", "lb", "t"))
                Wt = wk_["Ls"][1][:, 0:32].bitcast(BF16)
                rWt = rwk["Ls"][1]
                if kind == "new":
                    nk = 64
                    s = 0
                    ktv = lambda e, c: KTn[e * 64:(e + 1) * 64, c, 0:64]
                    rkt = rKTn
                    mb3 = bass.AP(m3.tensor, m3.offset + 4 * b, [list(m3.ap[0]), [0, H], [1, 4]])
                    v3 = lambda ap: ap.rearrange("p (h q) -> p h q", h=H)
                    Vcur, rVcur = Vn, rVn
                else:
                    nk = 128
                    s = blk % 2
                    col = b * NPG + pg
                    P.op("pool", I("indirect_dma_start", out=Kst[s][:, :], out_offset=None, in_=self.d_ck[:, :],
                                   in_offset=bass.IndirectOffsetOnAxis(ap=self.pidx[:, col:col + 1], axis=0)),
                         reads=[self.rpidx], writes=[rKst[s]], dma=rKst[s])
                    P.op("pool", I("indirect_dma_start", out=Vst[s][:, :], out_offset=None, in_=self.d_cv[:, :],
                                   in_offset=bass.IndirectOffsetOnAxis(ap=self.pidx[:, col:col + 1], axis=0)),
                         reads=[self.rpidx], writes=[rVst[s]], dma=rVst[s])
                    P.op("dve", I("tensor_copy", out=Kb, in_=Kst[s]), reads=[rKst[s]], writes=[rKb])
                    P.op("act", I("copy", out=Vb, in_=Vst[s]), reads=[rVst[s]], writes=[rVb])
                    pT = self.ps[4 + (blk % 2)][:, 0:512].bitcast(BF16).rearrange("p (c k) -> p c k", c=NCH)
                    for c in range(NCH):
                        P.op("pe", I("transpose", out=pT[:, c, :], in_=Kb[:, c * 128:(c + 1) * 128], identity=identb),
                             reads=[rKb, self.rcstb], writes=[self.rps[4 + (blk % 2)]])
                    P.op("act", I("copy", out=KTg, in_=pT), reads=[self.rps[4 + (blk % 2)]], writes=[rKTg])
                    ktv = lambda e, c: KTg[e * 64:(e + 1) * 64, c, :]
                    rkt = rKTg
                    Vcur, rVcur = Vb, rVb
                for h in range(H):
                    c, e = h // 2, h % 2
                    P.op("pe", I("matmul", self.ps[pz][0:nk, h * 4:(h + 1) * 4], lhsT=ktv(e, c), rhs=qT[e * 64:(e + 1) * 64, c, q0:q0 + 4],
                                 start=(h == 0), stop=True, skip_group_check=True), reads=[rkt, rqT], writes=[self.rps[pz]])
                zb = lb
                P.op("dve", I("tensor_tensor", out=zb[0:nk, :], in0=self.ps[pz][0:nk, 0:64], in1=brow[0:nk, :], op=ALU.add),
                     reads=[self.rps[pz], rbrow], writes=[rlb])
                P.op("act", I("activation", out=E[0:nk, :], in_=zb[0:nk, :], func=AF.Exp), reads=[rlb], writes=[rE])
                P.op("act", I("activation", out=L[0:nk, :], in_=E[0:nk, :], func=AF.Ln, bias=self.onec[0:nk, 0:1]), reads=[rE, self.rtiny], writes=[rL])
                P.op("dve", I("tensor_tensor", out=lb[0:nk, :], in0=zb[0:nk, :], in1=L[0:nk, :], op=ALU.subtract), reads=[rlb, rL], writes=[rlb])
                if kind == "new":
                    P.op("dve", I("tensor_tensor", out=v3(L[0:nk, :]), in0=v3(L[0:nk, :]), in1=mb3, op=ALU.mult), reads=[rL, self.rc64], writes=[rL])
                P.op("pe", I("matmul", self.ps[pc][0:nk, 0:64], lhsT=tri[0:nk, 0:nk], rhs=L[0:nk, :], start=True, stop=(kind == "new")),
                     reads=[self.rcst, rL], writes=[self.rps[pc]])
                if kind != "new":
                    P.op("pe", I("matmul", self.ps[pc][0:nk, 0:64], lhsT=ones[0:lsn, 0:nk], rhs=Lsum[0:lsn, :], start=False, stop=True),
                         reads=[self.rcst, rLsum], writes=[self.rps[pc]])
                P.op("dve", I("tensor_tensor", out=tt_[0:nk, :], in0=lb[0:nk, :], in1=self.ps[pc][0:nk, 0:64], op=ALU.subtract),
                     reads=[rlb, self.rps[pc]], writes=[rtt])
                if kind == "new":
                    P.op("pool", I("memset", Lsum, 0.0), writes=[rLsum])
                    P.op("pool", I("tensor_copy", out=Lsum[0:64, :], in_=L[0:64, :]), reads=[rL], writes=[rLsum])
                    P.op("act", I("activation", out=tt_[0:nk, :], in_=tt_[0:nk, :], func=AF.Exp), reads=[rtt], writes=[rtt])
                    P.op("dve", I("tensor_tensor", out=v3(Wt[0:nk, :]), in0=v3(tt_[0:nk, :]), in1=mb3, op=ALU.mult), reads=[rtt, self.rc64], writes=[rWt])
                else:
                    if bi < len(blocks) - 1:
                        P.op("pool", I("tensor_tensor", out=Lsum, in0=Lsum, in1=L, op=ALU.add), reads=[rL, rLsum], writes=[rLsum])
                    P.op("act", I("activation", out=Wt[0:nk, :], in_=tt_[0:nk, :], func=AF.Exp), reads=[rtt], writes=[rWt])
                for c in range(NCH):
                    P.op("pe", I("matmul", self.ps[po][:, c * 8:(c + 1) * 8], lhsT=Vcur[0:nk, c * 128:(c + 1) * 128], rhs=Wt[0:nk, c * 8:(c + 1) * 8],
                                 start=(bi == 0 and c == 0), stop=(bi == len(blocks) - 1), skip_group_check=True),
                         reads=[rVcur, rWt], writes=[self.rps[po]])
            pov = self.ps[po][:, 0:64].rearrange("p (c x) -> p c x", c=NCH)
            P.op("act", I("copy", out=oT[0:64, :, q0:q0 + 4], in_=pov[0:64, :, 0:4]), reads=[self.rps[po]], writes=[roT])
            P.op("act", I("copy", out=oT[64:128, :, q0:q0 + 4], in_=pov[64:128, :, 4:8]), reads=[self.rps[po]], writes=[roT])

    def layernorm(self, li, slot):
        cfg, P = self.cfg, self.P
        P.barrier()
        o = 0
        sq = self.aview(o, NCH * 512).rearrange("p (c t) -> p c t", c=NCH); o += NCH * 512
        mean = self.aview(o, 512); o += 512
        rstd = self.aview(o, 512); o += 512
        tmp = self.aview(o, 512); o += 512
        rsq, rmean, rrstd, rtmp = Reg("sq"), Reg("mean"), Reg("rstd"), Reg("tmp")
        epsb = self.aview(o, 1); o += 1
        reps = Reg("eps")
        P.op("pool", I("memset", epsb, LN_EPS), writes=[reps])
        for ti, (t0, n) in enumerate(self.tiles):
            b1, b2 = (2 * ti) % 8, (2 * ti + 1) % 8
            P.op("act", I("activation", out=sq[:, :, 0:n], in_=self.xT[:, :, t0:t0 + n], func=AF.Square),
                 reads=[self.rx[ti]], writes=[rsq])
            for c in range(NCH):
                P.op("pe", I("matmul", self.ps[b1][:, 0:n], lhsT=self.ones, rhs=self.xT[:, c, t0:t0 + n],
                                                                    start=(c == 0), stop=(c == NCH - 1)),
                     reads=[self.rcst, self.rx[ti]], writes=[self.rps[b1]])
            for c in range(NCH):
                P.op("pe", I("matmul", self.ps[b2][:, 0:n], lhsT=self.ones, rhs=sq[:, c, 0:n],
                                                              start=(c == 0), stop=(c == NCH - 1)),
                     reads=[self.rcst, rsq], writes=[self.rps[b2]])
            P.op("dve", I("tensor_scalar", out=mean[:, 0:n], in0=self.ps[b1][:, 0:n], scalar1=1.0 / D, scalar2=None, op0=ALU.mult),
                 reads=[self.rps[b1]], writes=[rmean])
            P.op("dve", I("tensor_tensor", out=tmp[:, 0:n], in0=mean[:, 0:n], in1=mean[:, 0:n], op=ALU.mult),
                 reads=[rmean], writes=[rtmp])
            P.op("dve", I("scalar_tensor_tensor", out=tmp[:, 0:n], in0=self.ps[b2][:, 0:n], scalar=1.0 / D, in1=tmp[:, 0:n],
                                                                   op0=ALU.mult, op1=ALU.subtract),
                 reads=[self.rps[b2], rtmp], writes=[rtmp])
            P.op("act", I("activation", out=tmp[:, 0:n], in_=tmp[:, 0:n], func=AF.Ln, bias=epsb[:, 0:1]),
                 reads=[rtmp, reps], writes=[rtmp])
            P.op("act", I("activation", out=rstd[:, 0:n], in_=tmp[:, 0:n], func=AF.Exp, scale=-0.5),
                 reads=[rtmp], writes=[rrstd])
            for c in range(NCH):
                xs = self.xT[:, c, t0:t0 + n]
                P.op("dve", I("tensor_tensor", out=xs, in0=xs, in1=mean[:, 0:n], op=ALU.subtract),
                     reads=[self.rx[ti], rmean], writes=[self.rx[ti]])
                P.op("dve", I("tensor_tensor", out=xs, in0=xs, in1=rstd[:, 0:n], op=ALU.mult),
                     reads=[self.rx[ti], rrstd], writes=[self.rx[ti]])
                P.op("act", I("activation", out=xs, in_=xs, func=AF.Identity,
                                                              bias=self.vec("ln_b", li, slot, c), scale=self.vec("ln_g", li, slot, c)),
                     reads=[self.rx[ti], self.rvec], writes=[self.rx[ti]])
                P.op("pool", I("tensor_copy", out=self.xb[:, c, t0:t0 + n], in_=xs),
                     reads=[self.rx[ti]], writes=[self.rb[ti]])


V64 = {}
for _li in range(NA):
    for _n in ("w0", "a0", "k_k", "k_a", "r_k", "gn_g", "gn_b", "v0"):
        V64[(_n, _li)] = len(V64)
NV64 = len(V64)


def pack_vecs64(inp):
    out = np.zeros((NV64, D), np.float32)
    for (n, li), i in V64.items():
        if n == "v0":
            v = inp["tm_v0"][0] if li == 1 else np.zeros(D, np.float32)
        elif n == "r_k":
            v = inp["tm_r_k"][li].reshape(D)
        else:
            v = inp["tm_" + n][li]
        out[i] = v
    return np.ascontiguousarray(out.reshape(NV64, H, DH).transpose(2, 0, 1).reshape(DH, NV64 * H))


def pack_consts64():
    i = np.arange(64)
    outs = []
    for C in (64, 4):
        su = (i[:, None] < i[None, :]).astype(np.float32)
        iu = (i[:, None] <= i[None, :]).astype(np.float32)
        lo = (i[:, None] > i[None, :]).astype(np.float32)
        ey = (i[:, None] == i[None, :]).astype(np.float32)
        m = np.zeros((64, 4 * 64), np.float32)
        m[:C, 0:C] = su[:C, :C]
        m[:C, C:2 * C] = iu[:C, :C]
        m[:C, 128:128 + C] = lo[:C, :C]
        m[:C, 192:192 + C] = ey[:C, :C]
        outs.append(m)
    m3 = np.zeros((64, 16, 4), np.float32)
    for s_ in range(64):
        for q in range(4):
            if s_ % 4 < q:
                m3[s_, s_ // 4, q] = 1.0
    outs.append(m3.reshape(64, 64))
    return np.ascontiguousarray(np.concatenate(outs, axis=1))


def pack_tm(inp):
    wrkv = np.stack([pack_proj(inp["tm_w_r"]), pack_proj(inp["tm_w_k"]), pack_proj(inp["tm_w_v"])], 1)
    v1 = np.concatenate([np.zeros_like(inp["tm_v1"]), inp["tm_v1"]], 0)
    v2 = np.concatenate([np.zeros_like(inp["tm_v2"]), inp["tm_v2"]], 0)
    l1 = np.concatenate([inp["tm_w1"], inp["tm_a1"], v1, inp["tm_g1"]], axis=2)
    l1 = pack_proj(l1)
    l2 = np.zeros((NA, 128, 4, D), np.float32)
    l2[:, :64, 0] = inp["tm_w2"]
    l2[:, :64, 1] = inp["tm_a2"]
    l2[:, :32, 2] = v2
    l2[:, :128, 3] = inp["tm_g2"]
    wo = np.ascontiguousarray(inp["tm_w_o"].reshape(NA, H, DH, D).transpose(0, 2, 1, 3))
    return np.ascontiguousarray(wrkv), l1, l2, wo


def make_in_maps(cfg, inp):
    vecs = pack_vecs(inp)
    cst = pack_consts()
    wffn = pack_ffn(inp)
    v64 = pack_vecs64(inp)
    c64 = pack_consts64()
    wrkv, l1, l2, _ = pack_tm(inp)
    wo = pack_proj(inp["tm_w_o"])
    wkv2 = np.ascontiguousarray(np.stack([pack_proj(inp["sb_w_k"]), pack_proj(inp["sb_w_v"])], 0))
    wq = pack_proj(inp["sb_w_q"])
    wo2 = pack_proj(inp["sb_w_o"])
    sbb = np.ascontiguousarray(np.broadcast_to(inp["sb_bias"].reshape(1, 2 * H), (128, 2 * H))).astype(np.float32)
    ck = inp["cache_k"].reshape(-1, D)
    cv = inp["cache_v"].reshape(-1, D)
    maps = []
    for c in range(cfg.ncores):
        m = {
            "xp": np.ascontiguousarray(inp["x_prompt"][c]),
            "xs": np.ascontiguousarray(np.concatenate([inp["x_sample"][c * cfg.sb:(c + 1) * cfg.sb].reshape(cfg.nts, D),
                                                       np.zeros((128 - cfg.nts, D), np.float32)], 0)),
            "vecs": vecs, "cst": cst, "wffn": wffn, "v64": v64, "c64": c64,
            "wrkv": wrkv, "wo": wo, "l1": l1, "l2": l2,
            "wkv2": wkv2, "wq": wq, "wo2": wo2, "sbb": sbb, "ck": ck, "cv": cv,
            "pt": np.ascontiguousarray(inp["page_table"][c * cfg.sb:(c + 1) * cfg.sb].reshape(1, -1).astype(np.int32)),
            "sshift": np.ascontiguousarray(inp["state_shift"][:, c * cfg.sb:(c + 1) * cfg.sb]),
            "swkv": np.ascontiguousarray(inp["state_wkv"][:, c * cfg.sb:(c + 1) * cfg.sb]),
        }
        maps.append(m)
    return maps


OUT_NAMES = ("yp", "ys", "wkv_p", "wkv_s", "shift_p", "shift_s", "kp", "vp", "ks", "vs")


def assemble(cfg, results):
    n = cfg.ncores
    SEQ = cfg.seq
    yp = np.stack([r["yp"] for r in results], 0)
    ys = np.concatenate([r["ys"][:cfg.nts].reshape(cfg.sb, cfg.dseq, D) for r in results], 0)
    wkv_p = np.stack([r["wkv_p"] for r in results], 1)
    shift_p = np.stack([r["shift_p"] for r in results], 1)
    kp = np.stack([r["kp"].reshape(SEQ // 128, 128, H, DH) for r in results], 0)
    vp = np.stack([r["vp"].reshape(SEQ // 128, 128, H, DH) for r in results], 0)
    wkv_s = np.concatenate([r["wkv_s"] for r in results], 1)
    shift_s = np.concatenate([r["shift_s"] for r in results], 1)
    ks = np.concatenate([r["ks"][:cfg.nts].reshape(cfg.sb, cfg.dseq, H, DH) for r in results], 0)
    vs = np.concatenate([r["vs"][:cfg.nts].reshape(cfg.sb, cfg.dseq, H, DH) for r in results], 0)
    return (yp, ys, wkv_p, shift_p, kp, vp, wkv_s, shift_s, ks, vs)


def kernel(**inp):
    cfg = Cfg()
    b = Builder(cfg)
    nc = b.build()
    inp = {k: np.asarray(v) for k, v in inp.items()}
    maps = make_in_maps(cfg, inp)
    res = run_bass_kernel_spmd(nc, maps, core_ids=list(range(cfg.ncores)))
    return assemble(cfg, res.results)
```

```python
import contextlib
import numpy as np
import concourse.bass as bass
import concourse.mybir as mybir
from concourse.bass_utils import run_bass_kernel_spmd

F32, BF16, I32 = mybir.dt.float32, mybir.dt.bfloat16, mybir.dt.int32
AF = mybir.ActivationFunctionType
ALU = mybir.AluOpType

D = 1024
NCH = 8
FF = 2816
NFC = 22
H = 16
DH = 64
DEPTH = 4
NA = 2
ALPHA = (2 * DEPTH) ** 0.25
LN_EPS = 1e-5
GN_EPS = 64e-5
SB_SCALE = DH ** -0.5


class Cfg:
    def __init__(self, seq=2048, npg=16, nphys=2560, sb=16, dseq=4, ncores=8):
        self.seq, self.npg, self.nphys, self.sb, self.dseq, self.ncores = seq, npg, nphys, sb, dseq, ncores
        self.nts = sb * dseq
        self.nt = seq + self.nts


class Reg:
    __slots__ = ("name", "w", "rs", "track", "id", "ndma", "psum")
    _n = 0

    def __init__(self, name=""):
        self.name = name
        self.w = None
        self.rs = {}
        self.track = True
        Reg._n += 1
        self.id = Reg._n
        self.ndma = 0
        self.psum = name.startswith("ps")


import os as _os
STRICT_SYNC = bool(_os.environ.get("KSTRICT"))
ENGS = ["pe", "act", "dve", "pool", "sp"]


class Prog:
    def __init__(self):
        self.ops = {e: [] for e in ENGS}
        self.seen = {e: {} for e in ENGS}
        self.dma_regs = {}

    def op(self, eng, fn, reads=(), writes=(), dma=None):
        need = {}

        def add(k, v):
            if need.get(k, 0) < v:
                need[k] = v

        isdma = dma is not None
        for r in reads:
            if r.w is not None:
                add(*r.w)
            if r.psum and eng in ("act", "dve"):
                other = "dve" if eng == "act" else "act"
                if other in r.rs:
                    add(other, r.rs[other])
        strict = isdma or (STRICT_SYNC and eng != "pe")
        for r in writes:
            if r.w is not None and (strict or r.w[0] != eng):
                add(*r.w)
            for k, v in r.rs.items():
                if strict or k != eng:
                    add(k, v)
        seen = self.seen[eng]
        waits = []
        for k, v in need.items():
            if seen.get(k, 0) < v:
                seen[k] = v
                waits.append((k, v))
                if isinstance(k, str):
                    self.ops[k][v - 1]["inc"] = True
        rec = dict(fn=fn, waits=waits, inc=False, dma=dma)
        self.ops[eng].append(rec)
        idx = len(self.ops[eng])
        if fn is None:
            return
        if isdma:
            dma.ndma += 1
            self.dma_regs[dma.id] = dma
            tok = (("d", dma.id), 16 * dma.ndma)
        else:
            tok = (eng, idx)
        for r in reads:
            if r.track:
                if r.rs.get(tok[0], 0) < tok[1]:
                    r.rs[tok[0]] = tok[1]
        for r in writes:
            r.w = tok
            r.rs = {}

    def barrier(self):
        toks = []
        for e in ENGS:
            for i in range(len(self.ops[e]), 0, -1):
                rec = self.ops[e][i - 1]
                if rec["fn"] is not None and rec["dma"] is None:
                    toks.append((e, i))
                    break
        for r in self.dma_regs.values():
            toks.append((("d", r.id), 16 * r.ndma))
        for e in ENGS:
            seen = self.seen[e]
            waits = []
            for k, v in toks:
                if seen.get(k, 0) < v:
                    seen[k] = v
                    waits.append((k, v))
                    if isinstance(k, str):
                        self.ops[k][v - 1]["inc"] = True
            self.ops[e].append(dict(fn=None, waits=waits, inc=False, dma=None))

    def emit(self, nc, es):
        sems = {e: es.enter_context(nc.semaphore("s_" + e)) for e in ENGS}
        dsem = {i: es.enter_context(nc.semaphore("d%d" % i)) for i in self.dma_regs}
        pref = {}
        for e in ENGS:
            c = 0
            arr = [0]
            for rec in self.ops[e]:
                if rec["inc"]:
                    c += 1
                arr.append(c)
            pref[e] = arr
        block = es.enter_context(nc.Block())

        def run(eh, eng):
            for rec in self.ops[eng]:
                for k, v in rec["waits"]:
                    if isinstance(k, str):
                        eh.wait_ge(sems[k], pref[k][v])
                    else:
                        eh.wait_ge(dsem[k[1]], v)
                if rec["fn"] is None:
                    continue
                ins = rec["fn"](eh)
                if rec["dma"] is not None:
                    ins.then_inc(dsem[rec["dma"].id], 16)
                elif rec["inc"]:
                    ins.then_inc(sems[eng], 1)

        @block.tensor
        def _(e):
            run(e, "pe")

        @block.scalar
        def _(e):
            run(e, "act")

        @block.vector
        def _(e):
            run(e, "dve")

        @block.gpsimd
        def _(e):
            run(e, "pool")

        @block.sync
        def _(e):
            run(e, "sp")


def vec_index():
    names = []
    for li in range(DEPTH):
        for s in range(3):
            names.append(("ln_g", li, s))
            names.append(("ln_b", li, s))
    for li in range(NA):
        for i in range(6):
            names.append(("mu", li, i))
        for n in ("w0", "a0", "k_k", "k_a", "r_k", "gn_g", "gn_b"):
            names.append((n, li, 0))
    names.append(("v0", 0, 0))
    return {n: i for i, n in enumerate(names)}


VIDX = vec_index()
NV = len(VIDX)


def pack_vecs(inp):
    out = np.zeros((NV, D), np.float32)
    for (n, li, s), i in VIDX.items():
        if n == "ln_g":
            v = inp["ln_g"][li, s]
        elif n == "ln_b":
            v = inp["ln_b"][li, s]
        elif n == "mu":
            v = inp["tm_mu"][li, s]
        elif n == "v0":
            v = inp["tm_v0"][0]
        elif n == "r_k":
            v = inp["tm_r_k"][li].reshape(D)
        else:
            v = inp["tm_" + n][li]
        out[i] = v
    return np.ascontiguousarray(out.reshape(NV, NCH, 128).transpose(2, 0, 1).reshape(128, NV * NCH))


def pack_consts():
    i = np.arange(128)
    ident = (i[:, None] == i[None, :]).astype(np.float32)
    ones = np.ones((128, 128), np.float32)
    blk = ((i[:, None] // 64) == (i[None, :] // 64)).astype(np.float32)
    tri = (i[:, None] > i[None, :]).astype(np.float32)
    dmask = (i[:, None] < i[None, :]).astype(np.float32)
    return np.ascontiguousarray(np.concatenate([ident, ones, blk, tri, dmask], axis=1))


def pack_ffn(inp):
    g = inp["ffn_w_gate"].reshape(8, NCH, 128, NFC, 128).transpose(0, 3, 2, 1, 4).reshape(8, NFC, 128, 1024)
    u = inp["ffn_w_up"].reshape(8, NCH, 128, NFC, 128).transpose(0, 3, 2, 1, 4).reshape(8, NFC, 128, 1024)
    d = inp["ffn_w_down"].reshape(8, NFC, 128, 1024)
    return np.ascontiguousarray(np.concatenate([g, u, d], axis=3))


def pack_proj(w):
    lead = w.shape[:-2]
    n = w.shape[-1]
    w = w.reshape(lead + (NCH, 128, n))
    nd = len(lead)
    perm = tuple(range(nd)) + (nd + 1, nd, nd + 2)
    return np.ascontiguousarray(w.transpose(perm))


def I(name, *a, **k):
    return lambda e: getattr(e, name)(*a, **k)


class Builder:
    def __init__(self, cfg, dbg=False, nlayers=DEPTH, stop_after=None, mixers=True, skip=()):
        self.mixers = mixers
        self.skip = skip
        self.cfg = cfg
        self.dbg = dbg
        self.nc = bass.Bass("TRN2", target_bir_lowering=False)
        self.P = Prog()
        self.es = contextlib.ExitStack()
        self.nlayers = nlayers
        self.stop_after = stop_after

    def sb(self, name, shape, dt):
        return self.es.enter_context(self.nc.sbuf_tensor(name, shape, dt))

    def dram_in(self, name, shape, dt=F32):
        return self.nc.dram_tensor(name, list(shape), dt, kind="ExternalInput").ap()

    def dram_out(self, name, shape, dt=F32):
        return self.nc.dram_tensor(name, list(shape), dt, kind="ExternalOutput").ap()

    def build(self):
        cfg, nc, P = self.cfg, self.nc, self.P
        NT, SEQ, NTS = cfg.nt, cfg.seq, cfg.nts
        self.tiles = [(t0, min(512, NT - t0)) for t0 in range(0, NT, 512)]
        self.d_xp = self.dram_in("xp", [SEQ, D])
        self.d_xs = self.dram_in("xs", [128, D])
        self.d_vecs = self.dram_in("vecs", [128, NV * NCH])
        self.d_cst = self.dram_in("cst", [128, 5 * 128])
        self.d_wffn = self.dram_in("wffn", [8, NFC, 128, 3072])
        SBN = cfg.sb
        self.d_v64 = self.dram_in("v64", [64, NV64 * H])
        self.d_c64 = self.dram_in("c64", [64, 576])
        self.d_wrkv = self.dram_in("wrkv", [NA, 3, 128, NCH, D])
        self.d_wo = self.dram_in("wo", [NA, 128, NCH, D])
        self.d_l1 = self.dram_in("l1", [NA, 128, NCH, 288])
        self.d_l2 = self.dram_in("l2", [NA, 128, 4, D])
        self.d_sshift = self.dram_in("sshift", [NA, SBN, D])
        self.d_swkv = self.dram_in("swkv", [NA, SBN, H, DH, DH])
        self.d_wkv_p = self.dram_out("wkv_p", [NA, H, DH, DH])
        self.d_wkv_s = self.dram_out("wkv_s", [NA, SBN, H, DH, DH])
        self.d_shift_p = self.dram_out("shift_p", [NA, D])
        self.d_shift_s = self.dram_out("shift_s", [NA, SBN, D])
        self.d_vf = self.nc.dram_tensor("vf", [64, H, NT], F32, kind="Internal").ap()
        NPG = cfg.npg
        self.d_wkv2 = self.dram_in("wkv2", [2, 128, NCH, D])
        self.d_wq = self.dram_in("wq", [2, 128, NCH, D])
        self.d_wo2 = self.dram_in("wo2", [2, 128, NCH, D])
        self.d_sbb = self.dram_in("sbb", [128, 2 * H])
        self.d_pt = self.dram_in("pt", [1, SBN * NPG], I32)
        self.d_ck = self.dram_in("ck", [cfg.nphys * 128, D])
        self.d_cv = self.dram_in("cv", [cfg.nphys * 128, D])
        self.d_kp = self.dram_out("kp", [SEQ, D])
        self.d_vp = self.dram_out("vp", [SEQ, D])
        self.d_ks = self.dram_out("ks", [128, D])
        self.d_vs = self.dram_out("vs", [128, D])
        self.d_KT = self.nc.dram_tensor("KTs", [NCH, 128, NT + 64], BF16, kind="Internal").ap()
        self.d_VB = self.nc.dram_tensor("VBs", [NT + 64, D], BF16, kind="Internal").ap()
        self.d_yp = self.dram_out("yp", [SEQ, D])
        self.d_ys = self.dram_out("ys", [128, D])
        self.xT = self.sb("xT", [128, NCH, NT + 64], F32)
        self.xb = self.sb("xb", [128, NCH, NT + 64], BF16)
        self.vecs = self.sb("vecs_sb", [128, NV * NCH], F32)
        self.cst = self.sb("cst_sb", [128, 5 * 128], F32)
        self.cstb = self.sb("cstb", [128, 5 * 128], BF16)
        self.vecs64 = self.sb("v64_sb", [64, NV64 * H], F32)
        self.cst64 = self.sb("c64_sb", [64, 576], F32)
        self.tiny = self.sb("tiny", [128, 1], F32)
        self.gneps = self.sb("gneps", [128, 1], F32)
        self.rv64, self.rc64, self.rtiny = Reg("v64"), Reg("c64"), Reg("tiny")
        self.onec = self.sb("onec", [128, 1], F32)
        self.sbb = self.sb("sbb_sb", [128, 2 * H], F32)
        self.pt_sb = self.sb("pt_sb", [128, cfg.sb * cfg.npg], I32)
        self.pidx = self.sb("pidx", [128, cfg.sb * cfg.npg], I32)
        self.iota_p = self.sb("iota_p", [128, 1], I32)
        self.rsbb, self.rpidx = Reg("sbb"), Reg("pidx")
        self.ARENA = getattr(self, 'ARENA_OVERRIDE', 24300)
        self.arena = self.sb("arena", [128, self.ARENA], F32)
        self.ps = [self.es.enter_context(nc.psum_tensor("ps%d" % i, [128, 512], F32)) for i in range(8)]
        self.rps = [Reg("ps%d" % i) for i in range(8)]
        self.rx = [Reg("x%d" % i) for i in range(len(self.tiles))]
        self.rb = [Reg("xb%d" % i) for i in range(len(self.tiles))]
        self.rvec = Reg("vecs")
        self.rcst = Reg("cst")
        self.out_regs = []

        P.op("sp", I("dma_start", out=self.vecs[:], in_=self.d_vecs[:, :]), writes=[self.rvec], dma=self.rvec)
        P.op("sp", I("dma_start", out=self.cst[:], in_=self.d_cst[:, :]), writes=[self.rcst], dma=self.rcst)
        P.op("sp", I("dma_start", out=self.vecs64[:], in_=self.d_v64[:, :]), writes=[self.rv64], dma=self.rv64)
        P.op("sp", I("dma_start", out=self.cst64[:], in_=self.d_c64[:, :]), writes=[self.rc64], dma=self.rc64)
        P.op("pool", I("memset", self.tiny[:], 1e-24), writes=[self.rtiny])
        P.op("pool", I("memset", self.gneps[:], GN_EPS), writes=[self.rtiny])
        P.op("pool", I("memset", self.onec[:], 1.0), writes=[self.rtiny])
        P.op("sp", I("dma_start", out=self.sbb[:], in_=self.d_sbb[:, :]), writes=[self.rsbb], dma=self.rsbb)
        rpt = Reg("ptsb")
        P.op("sp", I("dma_start", out=self.pt_sb[:], in_=self.d_pt[0:1, :].partition_broadcast(128)), writes=[rpt], dma=rpt)
        P.op("pool", I("iota", self.iota_p[:], pattern=[[0, 1]], base=0, channel_multiplier=1), writes=[self.rpidx])
        P.op("pool", I("tensor_scalar", out=self.pidx[:], in0=self.pt_sb[:], scalar1=128, scalar2=self.iota_p[:, 0:1], op0=ALU.mult, op1=ALU.add),
             reads=[rpt, self.rpidx], writes=[self.rpidx])
        rcb = Reg("cstb")
        P.op("dve", I("tensor_copy", out=self.cstb[:], in_=self.cst[:]), reads=[self.rcst], writes=[rcb])
        self.rcstb = rcb
        self.ident = self.cst[:, 0:128]
        self.ones = self.cst[:, 128:256]
        self.blk = self.cst[:, 256:384]

        if 'load' not in self.skip:
            self.load_x()
        for li in range(self.nlayers):
            if "ffn" not in self.skip:
                self.ffn(li, 0)
            if "ln" not in self.skip:
                self.layernorm(li, 0)
            if self.stop_after == (li, 0):
                break
            if self.mixers:
                if li < NA:
                    self.mixer_rwkv(li)
                else:
                    self.mixer_sb(li - NA)
                self.layernorm(li, 1)
            self.ffn(li, 1)
            self.layernorm(li, 2)
            if self.mixers and li == NA - 1:
                self.kv_proj()
        if 'store' not in self.skip:
            self.store_y()
        P.op("sp", None, reads=self.out_regs)
        P.barrier()
        P.emit(nc, self.es)
        self.es.close()
        return nc

    def vec(self, name, li, s, c):
        i = VIDX[(name, li, s)]
        return self.vecs[:, i * NCH + c:i * NCH + c + 1]

    def tile_of(self, t):
        return t // 512

    def aview(self, off, n, dt=F32):
        a = self.arena[:, off:off + n]
        return a if dt == F32 else a.bitcast(dt)

    def load_x(self):
        cfg, P = self.cfg, self.P
        stage = [self.aview(i * 1024, 1024) for i in range(2)]
        rst = [Reg("xstage%d" % i) for i in range(2)]
        srcs = [(self.d_xp, r0, 128, r0) for r0 in range(0, cfg.seq, 128)]
        srcs.append((self.d_xs, 0, 128, cfg.seq))
        import os
        srcs = srcs[:int(os.environ.get("NSRC", "99"))]
        for i, (src, r0, n, t0) in enumerate(srcs):
            s = i % 2
            P.op("sp", I("dma_start", out=stage[s][0:n, :], in_=src[r0:r0 + n, :]),
                 writes=[rst[s]], dma=rst[s])
            for half in range(2):
                pb = (i * 2 + half) % 2
                pst = self.ps[pb]
                for cc in range(4):
                    c = half * 4 + cc
                    P.op("pe", I("transpose",
                        out=pst[:, cc * 128:cc * 128 + 128], in_=stage[s][:, c * 128:(c + 1) * 128], identity=self.ident),
                        reads=[rst[s], self.rcst], writes=[self.rps[pb]])
                ti = self.tile_of(t0)
                pv = pst[:].rearrange("p (c t) -> p c t", c=4)[:, :, 0:n]
                P.op("dve", I("tensor_copy",
                    out=self.xT[:, half * 4:half * 4 + 4, t0:t0 + n], in_=pv), reads=[self.rps[pb]], writes=[self.rx[ti]])
                if not os.environ.get("NOACT"):
                    P.op("act", I("copy",
                        out=self.xb[:, half * 4:half * 4 + 4, t0:t0 + n], in_=pv), reads=[self.rps[pb]], writes=[self.rb[ti]])
        P.barrier()

    def store_y(self):
        cfg, P = self.cfg, self.P
        P.barrier()
        stage = [self.aview(i * 1024, 1024) for i in range(2)]
        rst = [Reg("ystage%d" % i) for i in range(2)]
        dsts = [(self.d_yp, r0, 128, r0) for r0 in range(0, cfg.seq, 128)]
        dsts.append((self.d_ys, 0, 128, cfg.seq))
        for i, (dst, r0, n, t0) in enumerate(dsts):
            s = i % 2
            ti = self.tile_of(t0)
            for half in range(2):
                pb = (i * 2 + half) % 2
                pst = self.ps[pb]
                for cc in range(4):
                    c = half * 4 + cc
                    P.op("pe", I("transpose",
                        out=pst[:, cc * 128:(cc + 1) * 128], in_=self.xT[:, c, t0:t0 + 128], identity=self.ident),
                        reads=[self.rx[ti], self.rcst], writes=[self.rps[pb]])
                eng = "dve" if half == 0 else "act"
                if eng == "dve":
                    P.op("dve", I("tensor_copy",
                        out=stage[s][0:n, half * 512:(half + 1) * 512], in_=pst[0:n, :]), reads=[self.rps[pb]], writes=[rst[s]])
                else:
                    P.op("act", I("copy",
                        out=stage[s][0:n, half * 512:(half + 1) * 512], in_=pst[0:n, :]), reads=[self.rps[pb]], writes=[rst[s]])
            P.op("sp", I("dma_start", out=dst[r0:r0 + n, :], in_=stage[s][0:n, :]),
                 reads=[rst[s]], dma=rst[s])
            if rst[s] not in self.out_regs:
                self.out_regs.append(rst[s])

    def ffn(self, li, j):
        cfg, P = self.cfg, self.P
        NT = cfg.nt
        P.barrier()
        widx = li * 2 + j
        G = 3
        groups = [list(range(g0, min(g0 + G, NFC))) for g0 in range(0, NFC, G)]
        o = 0
        stage = []
        for i in range(G):
            stage.append(self.aview(o, 3072)); o += 3072
        wb = []
        for i in range(2 * G):
            wb.append(self.aview(o, 1536, BF16)); o += 1536
        NH = 4
        hb = []
        for i in range(NH):
            hb.append(self.aview(o, 128, BF16)); o += 128
        sg = []
        for i in range(NH):
            sg.append(self.aview(o, 256)); o += 256
        assert o <= self.ARENA
        rstage = [Reg("wst%d" % i) for i in range(G)]
        rwb = [Reg("wb%d" % i) for i in range(2 * G)]
        rh = [Reg("h%d" % i) for i in range(NH)]
        rsg = [Reg("sg%d" % i) for i in range(NH)]
        ttiles = [(t0, min(256, NT - t0)) for t0 in range(0, NT, 256)]
        for ti, (t0, n) in enumerate(self.tiles):
            P.op("pool", I("tensor_scalar",
                out=self.xT[:, :, t0:t0 + n], in0=self.xT[:, :, t0:t0 + n], scalar1=ALPHA, scalar2=None, op0=ALU.mult),
                reads=[self.rx[ti]], writes=[self.rx[ti]])
        YB = [4, 5, 6, 7]
        GB = [0, 1, 2, 3]
        gctr = 0

        def w_dma(gi, k):
            fc = groups[gi][k]
            P.op("sp", I("dma_start", out=stage[k][:, :], in_=self.d_wffn[widx, fc, :, :]),
                 writes=[rstage[k]], dma=rstage[k])

        def w_cast(gi, k):
            slot = (gi % 2) * G + k
            P.op("pool", I("tensor_copy", out=wb[slot][:, 0:2048], in_=stage[k][:, 0:2048]),
                 reads=[rstage[k]], writes=[rwb[slot]])
            P.op("act", I("copy", out=wb[slot][:, 2048:3072], in_=stage[k][:, 2048:3072]),
                 reads=[rstage[k]], writes=[rwb[slot]])

        for k in range(len(groups[0])):
            w_dma(0, k)
        for k in range(len(groups[0])):
            w_cast(0, k)
        for gi, grp in enumerate(groups):
            nxt = groups[gi + 1] if gi + 1 < len(groups) else []
            for k in range(len(nxt)):
                w_dma(gi + 1, k)
            nsteps = len(ttiles) * len(grp)
            cast_at = {max(0, min(nsteps - len(nxt), nsteps // 2)) + k: k for k in range(len(nxt))}
            step = 0
            for tj, (t0, n) in enumerate(ttiles):
                ti = self.tile_of(t0)
                pend = None
                for k, fc in enumerate(grp):
                    if step in cast_at:
                        w_cast(gi + 1, cast_at[step])
                    step += 1
                    slot = (gi % 2) * G + k
                    gb = GB[gctr % 4]
                    hs = gctr % NH
                    gctr += 1
                    w = wb[slot]
                    for which in range(2):
                        for kc in range(NCH):
                            P.op("pe", I("matmul",
                                self.ps[gb][:, which * 256:which * 256 + n],
                                lhsT=w[:, which * 1024 + kc * 128:which * 1024 + (kc + 1) * 128],
                                rhs=self.xb[:, kc, t0:t0 + n], start=(kc == 0), stop=(kc == NCH - 1)),
                                reads=[rwb[slot], self.rb[ti]], writes=[self.rps[gb]])
                    P.op("act", I("activation", out=sg[hs][:, 0:n], in_=self.ps[gb][:, 0:n], func=AF.Silu),
                         reads=[self.rps[gb]], writes=[rsg[hs]])
                    P.op("dve", I("tensor_tensor",
                        out=hb[hs][:, 0:n], in0=sg[hs][:, 0:n], in1=self.ps[gb][:, 256:256 + n], op=ALU.mult),
                        reads=[rsg[hs], self.rps[gb]], writes=[rh[hs]])
                    if pend is not None:
                        self.ffn_down(*pend)
                    pend = (w, slot, rwb, hb[hs], rh[hs], n, YB, k == 0, k == len(grp) - 1)
                self.ffn_down(*pend)
                for dc in range(NCH):
                    yb = YB[dc // 2]
                    P.op("dve", I("scalar_tensor_tensor",
                        out=self.xT[:, dc, t0:t0 + n], in0=self.ps[yb][:, (dc % 2) * 256:(dc % 2) * 256 + n], scalar=0.5,
                        in1=self.xT[:, dc, t0:t0 + n], op0=ALU.mult, op1=ALU.add),
                        reads=[self.rps[yb], self.rx[ti]], writes=[self.rx[ti]])
            for st2 in range(step, nsteps + len(nxt)):
                if st2 in cast_at:
                    w_cast(gi + 1, cast_at[st2])

    def ffn_down(self, w, slot, rwb, h, rh, n, YB, first, last):
        P = self.P
        for dc in range(NCH):
            yb = YB[dc // 2]
            P.op("pe", I("matmul",
                self.ps[yb][:, (dc % 2) * 256:(dc % 2) * 256 + n],
                lhsT=w[:, 2048 + dc * 128:2048 + (dc + 1) * 128], rhs=h[:, 0:n],
                start=(first and dc % 2 == 0), stop=last, skip_group_check=True),
                reads=[rwb[slot], rh], writes=[self.rps[yb]])

    def mk_alloc(self):
        cfg = self.cfg
        xbw = self.xb[:].rearrange("p c t -> p (c t)").bitcast(F32)
        pools = [[self.arena, 0, self.ARENA], [xbw, 0, 4 * (cfg.nt + 64)]]

        def alloc(words, dt=F32):
            for p in pools:
                if p[1] + words <= p[2]:
                    a = p[0][:, p[1]:p[1] + words]
                    p[1] += words
                    return a if dt == F32 else a.bitcast(dt)
            raise RuntimeError("arena overflow")
        alloc.pools = pools
        return alloc

    def v64(self, name, li, h):
        i = V64[(name, li)]
        return self.vecs64[0:64, i * H + h:i * H + h + 1]

    def mixer_rwkv(self, li):
        cfg, P = self.cfg, self.P
        SEQ, NT = cfg.seq, cfg.nt
        P.barrier()
        alloc = self.mk_alloc()
        C0 = float(np.exp(-0.5))
        stage = alloc(4096)
        rstage = Reg("tmstage")
        wts = []
        rw = Reg("tmw")
        for wi in range(4):
            wb_ = alloc(4096, BF16).rearrange("p (k n) -> p k n", k=NCH)
            src = self.d_wrkv[li, wi] if wi < 3 else self.d_wo[li]
            for half in range(2):
                P.op("sp", I("dma_start",
                    out=stage.rearrange("p (k n) -> p k n", k=4), in_=src[:, half * 4:half * 4 + 4, :]),
                    writes=[rstage], dma=rstage)
                eng = "pool" if half == 0 else "act"
                if eng == "pool":
                    P.op("pool", I("tensor_copy",
                        out=wb_[:, half * 4:half * 4 + 4, :], in_=stage.rearrange("p (k n) -> p k n", k=4)), reads=[rstage], writes=[rw])
                else:
                    P.op("act", I("copy",
                        out=wb_[:, half * 4:half * 4 + 4, :], in_=stage.rearrange("p (k n) -> p k n", k=4)), reads=[rstage], writes=[rw])
            wts.append(wb_)
        wr, wk, wv, wo = wts
        l1 = alloc(1152, BF16).rearrange("p (k n) -> p k n", k=NCH)
        l2 = alloc(2048, BF16).rearrange("p (k n) -> p k n", k=4)
        P.op("sp", I("dma_start", out=stage[:, 0:2304].rearrange("p (k n) -> p k n", k=NCH), in_=self.d_l1[li]),
             writes=[rstage], dma=rstage)
        P.op("pool", I("tensor_copy", out=l1, in_=stage[:, 0:2304].rearrange("p (k n) -> p k n", k=NCH)), reads=[rstage], writes=[rw])
        P.op("sp", I("dma_start", out=stage.rearrange("p (k n) -> p k n", k=4), in_=self.d_l2[li]),
             writes=[rstage], dma=rstage)
        P.op("pool", I("tensor_copy", out=l2, in_=stage.rearrange("p (k n) -> p k n", k=4)), reads=[rstage], writes=[rw])
        LO = {"w": (0, 64), "a": (64, 64), "v": (128, 32), "g": (160, 128)}
        P.barrier()
        alloc.pools.append([stage, 0, 4096])

        TT = 64
        HG = 4
        mixes = [alloc(256, BF16).rearrange("p (c t) -> p c t", c=NCH) for _ in range(6)]
        rmix = [Reg("mix%d" % i) for i in range(6)]
        xx = alloc(512).rearrange("p (c t) -> p c t", c=NCH)
        rxx = Reg("xx")
        xlast = alloc(8)
        rxlast = Reg("xlast")
        lo1 = {k: alloc(32, BF16) for k in LO}
        rlo1 = {k: Reg("lo1" + k) for k in LO}
        ygs = alloc(256, BF16).rearrange("p (c t) -> p c t", c=NCH)
        ygo = alloc(256, BF16).rearrange("p (c t) -> p c t", c=NCH)
        rygs, rygo = Reg("ygs"), Reg("ygo")
        NB = 12
        fb = [alloc(256) for _ in range(NB)]
        rfb = [Reg("fb%d" % i) for i in range(NB)]
        f3 = lambda i: fb[i][0:64, :].rearrange("p (h t) -> p h t", h=HG)
        AR = alloc(256, BF16)
        BT = alloc(128, BF16)
        KT = alloc(128, BF16)
        BH = alloc(128, BF16)
        KH = alloc(128, BF16)
        rAR, rBT, rKT, rBH, rKH = Reg("AR"), Reg("BT"), Reg("KT"), Reg("BH"), Reg("KH")
        bkv = alloc(384, BF16)
        rbkv = Reg("bkv")
        Mb = alloc(256, BF16)
        Mk = alloc(256, BF16)
        rMb, rMk = Reg("Mb"), Reg("Mk")
        nbuf = [alloc(128, BF16) for _ in range(6)]
        rnb = [Reg("nb%d" % i) for i in range(6)]
        Wt = alloc(128, BF16)
        Ut = alloc(128, BF16)
        rWt, rUt = Reg("Wt"), Reg("Ut")
        S32 = alloc(1024)
        Sb = alloc(512, BF16)
        rS32, rSb = Reg("S32"), Reg("Sb")
        sst = [alloc(256) for _ in range(2)]
        rsst = [Reg("sst%d" % i) for i in range(2)]
        s32s = alloc(256)
        sbs = alloc(128, BF16)
        rs32s, rsbs = Reg("s32s"), Reg("sbs")
        sost = [alloc(256) for _ in range(2)]
        rsost = [Reg("sost%d" % i) for i in range(2)]
        shT = alloc(128)
        rshT = Reg("shT")
        ones64 = self.cst[0:64, 128:192]
        ident = self.ident
        identb = self.cstb[:, 0:128]

        P.op("pool", I("memset", S32[0:64, :], 0.0), writes=[rS32])
        P.op("pool", I("memset", Sb[0:64, :], 0.0), writes=[rSb])
        P.op("pool", I("memset", xlast, 0.0), writes=[rxlast])
        for c in range(NCH):
            P.op("sp", I("dma_start", out=shT[:, c * 16:(c + 1) * 16], in_=self.d_sshift[li][:, c * 128:(c + 1) * 128].rearrange("b p -> p b"),
                         allow_slow_non_contiguous=True), writes=[rshT], dma=rshT)

        tiles = [(t0, 64, 1, False) for t0 in range(0, SEQ, TT)] + [(SEQ, 4, 16, True)]
        for (t0, C, nch, samp) in tiles:
            ti = self.tile_of(t0)
            rxt = self.rx[ti]
            msk = self.cst64[0:C, (256 if samp else 0):(256 if samp else 0) + 256]
            m_si = msk[:, 0:2 * C]
            m_lo = msk[:, 128:128 + C]
            m_ey = msk[:, 192:192 + C]
            xt = self.xT[:, :, t0:t0 + TT]
            if not samp:
                P.op("dve", I("tensor_tensor", out=xx[:, :, 1:TT], in0=self.xT[:, :, t0:t0 + TT - 1],
                                                             in1=self.xT[:, :, t0 + 1:t0 + TT], op=ALU.subtract),
                     reads=[rxt], writes=[rxx])
                P.op("dve", I("tensor_tensor", out=xx[:, :, 0], in0=xlast, in1=self.xT[:, :, t0], op=ALU.subtract),
                     reads=[rxt, rxlast], writes=[rxx])
                P.op("pool", I("tensor_copy", out=xlast, in_=self.xT[:, :, t0 + TT - 1]), reads=[rxt, rxx], writes=[rxlast])
                if t0 + TT == SEQ:
                    P.op("sp", I("dma_start", out=self.d_shift_p[li].rearrange("(c p) -> p c", p=128), in_=xlast,
                                                     allow_slow_non_contiguous=True), reads=[rxlast], dma=rxlast)
                    self.out_regs.append(rxlast)
            else:
                for c in range(NCH):
                    xv = self.xT[:, c, t0:t0 + TT].rearrange("p (b t) -> p b t", t=4)
                    xxv = xx[:, c, :].rearrange("p (b t) -> p b t", t=4)
                    P.op("dve", I("tensor_tensor", out=xxv[:, :, 1:4], in0=xv[:, :, 0:3], in1=xv[:, :, 1:4], op=ALU.subtract),
                         reads=[rxt], writes=[rxx])
                    P.op("dve", I("tensor_tensor", out=xxv[:, :, 0], in0=shT[:, c * 16:(c + 1) * 16], in1=xv[:, :, 0], op=ALU.subtract),
                         reads=[rxt, rshT], writes=[rxx])
                rsh_out = Reg("shout")
                for c in range(NCH):
                    P.op("sp", I("dma_start", out=self.d_shift_s[li][:, c * 128:(c + 1) * 128].rearrange("b p -> p b"),
                                 in_=self.xT[:, c, t0:t0 + TT].rearrange("p (b t) -> p b t", t=4)[:, :, 3],
                                 allow_slow_non_contiguous=True), reads=[rxt], dma=rxt)
                if rxt not in self.out_regs:
                    self.out_regs.append(rxt)
            for m in range(6):
                for c in range(NCH):
                    P.op("dve", I("scalar_tensor_tensor",
                        out=mixes[m][:, c, :], in0=xx[:, c, :], scalar=self.vec("mu", li, m, c), in1=self.xT[:, c, t0:t0 + TT],
                        op0=ALU.mult, op1=ALU.add), reads=[rxx, rxt, self.rvec], writes=[rmix[m]])
            for key, mi in (("w", 1), ("a", 4), ("v", 3), ("g", 5)):
                if key == "v" and li == 0:
                    continue
                o_, wdt = LO[key]
                pb = 0
                for kc in range(NCH):
                    P.op("pe", I("matmul",
                        self.ps[pb][0:wdt, 0:TT], lhsT=l1[:, kc, o_:o_ + wdt], rhs=mixes[mi][:, kc, :], start=(kc == 0), stop=(kc == NCH - 1)),
                        reads=[rw, rmix[mi]], writes=[self.rps[pb]])
                fn = {"w": AF.Tanh, "a": AF.Identity, "v": AF.Identity, "g": AF.Sigmoid}[key]
                P.op("act", I("activation", out=lo1[key][0:wdt, 0:TT], in_=self.ps[pb][0:wdt, 0:TT], func=fn),
                     reads=[self.rps[pb]], writes=[rlo1[key]])
            def group(hg):
                h0 = hg * HG
                NCOL = HG * TT
                psA, psB, psC, psD = 0, 1, 2, 3

                def proj(wmat, mi, pb):
                    for hh in range(HG):
                        h = h0 + hh
                        for kc in range(NCH):
                            P.op("pe", I("matmul",
                                self.ps[pb][0:64, hh * TT:(hh + 1) * TT], lhsT=wmat[:, kc, h * 64:(h + 1) * 64], rhs=mixes[mi][:, kc, :],
                                start=(kc == 0 and hh == 0), stop=(kc == NCH - 1), skip_group_check=True),
                                reads=[rw, rmix[mi]], writes=[self.rps[pb]])

                def lora2(key, pb):
                    o_, wdt = LO[key]
                    idx = {"w": 0, "a": 1, "v": 2, "g": 3}[key]
                    for hh in range(HG):
                        h = h0 + hh
                        P.op("pe", I("matmul",
                            self.ps[pb][0:64, hh * TT:(hh + 1) * TT], lhsT=l2[0:wdt, idx, h * 64:(h + 1) * 64], rhs=lo1[key][0:wdt, 0:TT],
                            start=(hh == 0), stop=True, skip_group_check=True), reads=[rw, rlo1[key]], writes=[self.rps[pb]])

                def F(i):
                    return fb[i][0:64, 0:NCOL]
                R32, K32, V32, SG, A32, G32, KK, KP, CS, PP, PI, PX = range(12)
                proj(wr, 0, psA)
                P.op("act", I("copy", out=F(R32), in_=self.ps[psA][0:64, 0:NCOL]), reads=[self.rps[psA]], writes=[rfb[R32]])
                proj(wk, 2, psB)
                P.op("act", I("copy", out=F(K32), in_=self.ps[psB][0:64, 0:NCOL]), reads=[self.rps[psB]], writes=[rfb[K32]])
                proj(wv, 3, psC)
                P.op("act", I("copy", out=F(V32), in_=self.ps[psC][0:64, 0:NCOL]), reads=[self.rps[psC]], writes=[rfb[V32]])
                lora2("w", psD)
                for hh in range(HG):
                    P.op("act", I("activation", out=F(SG)[:, hh * TT:(hh + 1) * TT], in_=self.ps[psD][0:64, hh * TT:(hh + 1) * TT],
                                                            func=AF.Sigmoid, bias=self.v64("w0", li, h0 + hh)),
                         reads=[self.rps[psD], self.rv64], writes=[rfb[SG]])
                lora2("a", psA)
                for hh in range(HG):
                    P.op("act", I("activation", out=F(A32)[:, hh * TT:(hh + 1) * TT], in_=self.ps[psA][0:64, hh * TT:(hh + 1) * TT],
                                                            func=AF.Sigmoid, bias=self.v64("a0", li, h0 + hh)),
                         reads=[self.rps[psA], self.rv64], writes=[rfb[A32]])
                lora2("g", psB)
                P.op("act", I("copy", out=F(G32), in_=self.ps[psB][0:64, 0:NCOL]), reads=[self.rps[psB]], writes=[rfb[G32]])
                vfv = self.d_vf[:, h0:h0 + HG, t0:t0 + TT]
                if li == 0:
                    P.op("sp", I("dma_start", out=vfv, in_=F(V32).rearrange("p (h t) -> p h t", h=HG)), reads=[rfb[V32]], dma=rfb[V32])
                else:
                    lora2("v", psC)
                    P.op("sp", I("dma_start", out=F(KK).rearrange("p (h t) -> p h t", h=HG), in_=vfv), writes=[rfb[KK]], dma=rfb[KK])
                    for hh in range(HG):
                        P.op("act", I("activation", out=F(KP)[:, hh * TT:(hh + 1) * TT], in_=self.ps[psC][0:64, hh * TT:(hh + 1) * TT],
                                                                func=AF.Sigmoid, bias=self.v64("v0", li, h0 + hh)),
                             reads=[self.rps[psC], self.rv64], writes=[rfb[KP]])
                    P.op("dve", I("tensor_tensor", out=F(KK), in0=F(KK), in1=F(V32), op=ALU.subtract), reads=[rfb[KK], rfb[V32]], writes=[rfb[KK]])
                    P.op("dve", I("tensor_tensor", out=F(KK), in0=F(KK), in1=F(KP), op=ALU.mult), reads=[rfb[KK], rfb[KP]], writes=[rfb[KK]])
                    P.op("dve", I("tensor_tensor", out=F(V32), in0=F(V32), in1=F(KK), op=ALU.add), reads=[rfb[KK], rfb[V32]], writes=[rfb[V32]])
                for hh in range(HG):
                    P.op("dve", I("tensor_scalar", out=F(KK)[:, hh * TT:(hh + 1) * TT], in0=F(K32)[:, hh * TT:(hh + 1) * TT],
                                                               scalar1=self.v64("k_k", li, h0 + hh), scalar2=None, op0=ALU.mult),
                         reads=[rfb[K32], self.rv64], writes=[rfb[KK]])
                P.op("act", I("activation", out=F(PP), in_=F(KK), func=AF.Square), reads=[rfb[KK]], writes=[rfb[PP]])
                P.op("pe", I("matmul", self.ps[psD][0:64, 0:NCOL], lhsT=ones64, rhs=F(PP), start=True, stop=True),
                     reads=[self.rcst, rfb[PP]], writes=[self.rps[psD]])
                P.op("act", I("activation", out=F(PP), in_=self.ps[psD][0:64, 0:NCOL], func=AF.Ln, bias=self.tiny[0:64, 0:1]),
                     reads=[self.rps[psD], self.rtiny], writes=[rfb[PP]])
                P.op("act", I("activation", out=F(PP), in_=F(PP), func=AF.Exp, scale=-0.5), reads=[rfb[PP]], writes=[rfb[PP]])
                P.op("dve", I("tensor_tensor", out=F(KK), in0=F(KK), in1=F(PP), op=ALU.mult), reads=[rfb[KK], rfb[PP]], writes=[rfb[KK]])
                for hh in range(HG):
                    P.op("dve", I("tensor_scalar", out=F(KP)[:, hh * TT:(hh + 1) * TT], in0=F(A32)[:, hh * TT:(hh + 1) * TT],
                                                               scalar1=-1.0, scalar2=self.v64("k_a", li, h0 + hh), op0=ALU.add, op1=ALU.mult),
                         reads=[rfb[A32], self.rv64], writes=[rfb[KP]])
                P.op("dve", I("scalar_tensor_tensor", out=F(KP), in0=F(KP), scalar=1.0, in1=F(K32), op0=ALU.add, op1=ALU.mult),
                     reads=[rfb[KP], rfb[K32]], writes=[rfb[KP]])
                for hh in range(HG):
                    for j in range(nch):
                        sl = slice(hh * TT + j * C, hh * TT + (j + 1) * C)
                        P.op("dve", I("tensor_tensor_scan", out=F(CS)[:, sl], data0=self.cst[0:64, 128:128 + C], data1=F(SG)[:, sl],
                                                                        initial=0.0, op0=ALU.mult, op1=ALU.add),
                             reads=[rfb[SG], self.rcst], writes=[rfb[CS]])
                P.op("act", I("activation", out=F(PP), in_=F(CS), func=AF.Exp, scale=-C0), reads=[rfb[CS]], writes=[rfb[PP]])
                P.op("act", I("activation", out=F(PI), in_=F(CS), func=AF.Exp, scale=C0), reads=[rfb[CS]], writes=[rfb[PI]])
                P.op("dve", I("tensor_tensor", out=F(PX), in0=F(CS), in1=F(SG), op=ALU.subtract), reads=[rfb[CS], rfb[SG]], writes=[rfb[PX]])
                P.op("act", I("activation", out=F(PX), in_=F(PX), func=AF.Exp, scale=-C0), reads=[rfb[PX]], writes=[rfb[PX]])
                ARv = AR[0:64, 0:2 * NCOL].rearrange("p (h j a c) -> p h j a c", h=HG, j=nch, a=2)
                v4 = lambda ap: ap.rearrange("p (h j c) -> p h j c", h=HG, j=nch)
                P.op("dve", I("scalar_tensor_tensor", out=ARv[:, :, :, 0, :], in0=v4(F(KK)), scalar=-1.0, in1=v4(F(PX)), op0=ALU.mult, op1=ALU.mult),
                     reads=[rfb[KK], rfb[PX]], writes=[rAR])
                P.op("dve", I("tensor_tensor", out=ARv[:, :, :, 1, :], in0=v4(F(R32)), in1=v4(F(PP)), op=ALU.mult),
                     reads=[rfb[R32], rfb[PP]], writes=[rAR])
                ppv = v4(F(PP))
                a_ = ppv.ap
                PCb = bass.AP(ppv.tensor, ppv.offset + (C - 1), [list(a_[0]), list(a_[1]), list(a_[2]), [0, C]])
                P.op("dve", I("tensor_tensor", out=F(SG), in0=F(KK), in1=F(A32), op=ALU.mult), reads=[rfb[KK], rfb[A32]], writes=[rfb[SG]])
                P.op("dve", I("tensor_tensor", out=F(SG), in0=F(SG), in1=F(PI), op=ALU.mult), reads=[rfb[SG], rfb[PI]], writes=[rfb[SG]])
                P.op("pool", I("tensor_copy", out=BT[0:64, 0:NCOL], in_=F(SG)), reads=[rfb[SG]], writes=[rBT])
                P.op("dve", I("tensor_tensor", out=v4(BH[0:64, 0:NCOL]), in0=v4(F(SG)), in1=PCb, op=ALU.mult), reads=[rfb[SG], rfb[PP]], writes=[rBH])
                P.op("dve", I("tensor_tensor", out=F(CS), in0=F(KP), in1=F(PI), op=ALU.mult), reads=[rfb[KP], rfb[PI]], writes=[rfb[CS]])
                P.op("pool", I("tensor_copy", out=KT[0:64, 0:NCOL], in_=F(CS)), reads=[rfb[CS]], writes=[rKT])
                P.op("dve", I("tensor_tensor", out=v4(KH[0:64, 0:NCOL]), in0=v4(F(CS)), in1=PCb, op=ALU.mult), reads=[rfb[CS], rfb[PP]], writes=[rKH])
                for hh in range(HG):
                    P.op("dve", I("scalar_tensor_tensor", out=F(K32)[:, hh * TT:(hh + 1) * TT], in0=F(R32)[:, hh * TT:(hh + 1) * TT],
                                                                      scalar=self.v64("r_k", li, h0 + hh), in1=F(KP)[:, hh * TT:(hh + 1) * TT],
                                                                      op0=ALU.mult, op1=ALU.mult),
                         reads=[rfb[R32], rfb[KP], self.rv64], writes=[rfb[K32]])
                P.op("pe", I("matmul", self.ps[psD][0:64, 0:NCOL], lhsT=ones64, rhs=F(K32), start=True, stop=True),
                     reads=[self.rcst, rfb[K32]], writes=[self.rps[psD]])
                P.op("dve", I("tensor_tensor", out=F(K32), in0=self.ps[psD][0:64, 0:NCOL], in1=F(V32), op=ALU.mult),
                     reads=[self.rps[psD], rfb[V32]], writes=[rfb[K32]])
                VB = fb[KP][0:64, 0:NCOL // 2].bitcast(BF16)
                P.op("pool", I("tensor_copy", out=VB, in_=F(V32)), reads=[rfb[V32], rKT, rKH], writes=[rfb[KP]])
                NM = HG * nch
                Mbv = Mb[0:C, 0:NM * 2 * C].rearrange("p (m c) -> p m c", m=NM)
                Mkv = Mk[0:C, 0:NM * 2 * C].rearrange("p (m c) -> p m c", m=NM)
                nv = [nbuf[i][0:C, 0:NM * C].rearrange("p (m c) -> p m c", m=NM) for i in range(6)]
                BTv = BT[0:64, 0:NCOL].rearrange("p (m c) -> p m c", m=NM)
                KTv = KT[0:64, 0:NCOL].rearrange("p (m c) -> p m c", m=NM)
                BHv = BH[0:64, 0:NCOL].rearrange("p (m c) -> p m c", m=NM)
                KHv = KH[0:64, 0:NCOL].rearrange("p (m c) -> p m c", m=NM)
                VBv = VB.rearrange("p (m c) -> p m c", m=NM)
                ARm = AR[0:64, 0:2 * NCOL].rearrange("p (m c) -> p m c", m=NM)
                pMb = self.ps[psA][0:C, 0:NM * 2 * C].rearrange("p (m c) -> p m c", m=NM)
                pMk = self.ps[psB][0:C, 0:NM * 2 * C].rearrange("p (m c) -> p m c", m=NM)
                pNT = self.ps[psC][0:C, 0:NM * C].rearrange("p (m c) -> p m c", m=NM)
                for m in range(NM):
                    P.op("pe", I("matmul", pMb[:, m, :], lhsT=BTv[:, m, :], rhs=ARm[:, m, :], start=(m == 0), stop=True, skip_group_check=True),
                         reads=[rBT, rAR], writes=[self.rps[psA]])
                for m in range(NM):
                    P.op("pe", I("matmul", pMk[:, m, :], lhsT=KTv[:, m, :], rhs=ARm[:, m, :], start=(m == 0), stop=True, skip_group_check=True),
                         reads=[rKT, rAR], writes=[self.rps[psB]])
                for m in range(NM):
                    P.op("pe", I("matmul", pNT[:, m, :], lhsT=ARm[:, m, 0:C], rhs=BTv[:, m, :], start=(m == 0), stop=True, skip_group_check=True),
                         reads=[rBT, rAR], writes=[self.rps[psC]])
                bc = lambda mk, w: bass.AP(mk.tensor, mk.offset, [list(mk.ap[0]), [0, NM], [1, w]])
                P.op("dve", I("tensor_tensor", out=Mbv, in0=pMb, in1=bc(m_si, 2 * C), op=ALU.mult), reads=[self.rps[psA], self.rc64], writes=[rMb])
                P.op("dve", I("tensor_tensor", out=Mkv, in0=pMk, in1=bc(m_si, 2 * C), op=ALU.mult), reads=[self.rps[psB], self.rc64], writes=[rMk])
                N_, NT_, N2_, N2T_, T_, Tt_ = range(6)
                P.op("dve", I("tensor_tensor", out=nv[NT_], in0=pNT, in1=bc(m_lo, C), op=ALU.mult), reads=[self.rps[psC], self.rc64], writes=[rnb[NT_]])
                P.op("pool", I("tensor_copy", out=nv[N_], in_=Mbv[:, :, 0:C]), reads=[rMb], writes=[rnb[N_]])
                P.op("pool", I("tensor_tensor", out=nv[T_], in0=Mbv[:, :, 0:C], in1=bc(m_ey, C), op=ALU.add), reads=[rMb, self.rc64], writes=[rnb[T_]])
                P.op("pool", I("tensor_tensor", out=nv[Tt_], in0=nv[NT_], in1=bc(m_ey, C), op=ALU.add), reads=[rnb[NT_], self.rc64], writes=[rnb[Tt_]])
                nsteps = {64: 5, 4: 1}[C]
                cur, curT, nxt, nxtT = N_, NT_, N2_, N2T_
                pX = [self.ps[i][0:C, 0:NM * C].rearrange("p (m c) -> p m c", m=NM) for i in (psA, psB, psC, psD)]
                for s_ in range(nsteps):
                    last = (s_ == nsteps - 1)
                    for m in range(NM):
                        P.op("pe", I("matmul", pX[0][:, m, :], lhsT=nv[curT][:, m, :], rhs=nv[cur][:, m, :], start=(m == 0), stop=True, skip_group_check=True),
                             reads=[rnb[cur], rnb[curT]], writes=[self.rps[psA]])
                    P.op("act", I("copy", out=nv[nxt], in_=pX[0]), reads=[self.rps[psA]], writes=[rnb[nxt]])
                    if not last:
                        for m in range(NM):
                            P.op("pe", I("matmul", pX[1][:, m, :], lhsT=nv[cur][:, m, :], rhs=nv[curT][:, m, :], start=(m == 0), stop=True, skip_group_check=True),
                                 reads=[rnb[cur], rnb[curT]], writes=[self.rps[psB]])
                        P.op("act", I("copy", out=nv[nxtT], in_=pX[1]), reads=[self.rps[psB]], writes=[rnb[nxtT]])
                    for m in range(NM):
                        P.op("pe", I("matmul", pX[2][:, m, :], lhsT=nv[Tt_][:, m, :], rhs=nv[nxt][:, m, :], start=(m == 0), stop=True, skip_group_check=True),
                             reads=[rnb[Tt_], rnb[nxt]], writes=[self.rps[psC]])
                    if not last:
                        for m in range(NM):
                            P.op("pe", I("matmul", pX[3][:, m, :], lhsT=nv[nxt][:, m, :], rhs=nv[Tt_][:, m, :], start=(m == 0), stop=True, skip_group_check=True),
                                 reads=[rnb[Tt_], rnb[nxt]], writes=[self.rps[psD]])
                    P.op("dve", I("tensor_tensor", out=nv[T_], in0=pX[2], in1=nv[T_], op=ALU.add), reads=[self.rps[psC], rnb[T_]], writes=[rnb[T_]])
                    if not last:
                        P.op("dve", I("tensor_tensor", out=nv[Tt_], in0=pX[3], in1=nv[Tt_], op=ALU.add), reads=[self.rps[psD], rnb[Tt_]], writes=[rnb[Tt_]])
                    cur, curT, nxt, nxtT = nxt, nxtT, cur, curT
                pY = self.ps[4][0:64, 0:NCOL].rearrange("p (h j c) -> p h j c", h=HG, j=nch)
                for j in range(nch):
                    mm_ = lambda hh: hh * nch + j
                    pT = self.ps[5][0:C, 0:HG * 3 * 32].bitcast(BF16).rearrange("p (h a k) -> p h a k", h=HG, a=3)
                    bkvv = bkv[0:C, 0:HG * 3 * 64].rearrange("p (h a k) -> p h a k", h=HG, a=3)
                    for hh in range(HG):
                        for a, srcv, rsrc in ((0, BHv, rBH), (1, KHv, rKH), (2, VBv, rfb[KP])):
                            P.op("pe", I("transpose", out=pT[:, hh, a, :], in_=srcv[:, mm_(hh), :], identity=identb[0:64, 0:64]),
                                 reads=[rsrc, self.rcstb], writes=[self.rps[5]])
                    P.op("act", I("copy", out=bkvv, in_=pT), reads=[self.rps[5]], writes=[rbkv])
                    if samp:
                        b = j
                        s = j % 2
                        snat = sst[s][0:64, :]
                        P.op("sp", I("dma_start", out=snat.rearrange("p (h k) -> p h k", h=HG),
                                                                        in_=self.d_swkv[li, b, h0:h0 + HG].rearrange("h v k -> v h k")),
                             writes=[rsst[s]], dma=rsst[s])
                        snb = s32s[0:64, 0:128].bitcast(BF16)
                        P.op("pool", I("tensor_copy", out=snb, in_=snat), reads=[rsst[s]], writes=[rs32s])
                        pS = self.ps[6][0:64, 0:128].bitcast(BF16).rearrange("p (h v) -> p h v", h=HG)
                        for hh in range(HG):
                            P.op("pe", I("transpose", out=pS[:, hh, :], in_=snb[:, hh * 64:(hh + 1) * 64], identity=identb[0:64, 0:64]),
                                 reads=[rs32s, self.rcstb], writes=[self.rps[6]])
                        sbcur = sbs[0:64, 0:256].rearrange("p (h v) -> p h v", h=HG)
                        P.op("act", I("copy", out=sbcur, in_=pS), reads=[self.rps[6]], writes=[rsbs])
                        rsb_cur = rsbs
                    else:
                        sbcur = Sb[0:64, h0 * 64:(h0 + HG) * 64].rearrange("p (h v) -> p h v", h=HG)
                        rsb_cur = rSb
                    pW = self.ps[6][0:C, 256:256 + HG * 64].rearrange("p (h v) -> p h v", h=HG)
                    for hh in range(HG):
                        P.op("pe", I("matmul", pW[:, hh, :], lhsT=ARm[:, mm_(hh), 0:C], rhs=sbcur[:, hh, :],
                                                                        start=(hh == 0), stop=False, skip_group_check=True),
                             reads=[rAR, rsb_cur], writes=[self.rps[6]])
                        P.op("pe", I("matmul", pW[:, hh, :], lhsT=Mkv[:, mm_(hh), 0:C], rhs=bkvv[:, hh, 2, :],
                                                           start=False, stop=True, skip_group_check=True),
                             reads=[rMk, rbkv], writes=[self.rps[6]])
                    Wtv = Wt[0:C, 0:HG * 64].rearrange("p (h v) -> p h v", h=HG)
                    Utv = Ut[0:C, 0:HG * 64].rearrange("p (h v) -> p h v", h=HG)
                    P.op("act", I("copy", out=Wtv, in_=pW), reads=[self.rps[6]], writes=[rWt])
                    pU = self.ps[7][0:C, 0:HG * 64].rearrange("p (h v) -> p h v", h=HG)
                    for hh in range(HG):
                        P.op("pe", I("matmul", pU[:, hh, :], lhsT=nv[T_][:, mm_(hh), :], rhs=Wtv[:, hh, :], start=(hh == 0), stop=True, skip_group_check=True),
                             reads=[rnb[T_], rWt], writes=[self.rps[7]])
                    P.op("act", I("copy", out=Utv, in_=pU), reads=[self.rps[7]], writes=[rUt])
                    for hh in range(HG):
                        P.op("pe", I("matmul", pY[:, hh, j, :], lhsT=sbcur[:, hh, :], rhs=ARm[:, mm_(hh), C:2 * C],
                                                                        start=(hh == 0 and j == 0), stop=False, skip_group_check=True),
                             reads=[rAR, rsb_cur], writes=[self.rps[4]])
                        P.op("pe", I("matmul", pY[:, hh, j, :], lhsT=Utv[:, hh, :], rhs=Mbv[:, mm_(hh), C:2 * C],
                                                           start=False, stop=False, skip_group_check=True), reads=[rUt, rMb], writes=[self.rps[4]])
                        P.op("pe", I("matmul", pY[:, hh, j, :], lhsT=bkvv[:, hh, 2, :], rhs=Mkv[:, mm_(hh), C:2 * C],
                                                           start=False, stop=True, skip_group_check=True), reads=[rbkv, rMk], writes=[self.rps[4]])
                    pD = self.ps[7][0:64, 256:256 + HG * 64].rearrange("p (h v) -> p h v", h=HG)
                    if not samp:
                        for hh in range(HG):
                            P.op("pe", I("matmul", pD[:, hh, :], lhsT=bkvv[:, hh, 0, :], rhs=Utv[:, hh, :], start=(hh == 0), stop=False, skip_group_check=True),
                                 reads=[rbkv, rUt], writes=[self.rps[7]])
                            P.op("pe", I("matmul", pD[:, hh, :], lhsT=bkvv[:, hh, 1, :], rhs=bkvv[:, hh, 2, :], start=False, stop=True, skip_group_check=True),
                                 reads=[rbkv], writes=[self.rps[7]])
                        for hh in range(HG):
                            h = h0 + hh
                            pc = F(PP)[:, hh * TT + (j + 1) * C - 1:hh * TT + (j + 1) * C]
                            P.op("dve", I("scalar_tensor_tensor",
                                out=S32[0:64, h * 64:(h + 1) * 64], in0=S32[0:64, h * 64:(h + 1) * 64], scalar=pc, in1=pD[:, hh, :],
                                op0=ALU.mult, op1=ALU.add), reads=[rS32, rfb[PP], self.rps[7]], writes=[rS32])
                        P.op("pool", I("tensor_copy", out=Sb[0:64, h0 * 64:(h0 + HG) * 64], in_=S32[0:64, h0 * 64:(h0 + HG) * 64]),
                             reads=[rS32], writes=[rSb])
                    else:
                        for hh in range(HG):
                            P.op("pe", I("matmul", pD[:, hh, :], lhsT=Utv[:, hh, :], rhs=bkvv[:, hh, 0, :], start=(hh == 0), stop=False, skip_group_check=True),
                                 reads=[rbkv, rUt], writes=[self.rps[7]])
                            P.op("pe", I("matmul", pD[:, hh, :], lhsT=bkvv[:, hh, 2, :], rhs=bkvv[:, hh, 1, :], start=False, stop=True, skip_group_check=True),
                                 reads=[rbkv], writes=[self.rps[7]])
                        dg = fb[CS][0:64, 0:HG * 64]
                        for hh in range(HG):
                            pc = F(PP)[:, hh * TT + (j + 1) * C - 1:hh * TT + (j + 1) * C]
                            P.op("dve", I("tensor_scalar", out=dg[:, hh * 64:(hh + 1) * 64], in0=ident[0:64, 0:64], scalar1=pc, scalar2=None, op0=ALU.mult),
                                 reads=[rfb[PP], self.rcst, rKT, rKH], writes=[rfb[CS]])
                        pR = self.ps[6][0:64, 0:256]
                        P.op("pe", I("matmul", self.ps[5][0:64, 256:512], lhsT=ones64, rhs=dg, start=True, stop=True, skip_group_check=True),
                             reads=[rfb[CS], self.rcst], writes=[self.rps[5]])
                        so = sost[s][0:64, :]
                        P.op("dve", I("tensor_tensor", out=so, in0=snat, in1=self.ps[5][0:64, 256:512], op=ALU.mult),
                             reads=[rsst[s], self.rps[5]], writes=[rsost[s]])
                        P.op("dve", I("tensor_tensor", out=so, in0=so, in1=self.ps[7][0:64, 256:512], op=ALU.add),
                             reads=[self.rps[7], rsost[s]], writes=[rsost[s]])
                        P.op("sp", I("dma_start", out=self.d_wkv_s[li, b, h0:h0 + HG].rearrange("h v k -> v h k"),
                                                                    in_=so.rearrange("p (h k) -> p h k", h=HG)), reads=[rsost[s]], dma=rsost[s])
                        if rsost[s] not in self.out_regs:
                            self.out_regs.append(rsost[s])
                Y32, YSQ, MU, RS = R32, SG, PI, PX
                P.op("act", I("copy", out=F(Y32), in_=self.ps[4][0:64, 0:NCOL]), reads=[self.rps[4], rAR], writes=[rfb[Y32]])
                P.op("act", I("activation", out=F(YSQ), in_=F(Y32), func=AF.Square), reads=[rfb[Y32], rBT, rBH], writes=[rfb[YSQ]])
                P.op("pe", I("matmul", self.ps[psA][0:64, 0:NCOL], lhsT=ones64, rhs=F(Y32), start=True, stop=True), reads=[self.rcst, rfb[Y32]], writes=[self.rps[psA]])
                P.op("pe", I("matmul", self.ps[psB][0:64, 0:NCOL], lhsT=ones64, rhs=F(YSQ), start=True, stop=True), reads=[self.rcst, rfb[YSQ]], writes=[self.rps[psB]])
                P.op("dve", I("tensor_scalar", out=F(MU), in0=self.ps[psA][0:64, 0:NCOL], scalar1=1.0 / DH, scalar2=None, op0=ALU.mult),
                     reads=[self.rps[psA]], writes=[rfb[MU]])
                P.op("dve", I("tensor_tensor", out=F(RS), in0=F(MU), in1=F(MU), op=ALU.mult), reads=[rfb[MU]], writes=[rfb[RS]])
                P.op("dve", I("scalar_tensor_tensor", out=F(RS), in0=self.ps[psB][0:64, 0:NCOL], scalar=1.0 / DH, in1=F(RS), op0=ALU.mult, op1=ALU.subtract),
                     reads=[self.rps[psB], rfb[RS]], writes=[rfb[RS]])
                P.op("act", I("activation", out=F(RS), in_=F(RS), func=AF.Ln, bias=self.gneps[0:64, 0:1]), reads=[rfb[RS], self.rtiny], writes=[rfb[RS]])
                P.op("act", I("activation", out=F(RS), in_=F(RS), func=AF.Exp, scale=-0.5), reads=[rfb[RS]], writes=[rfb[RS]])
                P.op("dve", I("tensor_tensor", out=F(Y32), in0=F(Y32), in1=F(MU), op=ALU.subtract), reads=[rfb[Y32], rfb[MU]], writes=[rfb[Y32]])
                P.op("dve", I("tensor_tensor", out=F(Y32), in0=F(Y32), in1=F(RS), op=ALU.mult), reads=[rfb[Y32], rfb[RS]], writes=[rfb[Y32]])
                for hh in range(HG):
                    sl = slice(hh * TT, (hh + 1) * TT)
                    P.op("act", I("activation", out=F(Y32)[:, sl], in_=F(Y32)[:, sl], func=AF.Identity,
                                                                   bias=self.v64("gn_b", li, h0 + hh), scale=self.v64("gn_g", li, h0 + hh)),
                         reads=[rfb[Y32], self.rv64], writes=[rfb[Y32]])
                P.op("dve", I("tensor_tensor", out=F(Y32), in0=F(Y32), in1=F(K32), op=ALU.add), reads=[rfb[Y32], rfb[K32]], writes=[rfb[Y32]])
                for hh in range(HG):
                    h = h0 + hh
                    dst, rd = (ygs, rygs) if h % 2 == 0 else (ygo, rygo)
                    sl = slice(hh * TT, (hh + 1) * TT)
                    P.op("dve", I("tensor_tensor", out=dst[0:64, h // 2, :], in0=F(Y32)[:, sl], in1=F(G32)[:, sl], op=ALU.mult),
                         reads=[rfb[Y32], rfb[G32]], writes=[rd])

            for hg in range(H // HG):
                group(hg)
            P.op("sp", I("dma_start", out=ygs[64:128, :, :], in_=ygo[0:64, :, :]), reads=[rygo], writes=[rygs], dma=rygs)
            for oc in range(NCH):
                pb = 1 + (oc % 2)
                for kc in range(NCH):
                    P.op("pe", I("matmul", self.ps[pb][:, 0:TT], lhsT=wo[:, kc, oc * 128:(oc + 1) * 128], rhs=ygs[:, kc, :],
                                                                    start=(kc == 0), stop=(kc == NCH - 1)),
                         reads=[rw, rygs], writes=[self.rps[pb]])
                P.op("dve", I("scalar_tensor_tensor",
                    out=self.xT[:, oc, t0:t0 + TT], in0=self.xT[:, oc, t0:t0 + TT], scalar=ALPHA, in1=self.ps[pb][:, 0:TT],
                    op0=ALU.mult, op1=ALU.add), reads=[self.rps[pb], rxt, rxlast, rxx] + rmix, writes=[rxt])
        for hq in range(4):
            pb = 3 + (hq % 2)
            for hh in range(4):
                h = hq * 4 + hh
                P.op("pe", I("transpose", out=self.ps[pb][0:64, hh * 64:(hh + 1) * 64], in_=S32[0:64, h * 64:(h + 1) * 64],
                                                                  identity=ident[0:64, 0:64]), reads=[rS32, self.rcst], writes=[self.rps[pb]])
            so = sost[hq % 2]
            P.op("act", I("copy", out=so[0:64, :], in_=self.ps[pb][0:64, 0:256]), reads=[self.rps[pb]], writes=[rsost[hq % 2]])
            P.op("sp", I("dma_start", out=self.d_wkv_p[li, hq * 4:hq * 4 + 4].rearrange("h v k -> v h k"),
                                                          in_=so[0:64, :].rearrange("p (h k) -> p h k", h=4)), reads=[rsost[hq % 2]], dma=rsost[hq % 2])
            if rsost[hq % 2] not in self.out_regs:
                self.out_regs.append(rsost[hq % 2])

    def load_w(self, alloc, src, stage, rstage, rw):
        P = self.P
        wb_ = alloc(4096, BF16).rearrange("p (k n) -> p k n", k=NCH)
        for half in range(2):
            P.op("sp", I("dma_start", out=stage.rearrange("p (k n) -> p k n", k=4), in_=src[:, half * 4:half * 4 + 4, :]),
                 writes=[rstage], dma=rstage)
            if half == 0:
                P.op("pool", I("tensor_copy", out=wb_[:, 0:4, :], in_=stage.rearrange("p (k n) -> p k n", k=4)), reads=[rstage], writes=[rw])
            else:
                P.op("act", I("copy", out=wb_[:, 4:8, :], in_=stage.rearrange("p (k n) -> p k n", k=4)), reads=[rstage], writes=[rw])
        return wb_

    def kv_proj(self):
        cfg, P = self.cfg, self.P
        SEQ, NT = cfg.seq, cfg.nt
        P.barrier()
        alloc = self.mk_alloc_arena()
        stage = alloc(4096)
        rstage, rw = Reg("kvstage"), Reg("kvw")
        wk = self.load_w(alloc, self.d_wkv2[0], stage, rstage, rw)
        wv = self.load_w(alloc, self.d_wkv2[1], stage, rstage, rw)
        st32 = [alloc(1024) for _ in range(2)]
        rst32 = [Reg("kvst%d" % i) for i in range(2)]
        vb16 = [alloc(512, BF16) for _ in range(2)]
        rvb16 = [Reg("vb16%d" % i) for i in range(2)]
        ktb = [alloc(256, BF16) for _ in range(2)]
        rktb = [Reg("ktb%d" % i) for i in range(2)]
        ttiles = [(t0, 128) for t0 in range(0, SEQ, 128)] + [(SEQ, 128)]
        ctr = 0
        for (t0, n) in ttiles:
            ti = self.tile_of(t0)
            samp = t0 >= SEQ
            for which, w_, dp, ds in ((0, wk, self.d_kp, self.d_ks), (1, wv, self.d_vp, self.d_vs)):
                s = ctr % 2
                ctr += 1
                for half in range(2):
                    pb = (ctr * 2 + half) % 4
                    for kc in range(NCH):
                        P.op("pe", I("matmul", self.ps[pb][:, :], lhsT=self.xb[:, kc, t0:t0 + 128], rhs=w_[:, kc, half * 512:(half + 1) * 512],
                                     start=(kc == 0), stop=(kc == NCH - 1)), reads=[rw, self.rb[ti]], writes=[self.rps[pb]])
                    P.op("act", I("copy", out=st32[s][:, half * 512:(half + 1) * 512], in_=self.ps[pb][:, :]), reads=[self.rps[pb]], writes=[rst32[s]])
                dst = ds[:, :] if samp else dp[t0:t0 + 128, :]
                P.op("sp", I("dma_start", out=dst, in_=st32[s]), reads=[rst32[s]], dma=rst32[s])
                if rst32[s] not in self.out_regs:
                    self.out_regs.append(rst32[s])
                if which == 1:
                    P.op("dve", I("tensor_copy", out=vb16[s], in_=st32[s]), reads=[rst32[s]], writes=[rvb16[s]])
                    P.op("sp", I("dma_start", out=self.d_VB[t0:t0 + 128, :], in_=vb16[s]), reads=[rvb16[s]], dma=rvb16[s])
        ctr = 0
        for oc in range(NCH):
            for ti, (t0, n) in enumerate(self.tiles):
                s = ctr % 2
                pb = 4 + ctr % 2
                ctr += 1
                for kc in range(NCH):
                    P.op("pe", I("matmul", self.ps[pb][:, 0:n], lhsT=wk[:, kc, oc * 128:(oc + 1) * 128], rhs=self.xb[:, kc, t0:t0 + n],
                                 start=(kc == 0), stop=(kc == NCH - 1)), reads=[rw, self.rb[ti]], writes=[self.rps[pb]])
                P.op("act", I("copy", out=ktb[s][:, 0:n], in_=self.ps[pb][:, 0:n]), reads=[self.rps[pb]], writes=[rktb[s]])
                P.op("sp", I("dma_start", out=self.d_KT[oc, :, t0:t0 + n], in_=ktb[s][:, 0:n]), reads=[rktb[s]], dma=rktb[s])
        P.barrier()

    def mk_alloc_arena(self):
        pools = [[self.arena, 0, self.ARENA]]

        def alloc(words, dt=F32):
            for p in pools:
                if p[1] + words <= p[2]:
                    a = p[0][:, p[1]:p[1] + words]
                    p[1] += words
                    return a if dt == F32 else a.bitcast(dt)
            raise RuntimeError("arena overflow (%d words)" % words)
        alloc.pools = pools
        return alloc

    def mixer_sb(self, j):
        cfg, P = self.cfg, self.P
        SEQ, NT, NPG, SBN = cfg.seq, cfg.nt, cfg.npg, cfg.sb
        NQB = SEQ // 128
        P.barrier()
        alloc = self.mk_alloc_arena()
        stage = alloc(4096)
        rstage, rw = Reg("sbstage"), Reg("sbw")
        wq = self.load_w(alloc, self.d_wq[j], stage, rstage, rw)
        qT = alloc(4 * NT, BF16).rearrange("p (c t) -> p c t", c=NCH)
        rqT = Reg("qT")
        oT = self.xb
        roT = Reg("oT")
        bias = self.sbb[:, j * H:(j + 1) * H]
        ident, identb = self.ident, self.cstb[:, 0:128]
        ones = self.ones
        tri = self.cst[:, 384:512]
        dmask = self.cst[:, 512:640]
        ctr = 0
        for oc in range(NCH):
            for ti, (t0, n) in enumerate(self.tiles):
                pb = ctr % 2
                ctr += 1
                for kc in range(NCH):
                    P.op("pe", I("matmul", self.ps[pb][:, 0:n], lhsT=wq[:, kc, oc * 128:(oc + 1) * 128], rhs=self.xb[:, kc, t0:t0 + n],
                                 start=(kc == 0), stop=(kc == NCH - 1)), reads=[rw, self.rb[ti]], writes=[self.rps[pb]])
                P.op("act", I("activation", out=qT[:, oc, t0:t0 + n], in_=self.ps[pb][:, 0:n], func=AF.Identity, scale=SB_SCALE),
                     reads=[self.rps[pb]], writes=[rqT])
        P.barrier()
        alloc.pools.append([stage, 0, 4096])
        pmark = [q[1] for q in alloc.pools]
        GB = 4
        Lt = [alloc(512) for _ in range(2)]
        lbt = [alloc(512) for _ in range(2)]
        Lbt = [alloc(256, BF16) for _ in range(2)]
        Wtt = [alloc(256, BF16) for _ in range(2)]
        rLt, rlbt, rLbt, rWtt = ([Reg(n + "0"), Reg(n + "1")] for n in ("Lt", "lbt", "Lbt", "Wtt"))
        csum = alloc(128)
        rcsum = Reg("csum")
        trib, onesb = self.cstb[:, 384:512], self.cstb[:, 128:256]
        KTp = alloc(SEQ // 2, BF16)
        VZ = [alloc(NQB * 64, BF16).rearrange("p (k d) -> p k d", k=NQB) for e in range(2)]
        rKTp, rVZ = Reg("KTp"), Reg("VZ")
        for e in range(2):
            P.op("pool", I("memset", VZ[e], 0.0), writes=[rVZ])
        blk = 0
        import os
        for c in range(NCH if not os.environ.get("SKIP_SBP") else 0):
            P.op("sp", I("dma_start", out=KTp, in_=self.d_KT[c, :, 0:SEQ]), writes=[rKTp], dma=rKTp)
            for e in range(2):
                P.op("sp", I("dma_start", out=VZ[e][:, :, e * 64:(e + 1) * 64],
                             in_=self.d_VB[0:SEQ, c * 128 + e * 64:c * 128 + (e + 1) * 64].rearrange("(k p) d -> p k d", p=128)),
                     writes=[rVZ], dma=rVZ)
            for qb in range(NQB):
                po = 6 + (qb % 2)
                glist = []
                for e in range(2):
                    h = 2 * c + e
                    bh = bias[:, h:h + 1]
                    qv = qT[e * 64:(e + 1) * 64, c, qb * 128:(qb + 1) * 128]
                    kbs = list(range(qb, -1, -1))
                    groups = [kbs[i:i + GB] for i in range(0, len(kbs), GB)]
                    for gi, grp in enumerate(groups):
                        G = len(grp)
                        glist.append((e, gi, grp, len(groups)))
                def ctx(item, k):
                    e, gi, grp, ng = item
                    G = len(grp)
                    w = k % 2
                    pz, pc, pt_ = (k % 2), 2 + (k % 2), 4 + (k % 2)
                    h = 2 * c + e
                    bh = bias[:, h:h + 1]
                    qv = qT[e * 64:(e + 1) * 64, c, qb * 128:(qb + 1) * 128]
                    L, lb, Lb, W = Lt[w][:, 0:G * 128], lbt[w][:, 0:G * 128], Lbt[w][:, 0:G * 128], Wtt[w][:, 0:G * 128]
                    rL, rlb, rLb, rW = rLt[w], rlbt[w], rLbt[w], rWtt[w]
                    pzv, pcv = self.ps[pz][:, 0:G * 128], self.ps[pc][:, 0:G * 128]
                    diag = (gi == 0)
                    lastg = (gi == ng - 1)
                    return locals()

                def stage1(item, k):
                    globals_ = ctx(item, k)
                    e, gi, grp, G, w, pz, pc, pt_, h, bh, qv, L, lb, Lb, W, rL, rlb, rLb, rW, pzv, pcv, diag, lastg = (
                        globals_[n] for n in ("e", "gi", "grp", "G", "w", "pz", "pc", "pt_", "h", "bh", "qv", "L", "lb", "Lb", "W", "rL", "rlb", "rLb", "rW", "pzv", "pcv", "diag", "lastg"))
                    for i, kb in enumerate(grp):
                        P.op("pe", I("matmul", self.ps[pz][:, i * 128:(i + 1) * 128], lhsT=KTp[e * 64:(e + 1) * 64, kb * 128:(kb + 1) * 128], rhs=qv,
                                     start=(i == 0), stop=True, skip_group_check=True), reads=[rKTp, rqT], writes=[self.rps[pz]])
                    P.op("act", I("activation", out=L, in_=pzv, func=AF.Exp, bias=bh), reads=[self.rps[pz], self.rsbb], writes=[rL])
                    P.op("act", I("activation", out=L, in_=L, func=AF.Ln, bias=self.onec[:, 0:1]), reads=[rL, self.rtiny], writes=[rL])
                    P.op("dve", I("scalar_tensor_tensor", out=lb, in0=pzv, scalar=bh, in1=L, op0=ALU.add, op1=ALU.subtract),
                         reads=[self.rps[pz], rL, self.rsbb], writes=[rlb])
                    if diag:
                        P.op("dve", I("tensor_tensor", out=L[:, 0:128], in0=L[:, 0:128], in1=dmask, op=ALU.mult), reads=[rL, self.rcst], writes=[rL])
                    P.op("pool", I("tensor_copy", out=Lb, in_=L), reads=[rL], writes=[rLb])

                def stage2(item, k, first, lastitem):
                    globals_ = ctx(item, k)
                    e, gi, grp, G, w, pz, pc, pt_, h, bh, qv, L, lb, Lb, W, rL, rlb, rLb, rW, pzv, pcv, diag, lastg = (
                        globals_[n] for n in ("e", "gi", "grp", "G", "w", "pz", "pc", "pt_", "h", "bh", "qv", "L", "lb", "Lb", "W", "rL", "rlb", "rLb", "rW", "pzv", "pcv", "diag", "lastg"))
                    firstmm = True
                    for i in range(G):
                        P.op("pe", I("matmul", self.ps[pc][:, i * 128:(i + 1) * 128], lhsT=trib, rhs=Lb[:, i * 128:(i + 1) * 128],
                                     start=firstmm, stop=(i == 0), skip_group_check=True), reads=[self.rcstb, rLb], writes=[self.rps[pc]])
                        firstmm = False
                        for i2 in range(i):
                            P.op("pe", I("matmul", self.ps[pc][:, i * 128:(i + 1) * 128], lhsT=onesb, rhs=Lb[:, i2 * 128:(i2 + 1) * 128],
                                         start=False, stop=(i2 == i - 1), skip_group_check=True), reads=[self.rcstb, rLb], writes=[self.rps[pc]])
                    if not lastg:
                        for i in range(G):
                            P.op("pe", I("matmul", self.ps[pt_][:, 0:128], lhsT=onesb, rhs=Lb[:, i * 128:(i + 1) * 128],
                                         start=(i == 0), stop=(i == G - 1)), reads=[self.rcstb, rLb], writes=[self.rps[pt_]])
                    P.op("dve", I("tensor_tensor", out=lb, in0=lb, in1=pcv, op=ALU.subtract), reads=[rlb, self.rps[pc]], writes=[rlb])
                    if not diag:
                        cb = bass.AP(csum.tensor, csum.offset, [list(csum.ap[0]), [0, G], [1, 128]])
                        P.op("dve", I("tensor_tensor", out=lb.rearrange("p (g q) -> p g q", g=G), in0=lb.rearrange("p (g q) -> p g q", g=G), in1=cb,
                                      op=ALU.subtract), reads=[rlb, rcsum], writes=[rlb])
                    if not lastg:
                        if diag:
                            P.op("dve", I("tensor_copy", out=csum, in_=self.ps[pt_][:, 0:128]), reads=[self.rps[pt_]], writes=[rcsum])
                        else:
                            P.op("dve", I("tensor_tensor", out=csum, in0=csum, in1=self.ps[pt_][:, 0:128], op=ALU.add), reads=[self.rps[pt_], rcsum], writes=[rcsum])
                    P.op("act", I("activation", out=W, in_=lb, func=AF.Exp), reads=[rlb], writes=[rW])
                    if diag:
                        P.op("pool", I("tensor_tensor", out=W[:, 0:128], in0=W[:, 0:128], in1=dmask, op=ALU.mult), reads=[rW, self.rcst], writes=[rW])
                    for i, kb in enumerate(grp):
                        last = (lastitem and i == G - 1)
                        P.op("pe", I("matmul", self.ps[po][:, 0:128], lhsT=VZ[e][:, kb, :], rhs=W[:, i * 128:(i + 1) * 128], start=(first and i == 0), stop=last),
                             reads=[rVZ, rW], writes=[self.rps[po]])

                for k_, item in enumerate(glist):
                    kk_ = blk + k_
                    if k_ == 0:
                        stage1(item, kk_)
                    if k_ + 1 < len(glist):
                        stage1(glist[k_ + 1], kk_ + 1)
                    stage2(item, kk_, k_ == 0, k_ == len(glist) - 1)
                blk += len(glist)
                P.op("act", I("copy", out=oT[:, c, qb * 128:(qb + 1) * 128], in_=self.ps[po][:, 0:128]), reads=[self.rps[po]], writes=[roT])
        P.barrier()
        for q, m in zip(alloc.pools, pmark):
            q[1] = m
        wk_ = {nm: [alloc(64) for _ in range(2)] for nm in ("E", "L", "lb", "Ls", "t")}
        rwk = {nm: [Reg(nm + "0"), Reg(nm + "1")] for nm in wk_}
        if not os.environ.get("SKIP_SBS"):
            self.sb_sample(j, alloc, qT, rqT, oT, roT, wk_, rwk)
        P.barrier()
        for q, m in zip(alloc.pools, pmark):
            q[1] = m
        wo = self.load_w(alloc, self.d_wo2[j], stage, rstage, rw)
        ctr = 0
        for oc in range(NCH):
            for ti, (t0, n) in enumerate(self.tiles):
                pb = ctr % 2
                ctr += 1
                for kc in range(NCH):
                    P.op("pe", I("matmul", self.ps[pb][:, 0:n], lhsT=wo[:, kc, oc * 128:(oc + 1) * 128], rhs=oT[:, kc, t0:t0 + n],
                                 start=(kc == 0), stop=(kc == NCH - 1)), reads=[rw, roT], writes=[self.rps[pb]])
                P.op("dve", I("scalar_tensor_tensor", out=self.xT[:, oc, t0:t0 + n], in0=self.xT[:, oc, t0:t0 + n], scalar=ALPHA, in1=self.ps[pb][:, 0:n],
                              op0=ALU.mult, op1=ALU.add), reads=[self.rps[pb], self.rx[ti]], writes=[self.rx[ti]])

    def sb_sample(self, j, alloc, qT, rqT, oT, roT, wk_, rwk):
        cfg, P = self.cfg, self.P
        SEQ, NT, NPG, SBN = cfg.seq, cfg.nt, cfg.npg, cfg.sb
        bias = self.sbb[:, j * H:(j + 1) * H]
        identb = self.cstb[:, 0:128]
        ones, tri = self.ones, self.cst[:, 384:512]
        Kst = [alloc(1024)] * 2
        Vst = [alloc(1024)] * 2
        rKst, rVst = [Reg("Kst0")] * 2, [Reg("Vst0")] * 2
        Kb = alloc(512, BF16)
        Vb = alloc(512, BF16)
        KTg = alloc(512, BF16).rearrange("p (c k) -> p c k", c=NCH)
        rKb, rVb, rKTg = Reg("Kb"), Reg("Vb"), Reg("KTg")
        KTn = alloc(256, BF16).rearrange("p (c t) -> p c t", c=NCH)
        rKTn = Reg("KTn")
        brow = alloc(64)
        rbrow = Reg("brow")
        Vn = alloc(512, BF16)
        rVn = Reg("Vn")
        P.op("sp", I("dma_start", out=Vn[0:64, :], in_=self.d_VB[SEQ:SEQ + 64, :]), writes=[rVn], dma=rVn)
        P.op("sp", I("dma_start", out=KTn, in_=self.d_KT[:, :, SEQ:SEQ + 64].rearrange("c p t -> p c t")), writes=[rKTn], dma=rKTn)
        bsrc = bass.AP(bias.tensor, bias.offset, [list(bias.ap[0]), [1, H], [0, 4]])
        P.op("dve", I("tensor_copy", out=brow.rearrange("p (h q) -> p h q", h=H), in_=bsrc), reads=[self.rsbb], writes=[rbrow])
        m3 = self.cst64[0:64, 512:576]
        blk = 0
        lsn = 128
        for b in range(SBN):
            po = 6 + (b % 2)
            q0 = SEQ + 4 * b
            Lsum = wk_["Ls"][0][:, 0:64]
            rLsum = rwk["Ls"][0]
            blocks = [("new", None)] + [("page", pg) for pg in range(NPG - 1, -1, -1)]
            import os
            if os.environ.get("SB_NOPAGE"):
                blocks = blocks[:1]
            if os.environ.get("SB_NONEW"):
                blocks = blocks[1:]
            for bi, (kind, pg) in enumerate(blocks):
                w = blk % 2
                blk += 1
                pz, pc = (blk % 2), 2 + (blk % 2)
                E, L, lb, tt_ = (wk_[nm][w][:, 0:64] for nm in ("E", "L", "lb", "t"))
                rE, rL, rlb, rtt = (rwk[nm][w] for nm in ("E", "L", "lb", "t"))
                Wt = wk_["Ls"][1][:, 0:32].bitcast(BF16)
                rWt = rwk["Ls"][1]
                if kind == "new":
                    nk = 64
                    s = 0
                    ktv = lambda e, c: KTn[e * 64:(e + 1) * 64, c, 0:64]
                    rkt = rKTn
                    mb3 = bass.AP(m3.tensor, m3.offset + 4 * b, [list(m3.ap[0]), [0, H], [1, 4]])
                    v3 = lambda ap: ap.rearrange("p (h q) -> p h q", h=H)
                    Vcur, rVcur = Vn, rVn
                else:
                    nk = 128
                    s = blk % 2
                    col = b * NPG + pg
                    P.op("pool", I("indirect_dma_start", out=Kst[s][:, :], out_offset=None, in_=self.d_ck[:, :],
                                   in_offset=bass.IndirectOffsetOnAxis(ap=self.pidx[:, col:col + 1], axis=0)),
                         reads=[self.rpidx], writes=[rKst[s]], dma=rKst[s])
                    P.op("pool", I("indirect_dma_start", out=Vst[s][:, :], out_offset=None, in_=self.d_cv[:, :],
                                   in_offset=bass.IndirectOffsetOnAxis(ap=self.pidx[:, col:col + 1], axis=0)),
                         reads=[self.rpidx], writes=[rVst[s]], dma=rVst[s])
                    P.op("dve", I("tensor_copy", out=Kb, in_=Kst[s]), reads=[rKst[s]], writes=[rKb])
                    P.op("act", I("copy", out=Vb, in_=Vst[s]), reads=[rVst[s]], writes=[rVb])
                    pT = self.ps[4 + (blk % 2)][:, 0:512].bitcast(BF16).rearrange("p (c k) -> p c k", c=NCH)
                    for c in range(NCH):
                        P.op("pe", I("transpose", out=pT[:, c, :], in_=Kb[:, c * 128:(c + 1) * 128], identity=identb),
                             reads=[rKb, self.rcstb], writes=[self.rps[4 + (blk % 2)]])
                    P.op("act", I("copy", out=KTg, in_=pT), reads=[self.rps[4 + (blk % 2)]], writes=[rKTg])
                    ktv = lambda e, c: KTg[e * 64:(e + 1) * 64, c, :]
                    rkt = rKTg
                    Vcur, rVcur = Vb, rVb
                for h in range(H):
                    c, e = h // 2, h % 2
                    P.op("pe", I("matmul", self.ps[pz][0:nk, h * 4:(h + 1) * 4], lhsT=ktv(e, c), rhs=qT[e * 64:(e + 1) * 64, c, q0:q0 + 4],
                                 start=(h == 0), stop=True, skip_group_check=True), reads=[rkt, rqT], writes=[self.rps[pz]])
                zb = lb
                P.op("dve", I("tensor_tensor", out=zb[0:nk, :], in0=self.ps[pz][0:nk, 0:64], in1=brow[0:nk, :], op=ALU.add),
                     reads=[self.rps[pz], rbrow], writes=[rlb])
                P.op("act", I("activation", out=E[0:nk, :], in_=zb[0:nk, :], func=AF.Exp), reads=[rlb], writes=[rE])
                P.op("act", I("activation", out=L[0:nk, :], in_=E[0:nk, :], func=AF.Ln, bias=self.onec[0:nk, 0:1]), reads=[rE, self.rtiny], writes=[rL])
                P.op("dve", I("tensor_tensor", out=lb[0:nk, :], in0=zb[0:nk, :], in1=L[0:nk, :], op=ALU.subtract), reads=[rlb, rL], writes=[rlb])
                if kind == "new":
                    P.op("dve", I("tensor_tensor", out=v3(L[0:nk, :]), in0=v3(L[0:nk, :]), in1=mb3, op=ALU.mult), reads=[rL, self.rc64], writes=[rL])
                P.op("pe", I("matmul", self.ps[pc][0:nk, 0:64], lhsT=tri[0:nk, 0:nk], rhs=L[0:nk, :], start=True, stop=(kind == "new")),
                     reads=[self.rcst, rL], writes=[self.rps[pc]])
                if kind != "new":
                    P.op("pe", I("matmul", self.ps[pc][0:nk, 0:64], lhsT=ones[0:lsn, 0:nk], rhs=Lsum[0:lsn, :], start=False, stop=True),
                         reads=[self.rcst, rLsum], writes=[self.rps[pc]])
                P.op("dve", I("tensor_tensor", out=tt_[0:nk, :], in0=lb[0:nk, :], in1=self.ps[pc][0:nk, 0:64], op=ALU.subtract),
                     reads=[rlb, self.rps[pc]], writes=[rtt])
                if kind == "new":
                    P.op("pool", I("memset", Lsum, 0.0), writes=[rLsum])
                    P.op("pool", I("tensor_copy", out=Lsum[0:64, :], in_=L[0:64, :]), reads=[rL], writes=[rLsum])
                    P.op("act", I("activation", out=tt_[0:nk, :], in_=tt_[0:nk, :], func=AF.Exp), reads=[rtt], writes=[rtt])
                    P.op("dve", I("tensor_tensor", out=v3(Wt[0:nk, :]), in0=v3(tt_[0:nk, :]), in1=mb3, op=ALU.mult), reads=[rtt, self.rc64], writes=[rWt])
                else:
                    if bi < len(blocks) - 1:
                        P.op("pool", I("tensor_tensor", out=Lsum, in0=Lsum, in1=L, op=ALU.add), reads=[rL, rLsum], writes=[rLsum])
                    P.op("act", I("activation", out=Wt[0:nk, :], in_=tt_[0:nk, :], func=AF.Exp), reads=[rtt], writes=[rWt])
                for c in range(NCH):
                    P.op("pe", I("matmul", self.ps[po][:, c * 8:(c + 1) * 8], lhsT=Vcur[0:nk, c * 128:(c + 1) * 128], rhs=Wt[0:nk, c * 8:(c + 1) * 8],
                                 start=(bi == 0 and c == 0), stop=(bi == len(blocks) - 1), skip_group_check=True),
                         reads=[rVcur, rWt], writes=[self.rps[po]])
            pov = self.ps[po][:, 0:64].rearrange("p (c x) -> p c x", c=NCH)
            P.op("act", I("copy", out=oT[0:64, :, q0:q0 + 4], in_=pov[0:64, :, 0:4]), reads=[self.rps[po]], writes=[roT])
            P.op("act", I("copy", out=oT[64:128, :, q0:q0 + 4], in_=pov[64:128, :, 4:8]), reads=[self.rps[po]], writes=[roT])

    def layernorm(self, li, slot):
        cfg, P = self.cfg, self.P
        P.barrier()
        o = 0
        sq = self.aview(o, NCH * 512).rearrange("p (c t) -> p c t", c=NCH); o += NCH * 512
        mean = self.aview(o, 512); o += 512
        rstd = self.aview(o, 512); o += 512
        tmp = self.aview(o, 512); o += 512
        rsq, rmean, rrstd, rtmp = Reg("sq"), Reg("mean"), Reg("rstd"), Reg("tmp")
        epsb = self.aview(o, 1); o += 1
        reps = Reg("eps")
        P.op("pool", I("memset", epsb, LN_EPS), writes=[reps])
        for ti, (t0, n) in enumerate(self.tiles):
            b1, b2 = (2 * ti) % 8, (2 * ti + 1) % 8
            P.op("act", I("activation", out=sq[:, :, 0:n], in_=self.xT[:, :, t0:t0 + n], func=AF.Square),
                 reads=[self.rx[ti]], writes=[rsq])
            for c in range(NCH):
                P.op("pe", I("matmul", self.ps[b1][:, 0:n], lhsT=self.ones, rhs=self.xT[:, c, t0:t0 + n],
                                                                    start=(c == 0), stop=(c == NCH - 1)),
                     reads=[self.rcst, self.rx[ti]], writes=[self.rps[b1]])
            for c in range(NCH):
                P.op("pe", I("matmul", self.ps[b2][:, 0:n], lhsT=self.ones, rhs=sq[:, c, 0:n],
                                                              start=(c == 0), stop=(c == NCH - 1)),
                     reads=[self.rcst, rsq], writes=[self.rps[b2]])
            P.op("dve", I("tensor_scalar", out=mean[:, 0:n], in0=self.ps[b1][:, 0:n], scalar1=1.0 / D, scalar2=None, op0=ALU.mult),
                 reads=[self.rps[b1]], writes=[rmean])
            P.op("dve", I("tensor_tensor", out=tmp[:, 0:n], in0=mean[:, 0:n], in1=mean[:, 0:n], op=ALU.mult),
                 reads=[rmean], writes=[rtmp])
            P.op("dve", I("scalar_tensor_tensor", out=tmp[:, 0:n], in0=self.ps[b2][:, 0:n], scalar=1.0 / D, in1=tmp[:, 0:n],
                                                                   op0=ALU.mult, op1=ALU.subtract),
                 reads=[self.rps[b2], rtmp], writes=[rtmp])
            P.op("act", I("activation", out=tmp[:, 0:n], in_=tmp[:, 0:n], func=AF.Ln, bias=epsb[:, 0:1]),
                 reads=[rtmp, reps], writes=[rtmp])
            P.op("act", I("activation", out=rstd[:, 0:n], in_=tmp[:, 0:n], func=AF.Exp, scale=-0.5),
                 reads=[rtmp], writes=[rrstd])
            for c in range(NCH):
                xs = self.xT[:, c, t0:t0 + n]
                P.op("dve", I("tensor_tensor", out=xs, in0=xs, in1=mean[:, 0:n], op=ALU.subtract),
                     reads=[self.rx[ti], rmean], writes=[self.rx[ti]])
                P.op("dve", I("tensor_tensor", out=xs, in0=xs, in1=rstd[:, 0:n], op=ALU.mult),
                     reads=[self.rx[ti], rrstd], writes=[self.rx[ti]])
                P.op("act", I("activation", out=xs, in_=xs, func=AF.Identity,
                                                              bias=self.vec("ln_b", li, slot, c), scale=self.vec("ln_g", li, slot, c)),
                     reads=[self.rx[ti], self.rvec], writes=[self.rx[ti]])
                P.op("pool", I("tensor_copy", out=self.xb[:, c, t0:t0 + n], in_=xs),
                     reads=[self.rx[ti]], writes=[self.rb[ti]])


V64 = {}
for _li in range(NA):
    for _n in ("w0", "a0", "k_k", "k_a", "r_k", "gn_g", "gn_b", "v0"):
        V64[(_n, _li)] = len(V64)
NV64 = len(V64)


def pack_vecs64(inp):
    out = np.zeros((NV64, D), np.float32)
    for (n, li), i in V64.items():
        if n == "v0":
            v = inp["tm_v0"][0] if li == 1 else np.zeros(D, np.float32)
        elif n == "r_k":
            v = inp["tm_r_k"][li].reshape(D)
        else:
            v = inp["tm_" + n][li]
        out[i] = v
    return np.ascontiguousarray(out.reshape(NV64, H, DH).transpose(2, 0, 1).reshape(DH, NV64 * H))


def pack_consts64():
    i = np.arange(64)
    outs = []
    for C in (64, 4):
        su = (i[:, None] < i[None, :]).astype(np.float32)
        iu = (i[:, None] <= i[None, :]).astype(np.float32)
        lo = (i[:, None] > i[None, :]).astype(np.float32)
        ey = (i[:, None] == i[None, :]).astype(np.float32)
        m = np.zeros((64, 4 * 64), np.float32)
        m[:C, 0:C] = su[:C, :C]
        m[:C, C:2 * C] = iu[:C, :C]
        m[:C, 128:128 + C] = lo[:C, :C]
        m[:C, 192:192 + C] = ey[:C, :C]
        outs.append(m)
    m3 = np.zeros((64, 16, 4), np.float32)
    for s_ in range(64):
        for q in range(4):
            if s_ % 4 < q:
                m3[s_, s_ // 4, q] = 1.0
    outs.append(m3.reshape(64, 64))
    return np.ascontiguousarray(np.concatenate(outs, axis=1))


def pack_tm(inp):
    wrkv = np.stack([pack_proj(inp["tm_w_r"]), pack_proj(inp["tm_w_k"]), pack_proj(inp["tm_w_v"])], 1)
    v1 = np.concatenate([np.zeros_like(inp["tm_v1"]), inp["tm_v1"]], 0)
    v2 = np.concatenate([np.zeros_like(inp["tm_v2"]), inp["tm_v2"]], 0)
    l1 = np.concatenate([inp["tm_w1"], inp["tm_a1"], v1, inp["tm_g1"]], axis=2)
    l1 = pack_proj(l1)
    l2 = np.zeros((NA, 128, 4, D), np.float32)
    l2[:, :64, 0] = inp["tm_w2"]
    l2[:, :64, 1] = inp["tm_a2"]
    l2[:, :32, 2] = v2
    l2[:, :128, 3] = inp["tm_g2"]
    wo = np.ascontiguousarray(inp["tm_w_o"].reshape(NA, H, DH, D).transpose(0, 2, 1, 3))
    return np.ascontiguousarray(wrkv), l1, l2, wo


def make_in_maps(cfg, inp):
    vecs = pack_vecs(inp)
    cst = pack_consts()
    wffn = pack_ffn(inp)
    v64 = pack_vecs64(inp)
    c64 = pack_consts64()
    wrkv, l1, l2, _ = pack_tm(inp)
    wo = pack_proj(inp["tm_w_o"])
    wkv2 = np.ascontiguousarray(np.stack([pack_proj(inp["sb_w_k"]), pack_proj(inp["sb_w_v"])], 0))
    wq = pack_proj(inp["sb_w_q"])
    wo2 = pack_proj(inp["sb_w_o"])
    sbb = np.ascontiguousarray(np.broadcast_to(inp["sb_bias"].reshape(1, 2 * H), (128, 2 * H))).astype(np.float32)
    ck = inp["cache_k"].reshape(-1, D)
    cv = inp["cache_v"].reshape(-1, D)
    maps = []
    for c in range(cfg.ncores):
        m = {
            "xp": np.ascontiguousarray(inp["x_prompt"][c]),
            "xs": np.ascontiguousarray(np.concatenate([inp["x_sample"][c * cfg.sb:(c + 1) * cfg.sb].reshape(cfg.nts, D),
                                                       np.zeros((128 - cfg.nts, D), np.float32)], 0)),
            "vecs": vecs, "cst": cst, "wffn": wffn, "v64": v64, "c64": c64,
            "wrkv": wrkv, "wo": wo, "l1": l1, "l2": l2,
            "wkv2": wkv2, "wq": wq, "wo2": wo2, "sbb": sbb, "ck": ck, "cv": cv,
            "pt": np.ascontiguousarray(inp["page_table"][c * cfg.sb:(c + 1) * cfg.sb].reshape(1, -1).astype(np.int32)),
            "sshift": np.ascontiguousarray(inp["state_shift"][:, c * cfg.sb:(c + 1) * cfg.sb]),
            "swkv": np.ascontiguousarray(inp["state_wkv"][:, c * cfg.sb:(c + 1) * cfg.sb]),
        }
        maps.append(m)
    return maps


OUT_NAMES = ("yp", "ys", "wkv_p", "wkv_s", "shift_p", "shift_s", "kp", "vp", "ks", "vs")


def assemble(cfg, results):
    n = cfg.ncores
    SEQ = cfg.seq
    yp = np.stack([r["yp"] for r in results], 0)
    ys = np.concatenate([r["ys"][:cfg.nts].reshape(cfg.sb, cfg.dseq, D) for r in results], 0)
    wkv_p = np.stack([r["wkv_p"] for r in results], 1)
    shift_p = np.stack([r["shift_p"] for r in results], 1)
    kp = np.stack([r["kp"].reshape(SEQ // 128, 128, H, DH) for r in results], 0)
    vp = np.stack([r["vp"].reshape(SEQ // 128, 128, H, DH) for r in results], 0)
    wkv_s = np.concatenate([r["wkv_s"] for r in results], 1)
    shift_s = np.concatenate([r["shift_s"] for r in results], 1)
    ks = np.concatenate([r["ks"][:cfg.nts].reshape(cfg.sb, cfg.dseq, H, DH) for r in results], 0)
    vs = np.concatenate([r["vs"][:cfg.nts].reshape(cfg.sb, cfg.dseq, H, DH) for r in results], 0)
    return (yp, ys, wkv_p, shift_p, kp, vp, wkv_s, shift_s, ks, vs)


def kernel(**inp):
    cfg = Cfg()
    b = Builder(cfg)
    nc = b.build()
    inp = {k: np.asarray(v) for k, v in inp.items()}
    maps = make_in_maps(cfg, inp)
    res = run_bass_kernel_spmd(nc, maps, core_ids=list(range(cfg.ncores)))
    return assemble(cfg, res.results)
```

```python
import contextlib
import numpy as np
import concourse.bass as bass
import concourse.mybir as mybir
from concourse.bass_utils import run_bass_kernel_spmd

F32, BF16, I32 = mybir.dt.float32, mybir.dt.bfloat16, mybir.dt.int32
AF = mybir.ActivationFunctionType
ALU = mybir.AluOpType

D = 1024
NCH = 8
FF = 2816
NFC = 22
H = 16
DH = 64
DEPTH = 4
NA = 2
ALPHA = (2 * DEPTH) ** 0.25
LN_EPS = 1e-5
GN_EPS = 64e-5
SB_SCALE = DH ** -0.5


class Cfg:
    def __init__(self, seq=2048, npg=16, nphys=2560, sb=16, dseq=4, ncores=8):
        self.seq, self.npg, self.nphys, self.sb, self.dseq, self.ncores = seq, npg, nphys, sb, dseq, ncores
        self.nts = sb * dseq
        self.nt = seq + self.nts


class Reg:
    __slots__ = ("name", "w", "rs", "track", "id", "ndma", "psum")
    _n = 0

    def __init__(self, name=""):
        self.name = name
        self.w = None
        self.rs = {}
        self.track = True
        Reg._n += 1
        self.id = Reg._n
        self.ndma = 0
        self.psum = name.startswith("ps")


import os as _os
STRICT_SYNC = bool(_os.environ.get("KSTRICT"))
ENGS = ["pe", "act", "dve", "pool", "sp"]


class Prog:
    def __init__(self):
        self.ops = {e: [] for e in ENGS}
        self.seen = {e: {} for e in ENGS}
        self.dma_regs = {}

    def op(self, eng, fn, reads=(), writes=(), dma=None):
        need = {}

        def add(k, v):
            if need.get(k, 0) < v:
                need[k] = v

        isdma = dma is not None
        for r in reads:
            if r.w is not None:
                add(*r.w)
            if r.psum and eng in ("act", "dve"):
                other = "dve" if eng == "act" else "act"
                if other in r.rs:
                    add(other, r.rs[other])
        strict = isdma or (STRICT_SYNC and eng != "pe")
        for r in writes:
            if r.w is not None and (strict or r.w[0] != eng):
                add(*r.w)
            for k, v in r.rs.items():
                if strict or k != eng:
                    add(k, v)
        seen = self.seen[eng]
        waits = []
        for k, v in need.items():
            if seen.get(k, 0) < v:
                seen[k] = v
                waits.append((k, v))
                if isinstance(k, str):
                    self.ops[k][v - 1]["inc"] = True
        rec = dict(fn=fn, waits=waits, inc=False, dma=dma)
        self.ops[eng].append(rec)
        idx = len(self.ops[eng])
        if fn is None:
            return
        if isdma:
            dma.ndma += 1
            self.dma_regs[dma.id] = dma
            tok = (("d", dma.id), 16 * dma.ndma)
        else:
            tok = (eng, idx)
        for r in reads:
            if r.track:
                if r.rs.get(tok[0], 0) < tok[1]:
                    r.rs[tok[0]] = tok[1]
        for r in writes:
            r.w = tok
            r.rs = {}

    def barrier(self):
        toks = []
        for e in ENGS:
            for i in range(len(self.ops[e]), 0, -1):
                rec = self.ops[e][i - 1]
                if rec["fn"] is not None and rec["dma"] is None:
                    toks.append((e, i))
                    break
        for r in self.dma_regs.values():
            toks.append((("d", r.id), 16 * r.ndma))
        for e in ENGS:
            seen = self.seen[e]
            waits = []
            for k, v in toks:
                if seen.get(k, 0) < v:
                    seen[k] = v
                    waits.append((k, v))
                    if isinstance(k, str):
                        self.ops[k][v - 1]["inc"] = True
            self.ops[e].append(dict(fn=None, waits=waits, inc=False, dma=None))

    def emit(self, nc, es):
        sems = {e: es.enter_context(nc.semaphore("s_" + e)) for e in ENGS}
        dsem = {i: es.enter_context(nc.semaphore("d%d" % i)) for i in self.dma_regs}
        pref = {}
        for e in ENGS:
            c = 0
            arr = [0]
            for rec in self.ops[e]:
                if rec["inc"]:
                    c += 1
                arr.append(c)
            pref[e] = arr
        block = es.enter_context(nc.Block())

        def run(eh, eng):
            for rec in self.ops[eng]:
                for k, v in rec["waits"]:
                    if isinstance(k, str):
                        eh.wait_ge(sems[k], pref[k][v])
                    else:
                        eh.wait_ge(dsem[k[1]], v)
                if rec["fn"] is None:
                    continue
                ins = rec["fn"](eh)
                if rec["dma"] is not None:
                    ins.then_inc(dsem[rec["dma"].id], 16)
                elif rec["inc"]:
                    ins.then_inc(sems[eng], 1)

        @block.tensor
        def _(e):
            run(e, "pe")

        @block.scalar
        def _(e):
            run(e, "act")

        @block.vector
        def _(e):
            run(e, "dve")

        @block.gpsimd
        def _(e):
            run(e, "pool")

        @block.sync
        def _(e):
            run(e, "sp")


def vec_index():
    names = []
    for li in range(DEPTH):
        for s in range(3):
            names.append(("ln_g", li, s))
            names.append(("ln_b", li, s))
    for li in range(NA):
        for i in range(6):
            names.append(("mu", li, i))
        for n in ("w0", "a0", "k_k", "k_a", "r_k", "gn_g", "gn_b"):
            names.append((n, li, 0))
    names.append(("v0", 0, 0))
    return {n: i for i, n in enumerate(names)}


VIDX = vec_index()
NV = len(VIDX)


def pack_vecs(inp):
    out = np.zeros((NV, D), np.float32)
    for (n, li, s), i in VIDX.items():
        if n == "ln_g":
            v = inp["ln_g"][li, s]
        elif n == "ln_b":
            v = inp["ln_b"][li, s]
        elif n == "mu":
            v = inp["tm_mu"][li, s]
        elif n == "v0":
            v = inp["tm_v0"][0]
        elif n == "r_k":
            v = inp["tm_r_k"][li].reshape(D)
        else:
            v = inp["tm_" + n][li]
        out[i] = v
    return np.ascontiguousarray(out.reshape(NV, NCH, 128).transpose(2, 0, 1).reshape(128, NV * NCH))


def pack_consts():
    i = np.arange(128)
    ident = (i[:, None] == i[None, :]).astype(np.float32)
    ones = np.ones((128, 128), np.float32)
    blk = ((i[:, None] // 64) == (i[None, :] // 64)).astype(np.float32)
    tri = (i[:, None] > i[None, :]).astype(np.float32)
    dmask = (i[:, None] < i[None, :]).astype(np.float32)
    return np.ascontiguousarray(np.concatenate([ident, ones, blk, tri, dmask], axis=1))


def pack_ffn(inp):
    g = inp["ffn_w_gate"].reshape(8, NCH, 128, NFC, 128).transpose(0, 3, 2, 1, 4).reshape(8, NFC, 128, 1024)
    u = inp["ffn_w_up"].reshape(8, NCH, 128, NFC, 128).transpose(0, 3, 2, 1, 4).reshape(8, NFC, 128, 1024)
    d = inp["ffn_w_down"].reshape(8, NFC, 128, 1024)
    return np.ascontiguousarray(np.concatenate([g, u, d], axis=3))


def pack_proj(w):
    lead = w.shape[:-2]
    n = w.shape[-1]
    w = w.reshape(lead + (NCH, 128, n))
    nd = len(lead)
    perm = tuple(range(nd)) + (nd + 1, nd, nd + 2)
    return np.ascontiguousarray(w.transpose(perm))


def I(name, *a, **k):
    return lambda e: getattr(e, name)(*a, **k)


class Builder:
    def __init__(self, cfg, dbg=False, nlayers=DEPTH, stop_after=None, mixers=True, skip=()):
        self.mixers = mixers
        self.skip = skip
        self.cfg = cfg
        self.dbg = dbg
        self.nc = bass.Bass("TRN2", target_bir_lowering=False)
        self.P = Prog()
        self.es = contextlib.ExitStack()
        self.nlayers = nlayers
        self.stop_after = stop_after

    def sb(self, name, shape, dt):
        return self.es.enter_context(self.nc.sbuf_tensor(name, shape, dt))

    def dram_in(self, name, shape, dt=F32):
        return self.nc.dram_tensor(name, list(shape), dt, kind="ExternalInput").ap()

    def dram_out(self, name, shape, dt=F32):
        return self.nc.dram_tensor(name, list(shape), dt, kind="ExternalOutput").ap()

    def build(self):
        cfg, nc, P = self.cfg, self.nc, self.P
        NT, SEQ, NTS = cfg.nt, cfg.seq, cfg.nts
        self.tiles = [(t0, min(512, NT - t0)) for t0 in range(0, NT, 512)]
        self.d_xp = self.dram_in("xp", [SEQ, D])
        self.d_xs = self.dram_in("xs", [128, D])
        self.d_vecs = self.dram_in("vecs", [128, NV * NCH])
        self.d_cst = self.dram_in("cst", [128, 5 * 128])
        self.d_wffn = self.dram_in("wffn", [8, NFC, 128, 3072])
        SBN = cfg.sb
        self.d_v64 = self.dram_in("v64", [64, NV64 * H])
        self.d_c64 = self.dram_in("c64", [64, 576])
        self.d_wrkv = self.dram_in("wrkv", [NA, 3, 128, NCH, D])
        self.d_wo = self.dram_in("wo", [NA, 128, NCH, D])
        self.d_l1 = self.dram_in("l1", [NA, 128, NCH, 288])
        self.d_l2 = self.dram_in("l2", [NA, 128, 4, D])
        self.d_sshift = self.dram_in("sshift", [NA, SBN, D])
        self.d_swkv = self.dram_in("swkv", [NA, SBN, H, DH, DH])
        self.d_wkv_p = self.dram_out("wkv_p", [NA, H, DH, DH])
        self.d_wkv_s = self.dram_out("wkv_s", [NA, SBN, H, DH, DH])
        self.d_shift_p = self.dram_out("shift_p", [NA, D])
        self.d_shift_s = self.dram_out("shift_s", [NA, SBN, D])
        self.d_vf = self.nc.dram_tensor("vf", [64, H, NT], F32, kind="Internal").ap()
        NPG = cfg.npg
        self.d_wkv2 = self.dram_in("wkv2", [2, 128, NCH, D])
        self.d_wq = self.dram_in("wq", [2, 128, NCH, D])
        self.d_wo2 = self.dram_in("wo2", [2, 128, NCH, D])
        self.d_sbb = self.dram_in("sbb", [128, 2 * H])
        self.d_pt = self.dram_in("pt", [1, SBN * NPG], I32)
        self.d_ck = self.dram_in("ck", [cfg.nphys * 128, D])
        self.d_cv = self.dram_in("cv", [cfg.nphys * 128, D])
        self.d_kp = self.dram_out("kp", [SEQ, D])
        self.d_vp = self.dram_out("vp", [SEQ, D])
        self.d_ks = self.dram_out("ks", [128, D])
        self.d_vs = self.dram_out("vs", [128, D])
        self.d_KT = self.nc.dram_tensor("KTs", [NCH, 128, NT + 64], BF16, kind="Internal").ap()
        self.d_VB = self.nc.dram_tensor("VBs", [NT + 64, D], BF16, kind="Internal").ap()
        self.d_yp = self.dram_out("yp", [SEQ, D])
        self.d_ys = self.dram_out("ys", [128, D])
        self.xT = self.sb("xT", [128, NCH, NT + 64], F32)
        self.xb = self.sb("xb", [128, NCH, NT + 64], BF16)
        self.vecs = self.sb("vecs_sb", [128, NV * NCH], F32)
        self.cst = self.sb("cst_sb", [128, 5 * 128], F32)
        self.cstb = self.sb("cstb", [128, 5 * 128], BF16)
        self.vecs64 = self.sb("v64_sb", [64, NV64 * H], F32)
        self.cst64 = self.sb("c64_sb", [64, 576], F32)
        self.tiny = self.sb("tiny", [128, 1], F32)
        self.gneps = self.sb("gneps", [128, 1], F32)
        self.rv64, self.rc64, self.rtiny = Reg("v64"), Reg("c64"), Reg("tiny")
        self.onec = self.sb("onec", [128, 1], F32)
        self.sbb = self.sb("sbb_sb", [128, 2 * H], F32)
        self.pt_sb = self.sb("pt_sb", [128, cfg.sb * cfg.npg], I32)
        self.pidx = self.sb("pidx", [128, cfg.sb * cfg.npg], I32)
        self.iota_p = self.sb("iota_p", [128, 1], I32)
        self.rsbb, self.rpidx = Reg("sbb"), Reg("pidx")
        self.ARENA = getattr(self, 'ARENA_OVERRIDE', 24300)
        self.arena = self.sb("arena", [128, self.ARENA], F32)
        self.ps = [self.es.enter_context(nc.psum_tensor("ps%d" % i, [128, 512], F32)) for i in range(8)]
        self.rps = [Reg("ps%d" % i) for i in range(8)]
        self.rx = [Reg("x%d" % i) for i in range(len(self.tiles))]
        self.rb = [Reg("xb%d" % i) for i in range(len(self.tiles))]
        self.rvec = Reg("vecs")
        self.rcst = Reg("cst")
        self.out_regs = []

        P.op("sp", I("dma_start", out=self.vecs[:], in_=self.d_vecs[:, :]), writes=[self.rvec], dma=self.rvec)
        P.op("sp", I("dma_start", out=self.cst[:], in_=self.d_cst[:, :]), writes=[self.rcst], dma=self.rcst)
        P.op("sp", I("dma_start", out=self.vecs64[:], in_=self.d_v64[:, :]), writes=[self.rv64], dma=self.rv64)
        P.op("sp", I("dma_start", out=self.cst64[:], in_=self.d_c64[:, :]), writes=[self.rc64], dma=self.rc64)
        P.op("pool", I("memset", self.tiny[:], 1e-24), writes=[self.rtiny])
        P.op("pool", I("memset", self.gneps[:], GN_EPS), writes=[self.rtiny])
        P.op("pool", I("memset", self.onec[:], 1.0), writes=[self.rtiny])
        P.op("sp", I("dma_start", out=self.sbb[:], in_=self.d_sbb[:, :]), writes=[self.rsbb], dma=self.rsbb)
        rpt = Reg("ptsb")
        P.op("sp", I("dma_start", out=self.pt_sb[:], in_=self.d_pt[0:1, :].partition_broadcast(128)), writes=[rpt], dma=rpt)
        P.op("pool", I("iota", self.iota_p[:], pattern=[[0, 1]], base=0, channel_multiplier=1), writes=[self.rpidx])
        P.op("pool", I("tensor_scalar", out=self.pidx[:], in0=self.pt_sb[:], scalar1=128, scalar2=self.iota_p[:, 0:1], op0=ALU.mult, op1=ALU.add),
             reads=[rpt, self.rpidx], writes=[self.rpidx])
        rcb = Reg("cstb")
        P.op("dve", I("tensor_copy", out=self.cstb[:], in_=self.cst[:]), reads=[self.rcst], writes=[rcb])
        self.rcstb = rcb
        self.ident = self.cst[:, 0:128]
        self.ones = self.cst[:, 128:256]
        self.blk = self.cst[:, 256:384]

        if 'load' not in self.skip:
            self.load_x()
        for li in range(self.nlayers):
            if "ffn" not in self.skip:
                self.ffn(li, 0)
            if "ln" not in self.skip:
                self.layernorm(li, 0)
            if self.stop_after == (li, 0):
                break
            if self.mixers:
                if li < NA:
                    self.mixer_rwkv(li)
                else:
                    self.mixer_sb(li - NA)
                self.layernorm(li, 1)
            self.ffn(li, 1)
            self.layernorm(li, 2)
            if self.mixers and li == NA - 1:
                self.kv_proj()
        if 'store' not in self.skip:
            self.store_y()
        P.op("sp", None, reads=self.out_regs)
        P.barrier()
        P.emit(nc, self.es)
        self.es.close()
        return nc

    def vec(self, name, li, s, c):
        i = VIDX[(name, li, s)]
        return self.vecs[:, i * NCH + c:i * NCH + c + 1]

    def tile_of(self, t):
        return t // 512

    def aview(self, off, n, dt=F32):
        a = self.arena[:, off:off + n]
        return a if dt == F32 else a.bitcast(dt)

    def load_x(self):
        cfg, P = self.cfg, self.P
        stage = [self.aview(i * 1024, 1024) for i in range(2)]
        rst = [Reg("xstage%d" % i) for i in range(2)]
        srcs = [(self.d_xp, r0, 128, r0) for r0 in range(0, cfg.seq, 128)]
        srcs.append((self.d_xs, 0, 128, cfg.seq))
        import os
        srcs = srcs[:int(os.environ.get("NSRC", "99"))]
        for i, (src, r0, n, t0) in enumerate(srcs):
            s = i % 2
            P.op("sp", I("dma_start", out=stage[s][0:n, :], in_=src[r0:r0 + n, :]),
                 writes=[rst[s]], dma=rst[s])
            for half in range(2):
                pb = (i * 2 + half) % 2
                pst = self.ps[pb]
                for cc in range(4):
                    c = half * 4 + cc
                    P.op("pe", I("transpose",
                        out=pst[:, cc * 128:cc * 128 + 128], in_=stage[s][:, c * 128:(c + 1) * 128], identity=self.ident),
                        reads=[rst[s], self.rcst], writes=[self.rps[pb]])
                ti = self.tile_of(t0)
                pv = pst[:].rearrange("p (c t) -> p c t", c=4)[:, :, 0:n]
                P.op("dve", I("tensor_copy",
                    out=self.xT[:, half * 4:half * 4 + 4, t0:t0 + n], in_=pv), reads=[self.rps[pb]], writes=[self.rx[ti]])
                if not os.environ.get("NOACT"):
                    P.op("act", I("copy",
                        out=self.xb[:, half * 4:half * 4 + 4, t0:t0 + n], in_=pv), reads=[self.rps[pb]], writes=[self.rb[ti]])
        P.barrier()

    def store_y(self):
        cfg, P = self.cfg, self.P
        P.barrier()
        stage = [self.aview(i * 1024, 1024) for i in range(2)]
        rst = [Reg("ystage%d" % i) for i in range(2)]
        dsts = [(self.d_yp, r0, 128, r0) for r0 in range(0, cfg.seq, 128)]
        dsts.append((self.d_ys, 0, 128, cfg.seq))
        for i, (dst, r0, n, t0) in enumerate(dsts):
            s = i % 2
            ti = self.tile_of(t0)
            for half in range(2):
                pb = (i * 2 + half) % 2
                pst = self.ps[pb]
                for cc in range(4):
                    c = half * 4 + cc
                    P.op("pe", I("transpose",
                        out=pst[:, cc * 128:(cc + 1) * 128], in_=self.xT[:, c, t0:t0 + 128], identity=self.ident),
                        reads=[self.rx[ti], self.rcst], writes=[self.rps[pb]])
                eng = "dve" if half == 0 else "act"
                if eng == "dve":
                    P.op("dve", I("tensor_copy",
                        out=stage[s][0:n, half * 512:(half + 1) * 512], in_=pst[0:n, :]), reads=[self.rps[pb]], writes=[rst[s]])
                else:
                    P.op("act", I("copy",
                        out=stage[s][0:n, half * 512:(half + 1) * 512], in_=pst[0:n, :]), reads=[self.rps[pb]], writes=[rst[s]])
            P.op("sp", I("dma_start", out=dst[r0:r0 + n, :], in_=stage[s][0:n, :]),
                 reads=[rst[s]], dma=rst[s])
            if rst[s] not in self.out_regs:
                self.out_regs.append(rst[s])

    def ffn(self, li, j):
        cfg, P = self.cfg, self.P
        NT = cfg.nt
        P.barrier()
        widx = li * 2 + j
        G = 3
        groups = [list(range(g0, min(g0 + G, NFC))) for g0 in range(0, NFC, G)]
        o = 0
        stage = []
        for i in range(G):
            stage.append(self.aview(o, 3072)); o += 3072
        wb = []
        for i in range(2 * G):
            wb.append(self.aview(o, 1536, BF16)); o += 1536
        NH = 4
        hb = []
        for i in range(NH):
            hb.append(self.aview(o, 128, BF16)); o += 128
        sg = []
        for i in range(NH):
            sg.append(self.aview(o, 256)); o += 256
        assert o <= self.ARENA
        rstage = [Reg("wst%d" % i) for i in range(G)]
        rwb = [Reg("wb%d" % i) for i in range(2 * G)]
        rh = [Reg("h%d" % i) for i in range(NH)]
        rsg = [Reg("sg%d" % i) for i in range(NH)]
        ttiles = [(t0, min(256, NT - t0)) for t0 in range(0, NT, 256)]
        for ti, (t0, n) in enumerate(self.tiles):
            P.op("pool", I("tensor_scalar",
                out=self.xT[:, :, t0:t0 + n], in0=self.xT[:, :, t0:t0 + n], scalar1=ALPHA, scalar2=None, op0=ALU.mult),
                reads=[self.rx[ti]], writes=[self.rx[ti]])
        YB = [4, 5, 6, 7]
        GB = [0, 1, 2, 3]
        gctr = 0

        def w_dma(gi, k):
            fc = groups[gi][k]
            P.op("sp", I("dma_start", out=stage[k][:, :], in_=self.d_wffn[widx, fc, :, :]),
                 writes=[rstage[k]], dma=rstage[k])

        def w_cast(gi, k):
            slot = (gi % 2) * G + k
            P.op("pool", I("tensor_copy", out=wb[slot][:, 0:2048], in_=stage[k][:, 0:2048]),
                 reads=[rstage[k]], writes=[rwb[slot]])
            P.op("act", I("copy", out=wb[slot][:, 2048:3072], in_=stage[k][:, 2048:3072]),
                 reads=[rstage[k]], writes=[rwb[slot]])

        for k in range(len(groups[0])):
            w_dma(0, k)
        for k in range(len(groups[0])):
            w_cast(0, k)
        for gi, grp in enumerate(groups):
            nxt = groups[gi + 1] if gi + 1 < len(groups) else []
            for k in range(len(nxt)):
                w_dma(gi + 1, k)
            nsteps = len(ttiles) * len(grp)
            cast_at = {max(0, min(nsteps - len(nxt), nsteps // 2)) + k: k for k in range(len(nxt))}
            step = 0
            for tj, (t0, n) in enumerate(ttiles):
                ti = self.tile_of(t0)
                pend = None
                for k, fc in enumerate(grp):
                    if step in cast_at:
                        w_cast(gi + 1, cast_at[step])
                    step += 1
                    slot = (gi % 2) * G + k
                    gb = GB[gctr % 4]
                    hs = gctr % NH
                    gctr += 1
                    w = wb[slot]
                    for which in range(2):
                        for kc in range(NCH):
                            P.op("pe", I("matmul",
                                self.ps[gb][:, which * 256:which * 256 + n],
                                lhsT=w[:, which * 1024 + kc * 128:which * 1024 + (kc + 1) * 128],
                                rhs=self.xb[:, kc, t0:t0 + n], start=(kc == 0), stop=(kc == NCH - 1)),
                                reads=[rwb[slot], self.rb[ti]], writes=[self.rps[gb]])
                    P.op("act", I("activation", out=sg[hs][:, 0:n], in_=self.ps[gb][:, 0:n], func=AF.Silu),
                         reads=[self.rps[gb]], writes=[rsg[hs]])
                    P.op("dve", I("tensor_tensor",
                        out=hb[hs][:, 0:n], in0=sg[hs][:, 0:n], in1=self.ps[gb][:, 256:256 + n], op=ALU.mult),
                        reads=[rsg[hs], self.rps[gb]], writes=[rh[hs]])
                    if pend is not None:
                        self.ffn_down(*pend)
                    pend = (w, slot, rwb, hb[hs], rh[hs], n, YB, k == 0, k == len(grp) - 1)
                self.ffn_down(*pend)
                for dc in range(NCH):
                    yb = YB[dc // 2]
                    P.op("dve", I("scalar_tensor_tensor",
                        out=self.xT[:, dc, t0:t0 + n], in0=self.ps[yb][:, (dc % 2) * 256:(dc % 2) * 256 + n], scalar=0.5,
                        in1=self.xT[:, dc, t0:t0 + n], op0=ALU.mult, op1=ALU.add),
                        reads=[self.rps[yb], self.rx[ti]], writes=[self.rx[ti]])
            for st2 in range(step, nsteps + len(nxt)):
                if st2 in cast_at:
                    w_cast(gi + 1, cast_at[st2])

    def ffn_down(self, w, slot, rwb, h, rh, n, YB, first, last):
        P = self.P
        for dc in range(NCH):
            yb = YB[dc // 2]
            P.op("pe", I("matmul",
                self.ps[yb][:, (dc % 2) * 256:(dc % 2) * 256 + n],
                lhsT=w[:, 2048 + dc * 128:2048 + (dc + 1) * 128], rhs=h[:, 0:n],
                start=(first and dc % 2 == 0), stop=last, skip_group_check=True),
                reads=[rwb[slot], rh], writes=[self.rps[yb]])

    def mk_alloc(self):
        cfg = self.cfg
        xbw = self.xb[:].rearrange("p c t -> p (c t)").bitcast(F32)
        pools = [[self.arena, 0, self.ARENA], [xbw, 0, 4 * (cfg.nt + 64)]]

        def alloc(words, dt=F32):
            for p in pools:
                if p[1] + words <= p[2]:
                    a = p[0][:, p[1]:p[1] + words]
                    p[1] += words
                    return a if dt == F32 else a.bitcast(dt)
            raise RuntimeError("arena overflow")
        alloc.pools = pools
        return alloc

    def v64(self, name, li, h):
        i = V64[(name, li)]
        return self.vecs64[0:64, i * H + h:i * H + h + 1]

    def mixer_rwkv(self, li):
        cfg, P = self.cfg, self.P
        SEQ, NT = cfg.seq, cfg.nt
        P.barrier()
        alloc = self.mk_alloc()
        C0 = float(np.exp(-0.5))
        stage = alloc(4096)
        rstage = Reg("tmstage")
        wts = []
        rw = Reg("tmw")
        for wi in range(4):
            wb_ = alloc(4096, BF16).rearrange("p (k n) -> p k n", k=NCH)
            src = self.d_wrkv[li, wi] if wi < 3 else self.d_wo[li]
            for half in range(2):
                P.op("sp", I("dma_start",
                    out=stage.rearrange("p (k n) -> p k n", k=4), in_=src[:, half * 4:half * 4 + 4, :]),
                    writes=[rstage], dma=rstage)
                eng = "pool" if half == 0 else "act"
                if eng == "pool":
                    P.op("pool", I("tensor_copy",
                        out=wb_[:, half * 4:half * 4 + 4, :], in_=stage.rearrange("p (k n) -> p k n", k=4)), reads=[rstage], writes=[rw])
                else:
                    P.op("act", I("copy",
                        out=wb_[:, half * 4:half * 4 + 4, :], in_=stage.rearrange("p (k n) -> p k n", k=4)), reads=[rstage], writes=[rw])
            wts.append(wb_)
        wr, wk, wv, wo = wts
        l1 = alloc(1152, BF16).rearrange("p (k n) -> p k n", k=NCH)
        l2 = alloc(2048, BF16).rearrange("p (k n) -> p k n", k=4)
        P.op("sp", I("dma_start", out=stage[:, 0:2304].rearrange("p (k n) -> p k n", k=NCH), in_=self.d_l1[li]),
             writes=[rstage], dma=rstage)
        P.op("pool", I("tensor_copy", out=l1, in_=stage[:, 0:2304].rearrange("p (k n) -> p k n", k=NCH)), reads=[rstage], writes=[rw])
        P.op("sp", I("dma_start", out=stage.rearrange("p (k n) -> p k n", k=4), in_=self.d_l2[li]),
             writes=[rstage], dma=rstage)
        P.op("pool", I("tensor_copy", out=l2, in_=stage.rearrange("p (k n) -> p k n", k=4)), reads=[rstage], writes=[rw])
        LO = {"w": (0, 64), "a": (64, 64), "v": (128, 32), "g": (160, 128)}
        P.barrier()
        alloc.pools.append([stage, 0, 4096])

        TT = 64
        HG = 4
        mixes = [alloc(256, BF16).rearrange("p (c t) -> p c t", c=NCH) for _ in range(6)]
        rmix = [Reg("mix%d" % i) for i in range(6)]
        xx = alloc(512).rearrange("p (c t) -> p c t", c=NCH)
        rxx = Reg("xx")
        xlast = alloc(8)
        rxlast = Reg("xlast")
        lo1 = {k: alloc(32, BF16) for k in LO}
        rlo1 = {k: Reg("lo1" + k) for k in LO}
        ygs = alloc(256, BF16).rearrange("p (c t) -> p c t", c=NCH)
        ygo = alloc(256, BF16).rearrange("p (c t) -> p c t", c=NCH)
        rygs, rygo = Reg("ygs"), Reg("ygo")
        NB = 12
        fb = [alloc(256) for _ in range(NB)]
        rfb = [Reg("fb%d" % i) for i in range(NB)]
        f3 = lambda i: fb[i][0:64, :].rearrange("p (h t) -> p h t", h=HG)
        AR = alloc(256, BF16)
        BT = alloc(128, BF16)
        KT = alloc(128, BF16)
        BH = alloc(128, BF16)
        KH = alloc(128, BF16)
        rAR, rBT, rKT, rBH, rKH = Reg("AR"), Reg("BT"), Reg("KT"), Reg("BH"), Reg("KH")
        bkv = alloc(384, BF16)
        rbkv = Reg("bkv")
        Mb = alloc(256, BF16)
        Mk = alloc(256, BF16)
        rMb, rMk = Reg("Mb"), Reg("Mk")
        nbuf = [alloc(128, BF16) for _ in range(6)]
        rnb = [Reg("nb%d" % i) for i in range(6)]
        Wt = alloc(128, BF16)
        Ut = alloc(128, BF16)
        rWt, rUt = Reg("Wt"), Reg("Ut")
        S32 = alloc(1024)
        Sb = alloc(512, BF16)
        rS32, rSb = Reg("S32"), Reg("Sb")
        sst = [alloc(256) for _ in range(2)]
        rsst = [Reg("sst%d" % i) for i in range(2)]
        s32s = alloc(256)
        sbs = alloc(128, BF16)
        rs32s, rsbs = Reg("s32s"), Reg("sbs")
        sost = [alloc(256) for _ in range(2)]
        rsost = [Reg("sost%d" % i) for i in range(2)]
        shT = alloc(128)
        rshT = Reg("shT")
        ones64 = self.cst[0:64, 128:192]
        ident = self.ident
        identb = self.cstb[:, 0:128]

        P.op("pool", I("memset", S32[0:64, :], 0.0), writes=[rS32])
        P.op("pool", I("memset", Sb[0:64, :], 0.0), writes=[rSb])
        P.op("pool", I("memset", xlast, 0.0), writes=[rxlast])
        for c in range(NCH):
            P.op("sp", I("dma_start", out=shT[:, c * 16:(c + 1) * 16], in_=self.d_sshift[li][:, c * 128:(c + 1) * 128].rearrange("b p -> p b"),
                         allow_slow_non_contiguous=True), writes=[rshT], dma=rshT)

        tiles = [(t0, 64, 1, False) for t0 in range(0, SEQ, TT)] + [(SEQ, 4, 16, True)]
        for (t0, C, nch, samp) in tiles:
            ti = self.tile_of(t0)
            rxt = self.rx[ti]
            msk = self.cst64[0:C, (256 if samp else 0):(256 if samp else 0) + 256]
            m_si = msk[:, 0:2 * C]
            m_lo = msk[:, 128:128 + C]
            m_ey = msk[:, 192:192 + C]
            xt = self.xT[:, :, t0:t0 + TT]
            if not samp:
                P.op("dve", I("tensor_tensor", out=xx[:, :, 1:TT], in0=self.xT[:, :, t0:t0 + TT - 1],
                                                             in1=self.xT[:, :, t0 + 1:t0 + TT], op=ALU.subtract),
                     reads=[rxt], writes=[rxx])
                P.op("dve", I("tensor_tensor", out=xx[:, :, 0], in0=xlast, in1=self.xT[:, :, t0], op=ALU.subtract),
                     reads=[rxt, rxlast], writes=[rxx])
                P.op("pool", I("tensor_copy", out=xlast, in_=self.xT[:, :, t0 + TT - 1]), reads=[rxt, rxx], writes=[rxlast])
                if t0 + TT == SEQ:
                    P.op("sp", I("dma_start", out=self.d_shift_p[li].rearrange("(c p) -> p c", p=128), in_=xlast,
                                                     allow_slow_non_contiguous=True), reads=[rxlast], dma=rxlast)
                    self.out_regs.append(rxlast)
            else:
                for c in range(NCH):
                    xv = self.xT[:, c, t0:t0 + TT].rearrange("p (b t) -> p b t", t=4)
                    xxv = xx[:, c, :].rearrange("p (b t) -> p b t", t=4)
                    P.op("dve", I("tensor_tensor", out=xxv[:, :, 1:4], in0=xv[:, :, 0:3], in1=xv[:, :, 1:4], op=ALU.subtract),
                         reads=[rxt], writes=[rxx])
                    P.op("dve", I("tensor_tensor", out=xxv[:, :, 0], in0=shT[:, c * 16:(c + 1) * 16], in1=xv[:, :, 0], op=ALU.subtract),
                         reads=[rxt, rshT], writes=[rxx])
                rsh_out = Reg("shout")
                for c in range(NCH):
                    P.op("sp", I("dma_start", out=self.d_shift_s[li][:, c * 128:(c + 1) * 128].rearrange("b p -> p b"),
                                 in_=self.xT[:, c, t0:t0 + TT].rearrange("p (b t) -> p b t", t=4)[:, :, 3],
                                 allow_slow_non_contiguous=True), reads=[rxt], dma=rxt)
                if rxt not in self.out_regs:
                    self.out_regs.append(rxt)
            for m in range(6):
                for c in range(NCH):
                    P.op("dve", I("scalar_tensor_tensor",
                        out=mixes[m][:, c, :], in0=xx[:, c, :], scalar=self.vec("mu", li, m, c), in1=self.xT[:, c, t0:t0 + TT],
                        op0=ALU.mult, op1=ALU.add), reads=[rxx, rxt, self.rvec], writes=[rmix[m]])
            for key, mi in (("w", 1), ("a", 4), ("v", 3), ("g", 5)):
                if key == "v" and li == 0:
                    continue
                o_, wdt = LO[key]
                pb = 0
                for kc in range(NCH):
                    P.op("pe", I("matmul",
                        self.ps[pb][0:wdt, 0:TT], lhsT=l1[:, kc, o_:o_ + wdt], rhs=mixes[mi][:, kc, :], start=(kc == 0), stop=(kc == NCH - 1)),
                        reads=[rw, rmix[mi]], writes=[self.rps[pb]])
                fn = {"w": AF.Tanh, "a": AF.Identity, "v": AF.Identity, "g": AF.Sigmoid}[key]
                P.op("act", I("activation", out=lo1[key][0:wdt, 0:TT], in_=self.ps[pb][0:wdt, 0:TT], func=fn),
                     reads=[self.rps[pb]], writes=[rlo1[key]])
            def group(hg):
                h0 = hg * HG
                NCOL = HG * TT
                psA, psB, psC, psD = 0, 1, 2, 3

                def proj(wmat, mi, pb):
                    for hh in range(HG):
                        h = h0 + hh
                        for kc in range(NCH):
                            P.op("pe", I("matmul",
                                self.ps[pb][0:64, hh * TT:(hh + 1) * TT], lhsT=wmat[:, kc, h * 64:(h + 1) * 64], rhs=mixes[mi][:, kc, :],
                                start=(kc == 0 and hh == 0), stop=(kc == NCH - 1), skip_group_check=True),
                                reads=[rw, rmix[mi]], writes=[self.rps[pb]])

                def lora2(key, pb):
                    o_, wdt = LO[key]
                    idx = {"w": 0, "a": 1, "v": 2, "g": 3}[key]
                    for hh in range(HG):
                        h = h0 + hh
                        P.op("pe", I("matmul",
                            self.ps[pb][0:64, hh * TT:(hh + 1) * TT], lhsT=l2[0:wdt, idx, h * 64:(h + 1) * 64], rhs=lo1[key][0:wdt, 0:TT],
                            start=(hh == 0), stop=True, skip_group_check=True), reads=[rw, rlo1[key]], writes=[self.rps[pb]])

                def F(i):
                    return fb[i][0:64, 0:NCOL]
                R32, K32, V32, SG, A32, G32, KK, KP, CS, PP, PI, PX = range(12)
                proj(wr, 0, psA)
                P.op("act", I("copy", out=F(R32), in_=self.ps[psA][0:64, 0:NCOL]), reads=[self.rps[psA]], writes=[rfb[R32]])
                proj(wk, 2, psB)
                P.op("act", I("copy", out=F(K32), in_=self.ps[psB][0:64, 0:NCOL]), reads=[self.rps[psB]], writes=[rfb[K32]])
                proj(wv, 3, psC)
                P.op("act", I("copy", out=F(V32), in_=self.ps[psC][0:64, 0:NCOL]), reads=[self.rps[psC]], writes=[rfb[V32]])
                lora2("w", psD)
                for hh in range(HG):
                    P.op("act", I("activation", out=F(SG)[:, hh * TT:(hh + 1) * TT], in_=self.ps[psD][0:64, hh * TT:(hh + 1) * TT],
                                                            func=AF.Sigmoid, bias=self.v64("w0", li, h0 + hh)),
                         reads=[self.rps[psD], self.rv64], writes=[rfb[SG]])
                lora2("a", psA)
                for hh in range(HG):
                    P.op("act", I("activation", out=F(A32)[:, hh * TT:(hh + 1) * TT], in_=self.ps[psA][0:64, hh * TT:(hh + 1) * TT],
                                                            func=AF.Sigmoid, bias=self.v64("a0", li, h0 + hh)),
                         reads=[self.rps[psA], self.rv64], writes=[rfb[A32]])
                lora2("g", psB)
                P.op("act", I("copy", out=F(G32), in_=self.ps[psB][0:64, 0:NCOL]), reads=[self.rps[psB]], writes=[rfb[G32]])
                vfv = self.d_vf[:, h0:h0 + HG, t0:t0 + TT]
                if li == 0:
                    P.op("sp", I("dma_start", out=vfv, in_=F(V32).rearrange("p (h t) -> p h t", h=HG)), reads=[rfb[V32]], dma=rfb[V32])
                else:
                    lora2("v", psC)
                    P.op("sp", I("dma_start", out=F(KK).rearrange("p (h t) -> p h t", h=HG), in_=vfv), writes=[rfb[KK]], dma=rfb[KK])
                    for hh in range(HG):
                        P.op("act", I("activation", out=F(KP)[:, hh * TT:(hh + 1) * TT], in_=self.ps[psC][0:64, hh * TT:(hh + 1) * TT],
                                                                func=AF.Sigmoid, bias=self.v64("v0", li, h0 + hh)),
                             reads=[self.rps[psC], self.rv64], writes=[rfb[KP]])
                    P.op("dve", I("tensor_tensor", out=F(KK), in0=F(KK), in1=F(V32), op=ALU.subtract), reads=[rfb[KK], rfb[V32]], writes=[rfb[KK]])
                    P.op("dve", I("tensor_tensor", out=F(KK), in0=F(KK), in1=F(KP), op=ALU.mult), reads=[rfb[KK], rfb[KP]], writes=[rfb[KK]])
                    P.op("dve", I("tensor_tensor", out=F(V32), in0=F(V32), in1=F(KK), op=ALU.add), reads=[rfb[KK], rfb[V32]], writes=[rfb[V32]])
                for hh in range(HG):
                    P.op("dve", I("tensor_scalar", out=F(KK)[:, hh * TT:(hh + 1) * TT], in0=F(K32)[:, hh * TT:(hh + 1) * TT],
                                                               scalar1=self.v64("k_k", li, h0 + hh), scalar2=None, op0=ALU.mult),
                         reads=[rfb[K32], self.rv64], writes=[rfb[KK]])
                P.op("act", I("activation", out=F(PP), in_=F(KK), func=AF.Square), reads=[rfb[KK]], writes=[rfb[PP]])
                P.op("pe", I("matmul", self.ps[psD][0:64, 0:NCOL], lhsT=ones64, rhs=F(PP), start=True, stop=True),
                     reads=[self.rcst, rfb[PP]], writes=[self.rps[psD]])
                P.op("act", I("activation", out=F(PP), in_=self.ps[psD][0:64, 0:NCOL], func=AF.Ln, bias=self.tiny[0:64, 0:1]),
                     reads=[self.rps[psD], self.rtiny], writes=[rfb[PP]])
                P.op("act", I("activation", out=F(PP), in_=F(PP), func=AF.Exp, scale=-0.5), reads=[rfb[PP]], writes=[rfb[PP]])
                P.op("dve", I("tensor_tensor", out=F(KK), in0=F(KK), in1=F(PP), op=ALU.mult), reads=[rfb[KK], rfb[PP]], writes=[rfb[KK]])
                for hh in range(HG):
                    P.op("dve", I("tensor_scalar", out=F(KP)[:, hh * TT:(hh + 1) * TT], in0=F(A32)[:, hh * TT:(hh + 1) * TT],
                                                               scalar1=-1.0, scalar2=self.v64("k_a", li, h0 + hh), op0=ALU.add, op1=ALU.mult),
                         reads=[rfb[A32], self.rv64], writes=[rfb[KP]])
                P.op("dve", I("scalar_tensor_tensor", out=F(KP), in0=F(KP), scalar=1.0, in1=F(K32), op0=ALU.add, op1=ALU.mult),
                     reads=[rfb[KP], rfb[K32]], writes=[rfb[KP]])
                for hh in range(HG):
                    for j in range(nch):
                        sl = slice(hh * TT + j * C, hh * TT + (j + 1) * C)
                        P.op("dve", I("tensor_tensor_scan", out=F(CS)[:, sl], data0=self.cst[0:64, 128:128 + C], data1=F(SG)[:, sl],
                                                                        initial=0.0, op0=ALU.mult, op1=ALU.add),
                             reads=[rfb[SG], self.rcst], writes=[rfb[CS]])
                P.op("act", I("activation", out=F(PP), in_=F(CS), func=AF.Exp, scale=-C0), reads=[rfb[CS]], writes=[rfb[PP]])
                P.op("act", I("activation", out=F(PI), in_=F(CS), func=AF.Exp, scale=C0), reads=[rfb[CS]], writes=[rfb[PI]])
                P.op("dve", I("tensor_tensor", out=F(PX), in0=F(CS), in1=F(SG), op=ALU.subtract), reads=[rfb[CS], rfb[SG]], writes=[rfb[PX]])
                P.op("act", I("activation", out=F(PX), in_=F(PX), func=AF.Exp, scale=-C0), reads=[rfb[PX]], writes=[rfb[PX]])
                ARv = AR[0:64, 0:2 * NCOL].rearrange("p (h j a c) -> p h j a c", h=HG, j=nch, a=2)
                v4 = lambda ap: ap.rearrange("p (h j c) -> p h j c", h=HG, j=nch)
                P.op("dve", I("scalar_tensor_tensor", out=ARv[:, :, :, 0, :], in0=v4(F(KK)), scalar=-1.0, in1=v4(F(PX)), op0=ALU.mult, op1=ALU.mult),
                     reads=[rfb[KK], rfb[PX]], writes=[rAR])
                P.op("dve", I("tensor_tensor", out=ARv[:, :, :, 1, :], in0=v4(F(R32)), in1=v4(F(PP)), op=ALU.mult),
                     reads=[rfb[R32], rfb[PP]], writes=[rAR])
                ppv = v4(F(PP))
                a_ = ppv.ap
                PCb = bass.AP(ppv.tensor, ppv.offset + (C - 1), [list(a_[0]), list(a_[1]), list(a_[2]), [0, C]])
                P.op("dve", I("tensor_tensor", out=F(SG), in0=F(KK), in1=F(A32), op=ALU.mult), reads=[rfb[KK], rfb[A32]], writes=[rfb[SG]])
                P.op("dve", I("tensor_tensor", out=F(SG), in0=F(SG), in1=F(PI), op=ALU.mult), reads=[rfb[SG], rfb[PI]], writes=[rfb[SG]])
                P.op("pool", I("tensor_copy", out=BT[0:64, 0:NCOL], in_=F(SG)), reads=[rfb[SG]], writes=[rBT])
                P.op("dve", I("tensor_tensor", out=v4(BH[0:64, 0:NCOL]), in0=v4(F(SG)), in1=PCb, op=ALU.mult), reads=[rfb[SG], rfb[PP]], writes=[rBH])
                P.op("dve", I("tensor_tensor", out=F(CS), in0=F(KP), in1=F(PI), op=ALU.mult), reads=[rfb[KP], rfb[PI]], writes=[rfb[CS]])
                P.op("pool", I("tensor_copy", out=KT[0:64, 0:NCOL], in_=F(CS)), reads=[rfb[CS]], writes=[rKT])
                P.op("dve", I("tensor_tensor", out=v4(KH[0:64, 0:NCOL]), in0=v4(F(CS)), in1=PCb, op=ALU.mult), reads=[rfb[CS], rfb[PP]], writes=[rKH])
                for hh in range(HG):
                    P.op("dve", I("scalar_tensor_tensor", out=F(K32)[:, hh * TT:(hh + 1) * TT], in0=F(R32)[:, hh * TT:(hh + 1) * TT],
                                                                      scalar=self.v64("r_k", li, h0 + hh), in1=F(KP)[:, hh * TT:(hh + 1) * TT],
                                                                      op0=ALU.mult, op1=ALU.mult),
                         reads=[rfb[R32], rfb[KP], self.rv64], writes=[rfb[K32]])
                P.op("pe", I("matmul", self.ps[psD][0:64, 0:NCOL], lhsT=ones64, rhs=F(K32), start=True, stop=True),
                     reads=[self.rcst, rfb[K32]], writes=[self.rps[psD]])
                P.op("dve", I("tensor_tensor", out=F(K32), in0=self.ps[psD][0:64, 0:NCOL], in1=F(V32), op=ALU.mult),
                     reads=[self.rps[psD], rfb[V32]], writes=[rfb[K32]])
                VB = fb[KP][0:64, 0:NCOL // 2].bitcast(BF16)
                P.op("pool", I("tensor_copy", out=VB, in_=F(V32)), reads=[rfb[V32], rKT, rKH], writes=[rfb[KP]])
                NM = HG * nch
                Mbv = Mb[0:C, 0:NM * 2 * C].rearrange("p (m c) -> p m c", m=NM)
                Mkv = Mk[0:C, 0:NM * 2 * C].rearrange("p (m c) -> p m c", m=NM)
                nv = [nbuf[i][0:C, 0:NM * C].rearrange("p (m c) -> p m c", m=NM) for i in range(6)]
                BTv = BT[0:64, 0:NCOL].rearrange("p (m c) -> p m c", m=NM)
                KTv = KT[0:64, 0:NCOL].rearrange("p (m c) -> p m c", m=NM)
                BHv = BH[0:64, 0:NCOL].rearrange("p (m c) -> p m c", m=NM)
                KHv = KH[0:64, 0:NCOL].rearrange("p (m c) -> p m c", m=NM)
                VBv = VB.rearrange("p (m c) -> p m c", m=NM)
                ARm = AR[0:64, 0:2 * NCOL].rearrange("p (m c) -> p m c", m=NM)
                pMb = self.ps[psA][0:C, 0:NM * 2 * C].rearrange("p (m c) -> p m c", m=NM)
                pMk = self.ps[psB][0:C, 0:NM * 2 * C].rearrange("p (m c) -> p m c", m=NM)
                pNT = self.ps[psC][0:C, 0:NM * C].rearrange("p (m c) -> p m c", m=NM)
                for m in range(NM):
                    P.op("pe", I("matmul", pMb[:, m, :], lhsT=BTv[:, m, :], rhs=ARm[:, m, :], start=(m == 0), stop=True, skip_group_check=True),
                         reads=[rBT, rAR], writes=[self.rps[psA]])
                for m in range(NM):
                    P.op("pe", I("matmul", pMk[:, m, :], lhsT=KTv[:, m, :], rhs=ARm[:, m, :], start=(m == 0), stop=True, skip_group_check=True),
                         reads=[rKT, rAR], writes=[self.rps[psB]])
                for m in range(NM):
                    P.op("pe", I("matmul", pNT[:, m, :], lhsT=ARm[:, m, 0:C], rhs=BTv[:, m, :], start=(m == 0), stop=True, skip_group_check=True),
                         reads=[rBT, rAR], writes=[self.rps[psC]])
                bc = lambda mk, w: bass.AP(mk.tensor, mk.offset, [list(mk.ap[0]), [0, NM], [1, w]])
                P.op("dve", I("tensor_tensor", out=Mbv, in0=pMb, in1=bc(m_si, 2 * C), op=ALU.mult), reads=[self.rps[psA], self.rc64], writes=[rMb])
                P.op("dve", I("tensor_tensor", out=Mkv, in0=pMk, in1=bc(m_si, 2 * C), op=ALU.mult), reads=[self.rps[psB], self.rc64], writes=[rMk])
                N_, NT_, N2_, N2T_, T_, Tt_ = range(6)
                P.op("dve", I("tensor_tensor", out=nv[NT_], in0=pNT, in1=bc(m_lo, C), op=ALU.mult), reads=[self.rps[psC], self.rc64], writes=[rnb[NT_]])
                P.op("pool", I("tensor_copy", out=nv[N_], in_=Mbv[:, :, 0:C]), reads=[rMb], writes=[rnb[N_]])
                P.op("pool", I("tensor_tensor", out=nv[T_], in0=Mbv[:, :, 0:C], in1=bc(m_ey, C), op=ALU.add), reads=[rMb, self.rc64], writes=[rnb[T_]])
                P.op("pool", I("tensor_tensor", out=nv[Tt_], in0=nv[NT_], in1=bc(m_ey, C), op=ALU.add), reads=[rnb[NT_], self.rc64], writes=[rnb[Tt_]])
                nsteps = {64: 5, 4: 1}[C]
                cur, curT, nxt, nxtT = N_, NT_, N2_, N2T_
                pX = [self.ps[i][0:C, 0:NM * C].rearrange("p (m c) -> p m c", m=NM) for i in (psA, psB, psC, psD)]
                for s_ in range(nsteps):
                    last = (s_ == nsteps - 1)
                    for m in range(NM):
                        P.op("pe", I("matmul", pX[0][:, m, :], lhsT=nv[curT][:, m, :], rhs=nv[cur][:, m, :], start=(m == 0), stop=True, skip_group_check=True),
                             reads=[rnb[cur], rnb[curT]], writes=[self.rps[psA]])
                    P.op("act", I("copy", out=nv[nxt], in_=pX[0]), reads=[self.rps[psA]], writes=[rnb[nxt]])
                    if not last:
                        for m in range(NM):
                            P.op("pe", I("matmul", pX[1][:, m, :], lhsT=nv[cur][:, m, :], rhs=nv[curT][:, m, :], start=(m == 0), stop=True, skip_group_check=True),
                                 reads=[rnb[cur], rnb[curT]], writes=[self.rps[psB]])
                        P.op("act", I("copy", out=nv[nxtT], in_=pX[1]), reads=[self.rps[psB]], writes=[rnb[nxtT]])
                    for m in range(NM):
                        P.op("pe", I("matmul", pX[2][:, m, :], lhsT=nv[Tt_][:, m, :], rhs=nv[nxt][:, m, :], start=(m == 0), stop=True, skip_group_check=True),
                             reads=[rnb[Tt_], rnb[nxt]], writes=[self.rps[psC]])
                    if not last:
                        for m in range(NM):
                            P.op("pe", I("matmul", pX[3][:, m, :], lhsT=nv[nxt][:, m, :], rhs=nv[Tt_][:, m, :], start=(m == 0), stop=True, skip_group_check=True),
                                 reads=[rnb[Tt_], rnb[nxt]], writes=[self.rps[psD]])
                    P.op("dve", I("tensor_tensor", out=nv[T_], in0=pX[2], in1=nv[T_], op=ALU.add), reads=[self.rps[psC], rnb[T_]], writes=[rnb[T_]])
                    if not last:
                        P.op("dve", I("tensor_tensor", out=nv[Tt_], in0=pX[3], in1=nv[Tt_], op=ALU.add), reads=[self.rps[psD], rnb[Tt_]], writes=[rnb[Tt_]])
                    cur, curT, nxt, nxtT = nxt, nxtT, cur, curT
                pY = self.ps[4][0:64, 0:NCOL].rearrange("p (h j c) -> p h j c", h=HG, j=nch)
                for j in range(nch):
                    mm_ = lambda hh: hh * nch + j
                    pT = self.ps[5][0:C, 0:HG * 3 * 32].bitcast(BF16).rearrange("p (h a k) -> p h a k", h=HG, a=3)
                    bkvv = bkv[0:C, 0:HG * 3 * 64].rearrange("p (h a k) -> p h a k", h=HG, a=3)
                    for hh in range(HG):
                        for a, srcv, rsrc in ((0, BHv, rBH), (1, KHv, rKH), (2, VBv, rfb[KP])):
                            P.op("pe", I("transpose", out=pT[:, hh, a, :], in_=srcv[:, mm_(hh), :], identity=identb[0:64, 0:64]),
                                 reads=[rsrc, self.rcstb], writes=[self.rps[5]])
                    P.op("act", I("copy", out=bkvv, in_=pT), reads=[self.rps[5]], writes=[rbkv])
                    if samp:
                        b = j
                        s = j % 2
                        snat = sst[s][0:64, :]
                        P.op("sp", I("dma_start", out=snat.rearrange("p (h k) -> p h k", h=HG),
                                                                        in_=self.d_swkv[li, b, h0:h0 + HG].rearrange("h v k -> v h k")),
                             writes=[rsst[s]], dma=rsst[s])
                        snb = s32s[0:64, 0:128].bitcast(BF16)
                        P.op("pool", I("tensor_copy", out=snb, in_=snat), reads=[rsst[s]], writes=[rs32s])
                        pS = self.ps[6][0:64, 0:128].bitcast(BF16).rearrange("p (h v) -> p h v", h=HG)
                        for hh in range(HG):
                            P.op("pe", I("transpose", out=pS[:, hh, :], in_=snb[:, hh * 64:(hh + 1) * 64], identity=identb[0:64, 0:64]),
                                 reads=[rs32s, self.rcstb], writes=[self.rps[6]])
                        sbcur = sbs[0:64, 0:256].rearrange("p (h v) -> p h v", h=HG)
                        P.op("act", I("copy", out=sbcur, in_=pS), reads=[self.rps[6]], writes=[rsbs])
                        rsb_cur = rsbs
                    else:
                        sbcur = Sb[0:64, h0 * 64:(h0 + HG) * 64].rearrange("p (h v) -> p h v", h=HG)
                        rsb_cur = rSb
                    pW = self.ps[6][0:C, 256:256 + HG * 64].rearrange("p (h v) -> p h v", h=HG)
                    for hh in range(HG):
                        P.op("pe", I("matmul", pW[:, hh, :], lhsT=ARm[:, mm_(hh), 0:C], rhs=sbcur[:, hh, :],
                                                                        start=(hh == 0), stop=False, skip_group_check=True),
                             reads=[rAR, rsb_cur], writes=[self.rps[6]])
                        P.op("pe", I("matmul", pW[:, hh, :], lhsT=Mkv[:, mm_(hh), 0:C], rhs=bkvv[:, hh, 2, :],
                                                           start=False, stop=True, skip_group_check=True),
                             reads=[rMk, rbkv], writes=[self.rps[6]])
                    Wtv = Wt[0:C, 0:HG * 64].rearrange("p (h v) -> p h v", h=HG)
                    Utv = Ut[0:C, 0:HG * 64].rearrange("p (h v) -> p h v", h=HG)
                    P.op("act", I("copy", out=Wtv, in_=pW), reads=[self.rps[6]], writes=[rWt])
                    pU = self.ps[7][0:C, 0:HG * 64].rearrange("p (h v) -> p h v", h=HG)
                    for hh in range(HG):
                        P.op("pe", I("matmul", pU[:, hh, :], lhsT=nv[T_][:, mm_(hh), :], rhs=Wtv[:, hh, :], start=(hh == 0), stop=True, skip_group_check=True),
                             reads=[rnb[T_], rWt], writes=[self.rps[7]])
                    P.op("act", I("copy", out=Utv, in_=pU), reads=[self.rps[7]], writes=[rUt])
                    for hh in range(HG):
                        P.op("pe", I("matmul", pY[:, hh, j, :], lhsT=sbcur[:, hh, :], rhs=ARm[:, mm_(hh), C:2 * C],
                                                                        start=(hh == 0 and j == 0), stop=False, skip_group_check=True),
                             reads=[rAR, rsb_cur], writes=[self.rps[4]])
                        P.op("pe", I("matmul", pY[:, hh, j, :], lhsT=Utv[:, hh, :], rhs=Mbv[:, mm_(hh), C:2 * C],
                                                           start=False, stop=False, skip_group_check=True), reads=[rUt, rMb], writes=[self.rps[4]])
                        P.op("pe", I("matmul", pY[:, hh, j, :], lhsT=bkvv[:, hh, 2, :], rhs=Mkv[:, mm_(hh), C:2 * C],
                                                           start=False, stop=True, skip_group_check=True), reads=[rbkv, rMk], writes=[self.rps[4]])
                    pD = self.ps[7][0:64, 256:256 + HG * 64].rearrange("p (h v) -> p h v", h=HG)
                    if not samp:
                        for hh in range(HG):
                            P.op("pe", I("matmul", pD[:, hh, :], lhsT=bkvv[:, hh, 0, :], rhs=Utv[:, hh, :], start=(hh == 0), stop=False, skip_group_check=True),
                                 reads=[rbkv, rUt], writes=[self.rps[7]])
                            P.op("pe", I("matmul", pD[:, hh, :], lhsT=bkvv[:, hh, 1, :], rhs=bkvv[:, hh, 2, :], start=False, stop=True, skip_group_check=True),
                                 reads=[rbkv], writes=[self.rps[7]])
                        for hh in range(HG):
                            h = h0 + hh
                            pc = F(PP)[:, hh * TT + (j + 1) * C - 1:hh * TT + (j + 1) * C]
                            P.op("dve", I("scalar_tensor_tensor",
                                out=S32[0:64, h * 64:(h + 1) * 64], in0=S32[0:64, h * 64:(h + 1) * 64], scalar=pc, in1=pD[:, hh, :],
                                op0=ALU.mult, op1=ALU.add), reads=[rS32, rfb[PP], self.rps[7]], writes=[rS32])
                        P.op("pool", I("tensor_copy", out=Sb[0:64, h0 * 64:(h0 + HG) * 64], in_=S32[0:64, h0 * 64:(h0 + HG) * 64]),
                             reads=[rS32], writes=[rSb])
                    else:
                        for hh in range(HG):
                            P.op("pe", I("matmul", pD[:, hh, :], lhsT=Utv[:, hh, :], rhs=bkvv[:, hh, 0, :], start=(hh == 0), stop=False, skip_group_check=True),
                                 reads=[rbkv, rUt], writes=[self.rps[7]])
                            P.op("pe", I("matmul", pD[:, hh, :], lhsT=bkvv[:, hh, 2, :], rhs=bkvv[:, hh, 1, :], start=False, stop=True, skip_group_check=True),
                                 reads=[rbkv], writes=[self.rps[7]])
                        dg = fb[CS][0:64, 0:HG * 64]
                        for hh in range(HG):
                            pc = F(PP)[:, hh * TT + (j + 1) * C - 1:hh * TT + (j + 1) * C]
                            P.op("dve", I("tensor_scalar", out=dg[:, hh * 64:(hh + 1) * 64], in0=ident[0:64, 0:64], scalar1=pc, scalar2=None, op0=ALU.mult),
                                 reads=[rfb[PP], self.rcst, rKT, rKH], writes=[rfb[CS]])
                        pR = self.ps[6][0:64, 0:256]
                        P.op("pe", I("matmul", self.ps[5][0:64, 256:512], lhsT=ones64, rhs=dg, start=True, stop=True, skip_group_check=True),
                             reads=[rfb[CS], self.rcst], writes=[self.rps[5]])
                        so = sost[s][0:64, :]
                        P.op("dve", I("tensor_tensor", out=so, in0=snat, in1=self.ps[5][0:64, 256:512], op=ALU.mult),
                             reads=[rsst[s], self.rps[5]], writes=[rsost[s]])
                        P.op("dve", I("tensor_tensor", out=so, in0=so, in1=self.ps[7][0:64, 256:512], op=ALU.add),
                             reads=[self.rps[7], rsost[s]], writes=[rsost[s]])
                        P.op("sp", I("dma_start", out=self.d_wkv_s[li, b, h0:h0 + HG].rearrange("h v k -> v h k"),
                                                                    in_=so.rearrange("p (h k) -> p h k", h=HG)), reads=[rsost[s]], dma=rsost[s])
                        if rsost[s] not in self.out_regs:
                            self.out_regs.append(rsost[s])
                Y32, YSQ, MU, RS = R32, SG, PI, PX
                P.op("act", I("copy", out=F(Y32), in_=self.ps[4][0:64, 0:NCOL]), reads=[self.rps[4], rAR], writes=[rfb[Y32]])
                P.op("act", I("activation", out=F(YSQ), in_=F(Y32), func=AF.Square), reads=[rfb[Y32], rBT, rBH], writes=[rfb[YSQ]])
                P.op("pe", I("matmul", self.ps[psA][0:64, 0:NCOL], lhsT=ones64, rhs=F(Y32), start=True, stop=True), reads=[self.rcst, rfb[Y32]], writes=[self.rps[psA]])
                P.op("pe", I("matmul", self.ps[psB][0:64, 0:NCOL], lhsT=ones64, rhs=F(YSQ), start=True, stop=True), reads=[self.rcst, rfb[YSQ]], writes=[self.rps[psB]])
                P.op("dve", I("tensor_scalar", out=F(MU), in0=self.ps[psA][0:64, 0:NCOL], scalar1=1.0 / DH, scalar2=None, op0=ALU.mult),
                     reads=[self.rps[psA]], writes=[rfb[MU]])
                P.op("dve", I("tensor_tensor", out=F(RS), in0=F(MU), in1=F(MU), op=ALU.mult), reads=[rfb[MU]], writes=[rfb[RS]])
                P.op("dve", I("scalar_tensor_tensor", out=F(RS), in0=self.ps[psB][0:64, 0:NCOL], scalar=1.0 / DH, in1=F(RS), op0=ALU.mult, op1=ALU.subtract),
                     reads=[self.rps[psB], rfb[RS]], writes=[rfb[RS]])
                P.op("act", I("activation", out=F(RS), in_=F(RS), func=AF.Ln, bias=self.gneps[0:64, 0:1]), reads=[rfb[RS], self.rtiny], writes=[rfb[RS]])
                P.op("act", I("activation", out=F(RS), in_=F(RS), func=AF.Exp, scale=-0.5), reads=[rfb[RS]], writes=[rfb[RS]])
                P.op("dve", I("tensor_tensor", out=F(Y32), in0=F(Y32), in1=F(MU), op=ALU.subtract), reads=[rfb[Y32], rfb[MU]], writes=[rfb[Y32]])
                P.op("dve", I("tensor_tensor", out=F(Y32), in0=F(Y32), in1=F(RS), op=ALU.mult), reads=[rfb[Y32], rfb[RS]], writes=[rfb[Y32]])
                for hh in range(HG):
                    sl = slice(hh * TT, (hh + 1) * TT)
                    P.op("act", I("activation", out=F(Y32)[:, sl], in_=F(Y32)[:, sl], func=AF.Identity,
                                                                   bias=self.v64("gn_b", li, h0 + hh), scale=self.v64("gn_g", li, h0 + hh)),
                         reads=[rfb[Y32], self.rv64], writes=[rfb[Y32]])
                P.op("dve", I("tensor_tensor", out=F(Y32), in0=F(Y32), in1=F(K32), op=ALU.add), reads=[rfb[Y32], rfb[K32]], writes=[rfb[Y32]])
                for hh in range(HG):
                    h = h0 + hh
                    dst, rd = (ygs, rygs) if h % 2 == 0 else (ygo, rygo)
                    sl = slice(hh * TT, (hh + 1) * TT)
                    P.op("dve", I("tensor_tensor", out=dst[0:64, h // 2, :], in0=F(Y32)[:, sl], in1=F(G32)[:, sl], op=ALU.mult),
                         reads=[rfb[Y32], rfb[G32]], writes=[rd])

            for hg in range(H // HG):
                group(hg)
            P.op("sp", I("dma_start", out=ygs[64:128, :, :], in_=ygo[0:64, :, :]), reads=[rygo], writes=[rygs], dma=rygs)
            for oc in range(NCH):
                pb = 1 + (oc % 2)
                for kc in range(NCH):
                    P.op("pe", I("matmul", self.ps[pb][:, 0:TT], lhsT=wo[:, kc, oc * 128:(oc + 1) * 128], rhs=ygs[:, kc, :],
                                                                    start=(kc == 0), stop=(kc == NCH - 1)),
                         reads=[rw, rygs], writes=[self.rps[pb]])
                P.op("dve", I("scalar_tensor_tensor",
                    out=self.xT[:, oc, t0:t0 + TT], in0=self.xT[:, oc, t0:t0 + TT], scalar=ALPHA, in1=self.ps[pb][:, 0:TT],
                    op0=ALU.mult, op1=ALU.add), reads=[self.rps[pb], rxt, rxlast, rxx] + rmix, writes=[rxt])
        for hq in range(4):
            pb = 3 + (hq % 2)
            for hh in range(4):
                h = hq * 4 + hh
                P.op("pe", I("transpose", out=self.ps[pb][0:64, hh * 64:(hh + 1) * 64], in_=S32[0:64, h * 64:(h + 1) * 64],
                                                                  identity=ident[0:64, 0:64]), reads=[rS32, self.rcst], writes=[self.rps[pb]])
            so = sost[hq % 2]
            P.op("act", I("copy", out=so[0:64, :], in_=self.ps[pb][0:64, 0:256]), reads=[self.rps[pb]], writes=[rsost[hq % 2]])
            P.op("sp", I("dma_start", out=self.d_wkv_p[li, hq * 4:hq * 4 + 4].rearrange("h v k -> v h k"),
                                                          in_=so[0:64, :].rearrange("p (h k) -> p h k", h=4)), reads=[rsost[hq % 2]], dma=rsost[hq % 2])
            if rsost[hq % 2] not in self.out_regs:
                self.out_regs.append(rsost[hq % 2])

    def load_w(self, alloc, src, stage, rstage, rw):
        P = self.P
        wb_ = alloc(4096, BF16).rearrange("p (k n) -> p k n", k=NCH)
        for half in range(2):
            P.op("sp", I("dma_start", out=stage.rearrange("p (k n) -> p k n", k=4), in_=src[:, half * 4:half * 4 + 4, :]),
                 writes=[rstage], dma=rstage)
            if half == 0:
                P.op("pool", I("tensor_copy", out=wb_[:, 0:4, :], in_=stage.rearrange("p (k n) -> p k n", k=4)), reads=[rstage], writes=[rw])
            else:
                P.op("act", I("copy", out=wb_[:, 4:8, :], in_=stage.rearrange("p (k n) -> p k n", k=4)), reads=[rstage], writes=[rw])
        return wb_

    def kv_proj(self):
        cfg, P = self.cfg, self.P
        SEQ, NT = cfg.seq, cfg.nt
        P.barrier()
        alloc = self.mk_alloc_arena()
        stage = alloc(4096)
        rstage, rw = Reg("kvstage"), Reg("kvw")
        wk = self.load_w(alloc, self.d_wkv2[0], stage, rstage, rw)
        wv = self.load_w(alloc, self.d_wkv2[1], stage, rstage, rw)
        st32 = [alloc(1024) for _ in range(2)]
        rst32 = [Reg("kvst%d" % i) for i in range(2)]
        vb16 = [alloc(512, BF16) for _ in range(2)]
        rvb16 = [Reg("vb16%d" % i) for i in range(2)]
        ktb = [alloc(256, BF16) for _ in range(2)]
        rktb = [Reg("ktb%d" % i) for i in range(2)]
        ttiles = [(t0, 128) for t0 in range(0, SEQ, 128)] + [(SEQ, 128)]
        ctr = 0
        for (t0, n) in ttiles:
            ti = self.tile_of(t0)
            samp = t0 >= SEQ
            for which, w_, dp, ds in ((0, wk, self.d_kp, self.d_ks), (1, wv, self.d_vp, self.d_vs)):
                s = ctr % 2
                ctr += 1
                for half in range(2):
                    pb = (ctr * 2 + half) % 4
                    for kc in range(NCH):
                        P.op("pe", I("matmul", self.ps[pb][:, :], lhsT=self.xb[:, kc, t0:t0 + 128], rhs=w_[:, kc, half * 512:(half + 1) * 512],
                                     start=(kc == 0), stop=(kc == NCH - 1)), reads=[rw, self.rb[ti]], writes=[self.rps[pb]])
                    P.op("act", I("copy", out=st32[s][:, half * 512:(half + 1) * 512], in_=self.ps[pb][:, :]), reads=[self.rps[pb]], writes=[rst32[s]])
                dst = ds[:, :] if samp else dp[t0:t0 + 128, :]
                P.op("sp", I("dma_start", out=dst, in_=st32[s]), reads=[rst32[s]], dma=rst32[s])
                if rst32[s] not in self.out_regs:
                    self.out_regs.append(rst32[s])
                if which == 1:
                    P.op("dve", I("tensor_copy", out=vb16[s], in_=st32[s]), reads=[rst32[s]], writes=[rvb16[s]])
                    P.op("sp", I("dma_start", out=self.d_VB[t0:t0 + 128, :], in_=vb16[s]), reads=[rvb16[s]], dma=rvb16[s])
        ctr = 0
        for oc in range(NCH):
            for ti, (t0, n) in enumerate(self.tiles):
                s = ctr % 2
                pb = 4 + ctr % 2
                ctr += 1
                for kc in range(NCH):
                    P.op("pe", I("matmul", self.ps[pb][:, 0:n], lhsT=wk[:, kc, oc * 128:(oc + 1) * 128], rhs=self.xb[:, kc, t0:t0 + n],
                                 start=(kc == 0), stop=(kc == NCH - 1)), reads=[rw, self.rb[ti]], writes=[self.rps[pb]])
                P.op("act", I("copy", out=ktb[s][:, 0:n], in_=self.ps[pb][:, 0:n]), reads=[self.rps[pb]], writes=[rktb[s]])
                P.op("sp", I("dma_start", out=self.d_KT[oc, :, t0:t0 + n], in_=ktb[s][:, 0:n]), reads=[rktb[s]], dma=rktb[s])
        P.barrier()

    def mk_alloc_arena(self):
        pools = [[self.arena, 0, self.ARENA]]

        def alloc(words, dt=F32):
            for p in pools:
                if p[1] + words <= p[2]:
                    a = p[0][:, p[1]:p[1] + words]
                    p[1] += words
                    return a if dt == F32 else a.bitcast(dt)
            raise RuntimeError("arena overflow (%d words)" % words)
        alloc.pools = pools
        return alloc

    def mixer_sb(self, j):
        cfg, P = self.cfg, self.P
        SEQ, NT, NPG, SBN = cfg.seq, cfg.nt, cfg.npg, cfg.sb
        NQB = SEQ // 128
        P.barrier()
        alloc = self.mk_alloc_arena()
        stage = alloc(4096)
        rstage, rw = Reg("sbstage"), Reg("sbw")
        wq = self.load_w(alloc, self.d_wq[j], stage, rstage, rw)
        qT = alloc(4 * NT, BF16).rearrange("p (c t) -> p c t", c=NCH)
        rqT = Reg("qT")
        oT = self.xb
        roT = Reg("oT")
        bias = self.sbb[:, j * H:(j + 1) * H]
        ident, identb = self.ident, self.cstb[:, 0:128]
        ones = self.ones
        tri = self.cst[:, 384:512]
        dmask = self.cst[:, 512:640]
        ctr = 0
        for oc in range(NCH):
            for ti, (t0, n) in enumerate(self.tiles):
                pb = ctr % 2
                ctr += 1
                for kc in range(NCH):
                    P.op("pe", I("matmul", self.ps[pb][:, 0:n], lhsT=wq[:, kc, oc * 128:(oc + 1) * 128], rhs=self.xb[:, kc, t0:t0 + n],
                                 start=(kc == 0), stop=(kc == NCH - 1)), reads=[rw, self.rb[ti]], writes=[self.rps[pb]])
                P.op("act", I("activation", out=qT[:, oc, t0:t0 + n], in_=self.ps[pb][:, 0:n], func=AF.Identity, scale=SB_SCALE),
                     reads=[self.rps[pb]], writes=[rqT])
        P.barrier()
        alloc.pools.append([stage, 0, 4096])
        pmark = [q[1] for q in alloc.pools]
        GB = 4
        Lt = [alloc(512) for _ in range(2)]
        lbt = [alloc(512) for _ in range(2)]
        Lbt = [alloc(256, BF16) for _ in range(2)]
        Wtt = [alloc(256, BF16) for _ in range(2)]
        rLt, rlbt, rLbt, rWtt = ([Reg(n + "0"), Reg(n + "1")] for n in ("Lt", "lbt", "Lbt", "Wtt"))
        csum = alloc(128)
        rcsum = Reg("csum")
        trib, onesb = self.cstb[:, 384:512], self.cstb[:, 128:256]
        KTp = alloc(SEQ // 2, BF16)
        VZ = [alloc(NQB * 64, BF16).rearrange("p (k d) -> p k d", k=NQB) for e in range(2)]
        rKTp, rVZ = Reg("KTp"), Reg("VZ")
        for e in range(2):
            P.op("pool", I("memset", VZ[e], 0.0), writes=[rVZ])
        blk = 0
        import os
        for c in range(NCH if not os.environ.get("SKIP_SBP") else 0):
            P.op("sp", I("dma_start", out=KTp, in_=self.d_KT[c, :, 0:SEQ]), writes=[rKTp], dma=rKTp)
            for e in range(2):
                P.op("sp", I("dma_start", out=VZ[e][:, :, e * 64:(e + 1) * 64],
                             in_=self.d_VB[0:SEQ, c * 128 + e * 64:c * 128 + (e + 1) * 64].rearrange("(k p) d -> p k d", p=128)),
                     writes=[rVZ], dma=rVZ)
            for qb in range(NQB):
                po = 6 + (qb % 2)
                glist = []
                for e in range(2):
                    h = 2 * c + e
                    bh = bias[:, h:h + 1]
                    qv = qT[e * 64:(e + 1) * 64, c, qb * 128:(qb + 1) * 128]
                    kbs = list(range(qb, -1, -1))
                    groups = [kbs[i:i + GB] for i in range(0, len(kbs), GB)]
                    for gi, grp in enumerate(groups):
                        G = len(grp)
                        glist.append((e, gi, grp, len(groups)))
                def ctx(item, k):
                    e, gi, grp, ng = item
                    G = len(grp)
                    w = k % 2
                    pz, pc, pt_ = (k % 2), 2 + (k % 2), 4 + (k % 2)
                    h = 2 * c + e
                    bh = bias[:, h:h + 1]
                    qv = qT[e * 64:(e + 1) * 64, c, qb * 128:(qb + 1) * 128]
                    L, lb, Lb, W = Lt[w][:, 0:G * 128], lbt[w][:, 0:G * 128], Lbt[w][:, 0:G * 128], Wtt[w][:, 0:G * 128]
                    rL, rlb, rLb, rW = rLt[w], rlbt[w], rLbt[w], rWtt[w]
                    pzv, pcv = self.ps[pz][:, 0:G * 128], self.ps[pc][:, 0:G * 128]
                    diag = (gi == 0)
                    lastg = (gi == ng - 1)
                    return locals()

                def stage1(item, k):
                    globals_ = ctx(item, k)
                    e, gi, grp, G, w, pz, pc, pt_, h, bh, qv, L, lb, Lb, W, rL, rlb, rLb, rW, pzv, pcv, diag, lastg = (
                        globals_[n] for n in ("e", "gi", "grp", "G", "w", "pz", "pc", "pt_", "h", "bh", "qv", "L", "lb", "Lb", "W", "rL", "rlb", "rLb", "rW", "pzv", "pcv", "diag", "lastg"))
                    for i, kb in enumerate(grp):
                        P.op("pe", I("matmul", self.ps[pz][:, i * 128:(i + 1) * 128], lhsT=KTp[e * 64:(e + 1) * 64, kb * 128:(kb + 1) * 128], rhs=qv,
                                     start=(i == 0), stop=True, skip_group_check=True), reads=[rKTp, rqT], writes=[self.rps[pz]])
                    P.op("act", I("activation", out=L, in_=pzv, func=AF.Exp, bias=bh), reads=[self.rps[pz], self.rsbb], writes=[rL])
                    P.op("act", I("activation", out=L, in_=L, func=AF.Ln, bias=self.onec[:, 0:1]), reads=[rL, self.rtiny], writes=[rL])
                    P.op("dve", I("scalar_tensor_tensor", out=lb, in0=pzv, scalar=bh, in1=L, op0=ALU.add, op1=ALU.subtract),
                         reads=[self.rps[pz], rL, self.rsbb], writes=[rlb])
                    if diag:
                        P.op("dve", I("tensor_tensor", out=L[:, 0:128], in0=L[:, 0:128], in1=dmask, op=ALU.mult), reads=[rL, self.rcst], writes=[rL])
                    P.op("pool", I("tensor_copy", out=Lb, in_=L), reads=[rL], writes=[rLb])

                def stage2(item, k, first, lastitem):
                    globals_ = ctx(item, k)
                    e, gi, grp, G, w, pz, pc, pt_, h, bh, qv, L, lb, Lb, W, rL, rlb, rLb, rW, pzv, pcv, diag, lastg = (
                        globals_[n] for n in ("e", "gi", "grp", "G", "w", "pz", "pc", "pt_", "h", "bh", "qv", "L", "lb", "Lb", "W", "rL", "rlb", "rLb", "rW", "pzv", "pcv", "diag", "lastg"))
                    firstmm = True
                    for i in range(G):
                        P.op("pe", I("matmul", self.ps[pc][:, i * 128:(i + 1) * 128], lhsT=trib, rhs=Lb[:, i * 128:(i + 1) * 128],
                                     start=firstmm, stop=(i == 0), skip_group_check=True), reads=[self.rcstb, rLb], writes=[self.rps[pc]])
                        firstmm = False
                        for i2 in range(i):
                            P.op("pe", I("matmul", self.ps[pc][:, i * 128:(i + 1) * 128], lhsT=onesb, rhs=Lb[:, i2 * 128:(i2 + 1) * 128],
                                         start=False, stop=(i2 == i - 1), skip_group_check=True), reads=[self.rcstb, rLb], writes=[self.rps[pc]])
                    if not lastg:
                        for i in range(G):
                            P.op("pe", I("matmul", self.ps[pt_][:, 0:128], lhsT=onesb, rhs=Lb[:, i * 128:(i + 1) * 128],
                                         start=(i == 0), stop=(i == G - 1)), reads=[self.rcstb, rLb], writes=[self.rps[pt_]])
                    P.op("dve", I("tensor_tensor", out=lb, in0=lb, in1=pcv, op=ALU.subtract), reads=[rlb, self.rps[pc]], writes=[rlb])
                    if not diag:
                        cb = bass.AP(csum.tensor, csum.offset, [list(csum.ap[0]), [0, G], [1, 128]])
                        P.op("dve", I("tensor_tensor", out=lb.rearrange("p (g q) -> p g q", g=G), in0=lb.rearrange("p (g q) -> p g q", g=G), in1=cb,
                                      op=ALU.subtract), reads=[rlb, rcsum], writes=[rlb])
                    if not lastg:
                        if diag:
                            P.op("dve", I("tensor_copy", out=csum, in_=self.ps[pt_][:, 0:128]), reads=[self.rps[pt_]], writes=[rcsum])
                        else:
                            P.op("dve", I("tensor_tensor", out=csum, in0=csum, in1=self.ps[pt_][:, 0:128], op=ALU.add), reads=[self.rps[pt_], rcsum], writes=[rcsum])
                    P.op("act", I("activation", out=W, in_=lb, func=AF.Exp), reads=[rlb], writes=[rW])
                    if diag:
                        P.op("pool", I("tensor_tensor", out=W[:, 0:128], in0=W[:, 0:128], in1=dmask, op=ALU.mult), reads=[rW, self.rcst], writes=[rW])
                    for i, kb in enumerate(grp):
                        last = (lastitem and i == G - 1)
                        P.op("pe", I("matmul", self.ps[po][:, 0:128], lhsT=VZ[e][:, kb, :], rhs=W[:, i * 128:(i + 1) * 128], start=(first and i == 0), stop=last),
                             reads=[rVZ, rW], writes=[self.rps[po]])

                for k_, item in enumerate(glist):
                    kk_ = blk + k_
                    if k_ == 0:
                        stage1(item, kk_)
                    if k_ + 1 < len(glist):
                        stage1(glist[k_ + 1], kk_ + 1)
                    stage2(item, kk_, k_ == 0, k_ == len(glist) - 1)
                blk += len(glist)
                P.op("act", I("copy", out=oT[:, c, qb * 128:(qb + 1) * 128], in_=self.ps[po][:, 0:128]), reads=[self.rps[po]], writes=[roT])
        P.barrier()
        for q, m in zip(alloc.pools, pmark):
            q[1] = m
        wk_ = {nm: [alloc(64) for _ in range(2)] for nm in ("E", "L", "lb", "Ls", "t")}
        rwk = {nm: [Reg(nm + "0"), Reg(nm + "1")] for nm in wk_}
        if not os.environ.get("SKIP_SBS"):
            self.sb_sample(j, alloc, qT, rqT, oT, roT, wk_, rwk)
        P.barrier()
        for q, m in zip(alloc.pools, pmark):
            q[1] = m
        wo = self.load_w(alloc, self.d_wo2[j], stage, rstage, rw)
        ctr = 0
        for oc in range(NCH):
            for ti, (t0, n) in enumerate(self.tiles):
                pb = ctr % 2
                ctr += 1
                for kc in range(NCH):
                    P.op("pe", I("matmul", self.ps[pb][:, 0:n], lhsT=wo[:, kc, oc * 128:(oc + 1) * 128], rhs=oT[:, kc, t0:t0 + n],
                                 start=(kc == 0), stop=(kc == NCH - 1)), reads=[rw, roT], writes=[self.rps[pb]])
                P.op("dve", I("scalar_tensor_tensor", out=self.xT[:, oc, t0:t0 + n], in0=self.xT[:, oc, t0:t0 + n], scalar=ALPHA, in1=self.ps[pb][:, 0:n],
                              op0=ALU.mult, op1=ALU.add), reads=[self.rps[pb], self.rx[ti]], writes=[self.rx[ti]])

    def sb_sample(self, j, alloc, qT, rqT, oT, roT, wk_, rwk):
        cfg, P = self.cfg, self.P
        SEQ, NT, NPG, SBN = cfg.seq, cfg.nt, cfg.npg, cfg.sb
        bias = self.sbb[:, j * H:(j + 1) * H]
        identb = self.cstb[:, 0:128]
        ones, tri = self.ones, self.cst[:, 384:512]
        Kst = [alloc(1024) for _ in range(2)]
        Vst = [alloc(1024) for _ in range(2)]
        rKst, rVst = [Reg("Kst0"), Reg("Kst1")], [Reg("Vst0"), Reg("Vst1")]
        pgctr = [0]

        def gather(bb, pg_):
            s_ = pgctr[0] % 2
            pgctr[0] += 1
            col_ = bb * NPG + pg_
            P.op("pool", I("indirect_dma_start", out=Kst[s_][:, :], out_offset=None, in_=self.d_ck[:, :],
                           in_offset=bass.IndirectOffsetOnAxis(ap=self.pidx[:, col_:col_ + 1], axis=0)),
                 reads=[self.rpidx], writes=[rKst[s_]], dma=rKst[s_])
            P.op("pool", I("indirect_dma_start", out=Vst[s_][:, :], out_offset=None, in_=self.d_cv[:, :],
                           in_offset=bass.IndirectOffsetOnAxis(ap=self.pidx[:, col_:col_ + 1], axis=0)),
                 reads=[self.rpidx], writes=[rVst[s_]], dma=rVst[s_])
        pages = [(bb, pg_) for bb in range(SBN) for pg_ in range(NPG - 1, -1, -1)]
        pgi = [0]
        gather(*pages[0])
        Kb = alloc(512, BF16)
        Vb = alloc(512, BF16)
        KTg = alloc(512, BF16).rearrange("p (c k) -> p c k", c=NCH)
        rKb, rVb, rKTg = Reg("Kb"), Reg("Vb"), Reg("KTg")
        KTn = alloc(256, BF16).rearrange("p (c t) -> p c t", c=NCH)
        rKTn = Reg("KTn")
        brow = alloc(64)
        rbrow = Reg("brow")
        Vn = alloc(512, BF16)
        rVn = Reg("Vn")
        P.op("sp", I("dma_start", out=Vn[0:64, :], in_=self.d_VB[SEQ:SEQ + 64, :]), writes=[rVn], dma=rVn)
        P.op("sp", I("dma_start", out=KTn, in_=self.d_KT[:, :, SEQ:SEQ + 64].rearrange("c p t -> p c t")), writes=[rKTn], dma=rKTn)
        bsrc = bass.AP(bias.tensor, bias.offset, [list(bias.ap[0]), [1, H], [0, 4]])
        P.op("dve", I("tensor_copy", out=brow.rearrange("p (h q) -> p h q", h=H), in_=bsrc), reads=[self.rsbb], writes=[rbrow])
        m3 = self.cst64[0:64, 512:576]
        blk = 0
        lsn = 128
        for b in range(SBN):
            po = 6 + (b % 2)
            q0 = SEQ + 4 * b
            Lsum = wk_["Ls"][0][:, 0:64]
            rLsum = rwk["Ls"][0]
            blocks = [("new", None)] + [("page", pg) for pg in range(NPG - 1, -1, -1)]
            import os
            if os.environ.get("SB_NOPAGE"):
                blocks = blocks[:1]
            if os.environ.get("SB_NONEW"):
                blocks = blocks[1:]
            for bi, (kind, pg) in enumerate(blocks):
                w = blk % 2
                blk += 1
                pz, pc = (blk % 2), 2 + (blk % 2)
                E, L, lb, tt_ = (wk_[nm][w][:, 0:64] for nm in ("E", "L", "lb", "t"))
                rE, rL, rlb, rtt = (rwk[nm][w] for nm in ("E", "L", "lb", "t"))
                Wt = wk_["Ls"][1][:, 0:32].bitcast(BF16)
                rWt = rwk["Ls"][1]
                if kind == "new":
                    nk = 64
                    s = 0
                    ktv = lambda e, c: KTn[e * 64:(e + 1) * 64, c, 0:64]
                    rkt = rKTn
                    mb3 = bass.AP(m3.tensor, m3.offset + 4 * b, [list(m3.ap[0]), [0, H], [1, 4]])
                    v3 = lambda ap: ap.rearrange("p (h q) -> p h q", h=H)
                    Vcur, rVcur = Vn, rVn
                else:
                    nk = 128
                    s = pgi[0] % 2
                    pgi[0] += 1
                    P.op("dve", I("tensor_copy", out=Kb, in_=Kst[s]), reads=[rKst[s]], writes=[rKb])
                    P.op("act", I("copy", out=Vb, in_=Vst[s]), reads=[rVst[s]], writes=[rVb])
                    if pgi[0] < len(pages):
                        gather(*pages[pgi[0]])
                    pT = self.ps[4 + (blk % 2)][:, 0:512].bitcast(BF16).rearrange("p (c k) -> p c k", c=NCH)
                    for c in range(NCH):
                        P.op("pe", I("transpose", out=pT[:, c, :], in_=Kb[:, c * 128:(c + 1) * 128], identity=identb),
                             reads=[rKb, self.rcstb], writes=[self.rps[4 + (blk % 2)]])
                    P.op("act", I("copy", out=KTg, in_=pT), reads=[self.rps[4 + (blk % 2)]], writes=[rKTg])
                    ktv = lambda e, c: KTg[e * 64:(e + 1) * 64, c, :]
                    rkt = rKTg
                    Vcur, rVcur = Vb, rVb
                for h in range(H):
                    c, e = h // 2, h % 2
                    P.op("pe", I("matmul", self.ps[pz][0:nk, h * 4:(h + 1) * 4], lhsT=ktv(e, c), rhs=qT[e * 64:(e + 1) * 64, c, q0:q0 + 4],
                                 start=(h == 0), stop=True, skip_group_check=True), reads=[rkt, rqT], writes=[self.rps[pz]])
                zb = lb
                P.op("dve", I("tensor_tensor", out=zb[0:nk, :], in0=self.ps[pz][0:nk, 0:64], in1=brow[0:nk, :], op=ALU.add),
                     reads=[self.rps[pz], rbrow], writes=[rlb])
                P.op("act", I("activation", out=E[0:nk, :], in_=zb[0:nk, :], func=AF.Exp), reads=[rlb], writes=[rE])
                P.op("act", I("activation", out=L[0:nk, :], in_=E[0:nk, :], func=AF.Ln, bias=self.onec[0:nk, 0:1]), reads=[rE, self.rtiny], writes=[rL])
                P.op("dve", I("tensor_tensor", out=lb[0:nk, :], in0=zb[0:nk, :], in1=L[0:nk, :], op=ALU.subtract), reads=[rlb, rL], writes=[rlb])
                if kind == "new":
                    P.op("dve", I("tensor_tensor", out=v3(L[0:nk, :]), in0=v3(L[0:nk, :]), in1=mb3, op=ALU.mult), reads=[rL, self.rc64], writes=[rL])
                P.op("pe", I("matmul", self.ps[pc][0:nk, 0:64], lhsT=tri[0:nk, 0:nk], rhs=L[0:nk, :], start=True, stop=(kind == "new")),
                     reads=[self.rcst, rL], writes=[self.rps[pc]])
                if kind != "new":
                    P.op("pe", I("matmul", self.ps[pc][0:nk, 0:64], lhsT=ones[0:lsn, 0:nk], rhs=Lsum[0:lsn, :], start=False, stop=True),
                         reads=[self.rcst, rLsum], writes=[self.rps[pc]])
                P.op("dve", I("tensor_tensor", out=tt_[0:nk, :], in0=lb[0:nk, :], in1=self.ps[pc][0:nk, 0:64], op=ALU.subtract),
                     reads=[rlb, self.rps[pc]], writes=[rtt])
                if kind == "new":
                    P.op("pool", I("memset", Lsum, 0.0), writes=[rLsum])
                    P.op("pool", I("tensor_copy", out=Lsum[0:64, :], in_=L[0:64, :]), reads=[rL], writes=[rLsum])
                    P.op("act", I("activation", out=tt_[0:nk, :], in_=tt_[0:nk, :], func=AF.Exp), reads=[rtt], writes=[rtt])
                    P.op("dve", I("tensor_tensor", out=v3(Wt[0:nk, :]), in0=v3(tt_[0:nk, :]), in1=mb3, op=ALU.mult), reads=[rtt, self.rc64], writes=[rWt])
                else:
                    if bi < len(blocks) - 1:
                        P.op("pool", I("tensor_tensor", out=Lsum, in0=Lsum, in1=L, op=ALU.add), reads=[rL, rLsum], writes=[rLsum])
                    P.op("act", I("activation", out=Wt[0:nk, :], in_=tt_[0:nk, :], func=AF.Exp), reads=[rtt], writes=[rWt])
                for c in range(NCH):
                    P.op("pe", I("matmul", self.ps[po][:, c * 8:(c + 1) * 8], lhsT=Vcur[0:nk, c * 128:(c + 1) * 128], rhs=Wt[0:nk, c * 8:(c + 1) * 8],
                                 start=(bi == 0 and c == 0), stop=(bi == len(blocks) - 1), skip_group_check=True),
                         reads=[rVcur, rWt], writes=[self.rps[po]])
            pov = self.ps[po][:, 0:64].rearrange("p (c x) -> p c x", c=NCH)
            P.op("act", I("copy", out=oT[0:64, :, q0:q0 + 4], in_=pov[0:64, :, 0:4]), reads=[self.rps[po]], writes=[roT])
            P.op("act", I("copy", out=oT[64:128, :, q0:q0 + 4], in_=pov[64:128, :, 4:8]), reads=[self.rps[po]], writes=[roT])

    def layernorm(self, li, slot):
        cfg, P = self.cfg, self.P
        P.barrier()
        o = 0
        sq = self.aview(o, NCH * 512).rearrange("p (c t) -> p c t", c=NCH); o += NCH * 512
        mean = self.aview(o, 512); o += 512
        rstd = self.aview(o, 512); o += 512
        tmp = self.aview(o, 512); o += 512
        rsq, rmean, rrstd, rtmp = Reg("sq"), Reg("mean"), Reg("rstd"), Reg("tmp")
        epsb = self.aview(o, 1); o += 1
        reps = Reg("eps")
        P.op("pool", I("memset", epsb, LN_EPS), writes=[reps])
        for ti, (t0, n) in enumerate(self.tiles):
            b1, b2 = (2 * ti) % 8, (2 * ti + 1) % 8
            P.op("act", I("activation", out=sq[:, :, 0:n], in_=self.xT[:, :, t0:t0 + n], func=AF.Square),
                 reads=[self.rx[ti]], writes=[rsq])
            for c in range(NCH):
                P.op("pe", I("matmul", self.ps[b1][:, 0:n], lhsT=self.ones, rhs=self.xT[:, c, t0:t0 + n],
                                                                    start=(c == 0), stop=(c == NCH - 1)),
                     reads=[self.rcst, self.rx[ti]], writes=[self.rps[b1]])
            for c in range(NCH):
                P.op("pe", I("matmul", self.ps[b2][:, 0:n], lhsT=self.ones, rhs=sq[:, c, 0:n],
                                                              start=(c == 0), stop=(c == NCH - 1)),
                     reads=[self.rcst, rsq], writes=[self.rps[b2]])
            P.op("dve", I("tensor_scalar", out=mean[:, 0:n], in0=self.ps[b1][:, 0:n], scalar1=1.0 / D, scalar2=None, op0=ALU.mult),
                 reads=[self.rps[b1]], writes=[rmean])
            P.op("dve", I("tensor_tensor", out=tmp[:, 0:n], in0=mean[:, 0:n], in1=mean[:, 0:n], op=ALU.mult),
                 reads=[rmean], writes=[rtmp])
            P.op("dve", I("scalar_tensor_tensor", out=tmp[:, 0:n], in0=self.ps[b2][:, 0:n], scalar=1.0 / D, in1=tmp[:, 0:n],
                                                                   op0=ALU.mult, op1=ALU.subtract),
                 reads=[self.rps[b2], rtmp], writes=[rtmp])
            P.op("act", I("activation", out=tmp[:, 0:n], in_=tmp[:, 0:n], func=AF.Ln, bias=epsb[:, 0:1]),
                 reads=[rtmp, reps], writes=[rtmp])
            P.op("act", I("activation", out=rstd[:, 0:n], in_=tmp[:, 0:n], func=AF.Exp, scale=-0.5),
                 reads=[rtmp], writes=[rrstd])
            for c in range(NCH):
                xs = self.xT[:, c, t0:t0 + n]
                P.op("dve", I("tensor_tensor", out=xs, in0=xs, in1=mean[:, 0:n], op=ALU.subtract),
                     reads=[self.rx[ti], rmean], writes=[self.rx[ti]])
                P.op("dve", I("tensor_tensor", out=xs, in0=xs, in1=rstd[:, 0:n], op=ALU.mult),
                     reads=[self.rx[ti], rrstd], writes=[self.rx[ti]])
                P.op("act", I("activation", out=xs, in_=xs, func=AF.Identity,
                                                              bias=self.vec("ln_b", li, slot, c), scale=self.vec("ln_g", li, slot, c)),
                     reads=[self.rx[ti], self.rvec], writes=[self.rx[ti]])
                P.op("pool", I("tensor_copy", out=self.xb[:, c, t0:t0 + n], in_=xs),
                     reads=[self.rx[ti]], writes=[self.rb[ti]])


V64 = {}
for _li in range(NA):
    for _n in ("w0", "a0", "k_k", "k_a", "r_k", "gn_g", "gn_b", "v0"):
        V64[(_n, _li)] = len(V64)
NV64 = len(V64)


def pack_vecs64(inp):
    out = np.zeros((NV64, D), np.float32)
    for (n, li), i in V64.items():
        if n == "v0":
            v = inp["tm_v0"][0] if li == 1 else np.zeros(D, np.float32)
        elif n == "r_k":
            v = inp["tm_r_k"][li].reshape(D)
        else:
            v = inp["tm_" + n][li]
        out[i] = v
    return np.ascontiguousarray(out.reshape(NV64, H, DH).transpose(2, 0, 1).reshape(DH, NV64 * H))


def pack_consts64():
    i = np.arange(64)
    outs = []
    for C in (64, 4):
        su = (i[:, None] < i[None, :]).astype(np.float32)
        iu = (i[:, None] <= i[None, :]).astype(np.float32)
        lo = (i[:, None] > i[None, :]).astype(np.float32)
        ey = (i[:, None] == i[None, :]).astype(np.float32)
        m = np.zeros((64, 4 * 64), np.float32)
        m[:C, 0:C] = su[:C, :C]
        m[:C, C:2 * C] = iu[:C, :C]
        m[:C, 128:128 + C] = lo[:C, :C]
        m[:C, 192:192 + C] = ey[:C, :C]
        outs.append(m)
    m3 = np.zeros((64, 16, 4), np.float32)
    for s_ in range(64):
        for q in range(4):
            if s_ % 4 < q:
                m3[s_, s_ // 4, q] = 1.0
    outs.append(m3.reshape(64, 64))
    return np.ascontiguousarray(np.concatenate(outs, axis=1))


def pack_tm(inp):
    wrkv = np.stack([pack_proj(inp["tm_w_r"]), pack_proj(inp["tm_w_k"]), pack_proj(inp["tm_w_v"])], 1)
    v1 = np.concatenate([np.zeros_like(inp["tm_v1"]), inp["tm_v1"]], 0)
    v2 = np.concatenate([np.zeros_like(inp["tm_v2"]), inp["tm_v2"]], 0)
    l1 = np.concatenate([inp["tm_w1"], inp["tm_a1"], v1, inp["tm_g1"]], axis=2)
    l1 = pack_proj(l1)
    l2 = np.zeros((NA, 128, 4, D), np.float32)
    l2[:, :64, 0] = inp["tm_w2"]
    l2[:, :64, 1] = inp["tm_a2"]
    l2[:, :32, 2] = v2
    l2[:, :128, 3] = inp["tm_g2"]
    wo = np.ascontiguousarray(inp["tm_w_o"].reshape(NA, H, DH, D).transpose(0, 2, 1, 3))
    return np.ascontiguousarray(wrkv), l1, l2, wo


def make_in_maps(cfg, inp):
    vecs = pack_vecs(inp)
    cst = pack_consts()
    wffn = pack_ffn(inp)
    v64 = pack_vecs64(inp)
    c64 = pack_consts64()
    wrkv, l1, l2, _ = pack_tm(inp)
    wo = pack_proj(inp["tm_w_o"])
    wkv2 = np.ascontiguousarray(np.stack([pack_proj(inp["sb_w_k"]), pack_proj(inp["sb_w_v"])], 0))
    wq = pack_proj(inp["sb_w_q"])
    wo2 = pack_proj(inp["sb_w_o"])
    sbb = np.ascontiguousarray(np.broadcast_to(inp["sb_bias"].reshape(1, 2 * H), (128, 2 * H))).astype(np.float32)
    ck = inp["cache_k"].reshape(-1, D)
    cv = inp["cache_v"].reshape(-1, D)
    maps = []
    for c in range(cfg.ncores):
        m = {
            "xp": np.ascontiguousarray(inp["x_prompt"][c]),
            "xs": np.ascontiguousarray(np.concatenate([inp["x_sample"][c * cfg.sb:(c + 1) * cfg.sb].reshape(cfg.nts, D),
                                                       np.zeros((128 - cfg.nts, D), np.float32)], 0)),
            "vecs": vecs, "cst": cst, "wffn": wffn, "v64": v64, "c64": c64,
            "wrkv": wrkv, "wo": wo, "l1": l1, "l2": l2,
            "wkv2": wkv2, "wq": wq, "wo2": wo2, "sbb": sbb, "ck": ck, "cv": cv,
            "pt": np.ascontiguousarray(inp["page_table"][c * cfg.sb:(c + 1) * cfg.sb].reshape(1, -1).astype(np.int32)),
            "sshift": np.ascontiguousarray(inp["state_shift"][:, c * cfg.sb:(c + 1) * cfg.sb]),
            "swkv": np.ascontiguousarray(inp["state_wkv"][:, c * cfg.sb:(c + 1) * cfg.sb]),
        }
        maps.append(m)
    return maps


OUT_NAMES = ("yp", "ys", "wkv_p", "wkv_s", "shift_p", "shift_s", "kp", "vp", "ks", "vs")


def assemble(cfg, results):
    n = cfg.ncores
    SEQ = cfg.seq
    yp = np.stack([r["yp"] for r in results], 0)
    ys = np.concatenate([r["ys"][:cfg.nts].reshape(cfg.sb, cfg.dseq, D) for r in results], 0)
    wkv_p = np.stack([r["wkv_p"] for r in results], 1)
    shift_p = np.stack([r["shift_p"] for r in results], 1)
    kp = np.stack([r["kp"].reshape(SEQ // 128, 128, H, DH) for r in results], 0)
    vp = np.stack([r["vp"].reshape(SEQ // 128, 128, H, DH) for r in results], 0)
    wkv_s = np.concatenate([r["wkv_s"] for r in results], 1)
    shift_s = np.concatenate([r["shift_s"] for r in results], 1)
    ks = np.concatenate([r["ks"][:cfg.nts].reshape(cfg.sb, cfg.dseq, H, DH) for r in results], 0)
    vs = np.concatenate([r["vs"][:cfg.nts].reshape(cfg.sb, cfg.dseq, H, DH) for r in results], 0)
    return (yp, ys, wkv_p, shift_p, kp, vp, wkv_s, shift_s, ks, vs)


def kernel(**inp):
    cfg = Cfg()
    b = Builder(cfg)
    nc = b.build()
    inp = {k: np.asarray(v) for k, v in inp.items()}
    maps = make_in_maps(cfg, inp)
    res = run_bass_kernel_spmd(nc, maps, core_ids=list(range(cfg.ncores)))
    return assemble(cfg, res.results)
```
